# Optimizing a Trainium2 kernel written in Bass

```python
import math
import jax
import jax.numpy as jnp
from jax import lax
import numpy as np

D_MODEL = 1024
BATCH = 4
SEQ = 8192
DEPTH = 2
DEC_BATCH = 2
DEC_SEQ = 8192
PAST_LEN = 128

GRID_W = 64
N_HEADS = 16
HEAD_DIM = 64
A_KV_HEADS = 4
A_GROUP = N_HEADS // A_KV_HEADS
ROPE_THETA = 10000.0
AXIS_DIM = HEAD_DIM // 2
NA_WIN_H_MAX = 8
NA_WIN_W = 16
NA_BIAS_H = 2 * NA_WIN_H_MAX - 1
NA_BIAS_W = 2 * NA_WIN_W - 1
Q_BLOCK = 128
NA_Q_BLOCK = GRID_W
D_FF = 4 * D_MODEL
NORM_EPS = 1e-6
N_A_LAYERS = (DEPTH + 1) // 2
N_B_LAYERS = DEPTH // 2

kernel_name = "hybrid_axial_gqa_natten_encoder"


def rms_norm(x, gain):
    xf = x.astype(jnp.float32)
    y = xf * lax.rsqrt(jnp.mean(xf * xf, axis=-1, keepdims=True) + NORM_EPS)
    return (y * gain.astype(jnp.float32)).astype(x.dtype)


def axial_rope_tables(seq):
    t = jnp.arange(seq)
    row = (t // GRID_W).astype(jnp.float32)
    col = (t % GRID_W).astype(jnp.float32)
    inv = ROPE_THETA ** (-jnp.arange(0, AXIS_DIM, 2, dtype=jnp.float32) / AXIS_DIM)
    ang = jnp.concatenate([row[:, None] * inv, col[:, None] * inv], axis=-1)
    return jnp.cos(ang), jnp.sin(ang)


def apply_axial_rope(x, cos, sin):
    b, s, h, _ = x.shape
    half = AXIS_DIM // 2
    xa = x.reshape(b, s, h, 2, AXIS_DIM)
    x1, x2 = xa[..., :half], xa[..., half:]
    c = cos.reshape(s, 1, 2, half)
    sn = sin.reshape(s, 1, 2, half)
    out = jnp.concatenate([x1 * c - x2 * sn, x2 * c + x1 * sn], axis=-1)
    return out.reshape(b, s, h, HEAD_DIM)


def global_axial_gqa(h, w_qkv, q_gain, k_gain, w_o):
    b, s, _ = h.shape
    qkv = h @ w_qkv
    q, k, v = jnp.split(qkv, [N_HEADS * HEAD_DIM, (N_HEADS + A_KV_HEADS) * HEAD_DIM], axis=-1)
    q = q.reshape(b, s, N_HEADS, HEAD_DIM)
    k = k.reshape(b, s, A_KV_HEADS, HEAD_DIM)
    v = v.reshape(b, s, A_KV_HEADS, HEAD_DIM)
    cos, sin = axial_rope_tables(s)
    q = (apply_axial_rope(rms_norm(q, q_gain).astype(jnp.float32), cos, sin) * HEAD_DIM ** -0.5).astype(h.dtype)
    k = apply_axial_rope(rms_norm(k, k_gain).astype(jnp.float32), cos, sin).astype(h.dtype)
    nblk = s // Q_BLOCK
    qb = q.reshape(b, nblk, Q_BLOCK, A_KV_HEADS, A_GROUP, HEAD_DIM).transpose(1, 0, 2, 3, 4, 5)

    def block(qi):
        sc = jnp.einsum('bqkgd,bskd->bkgqs', qi, k, preferred_element_type=jnp.float32)
        p = jax.nn.softmax(sc, axis=-1).astype(v.dtype)
        return jnp.einsum('bkgqs,bskd->bqkgd', p, v)

    o = lax.map(block, qb)
    o = o.transpose(1, 0, 2, 3, 4, 5).reshape(b, s, N_HEADS * HEAD_DIM)
    return o @ w_o


def neighbourhood_indices(rows):
    kh = min(NA_WIN_H_MAX, rows)
    s = rows * GRID_W
    t = jnp.arange(s)
    r = t // GRID_W
    c = t % GRID_W
    r0 = jnp.clip(r - kh // 2, 0, rows - kh)
    c0 = jnp.clip(c - NA_WIN_W // 2, 0, GRID_W - NA_WIN_W)
    kr = r0[:, None] + jnp.arange(kh)[None, :]
    kc = c0[:, None] + jnp.arange(NA_WIN_W)[None, :]
    key_idx = (kr[:, :, None] * GRID_W + kc[:, None, :]).reshape(s, kh * NA_WIN_W)
    dr = kr - r[:, None] + (NA_WIN_H_MAX - 1)
    dc = kc - c[:, None] + (NA_WIN_W - 1)
    bias_idx = (dr[:, :, None] * NA_BIAS_W + dc[:, None, :]).reshape(s, kh * NA_WIN_W)
    return key_idx, bias_idx


def neighbourhood_attention(h, w_qkv, rel_bias, w_o):
    b, s, _ = h.shape
    rows = s // GRID_W
    qkv = h @ w_qkv
    q, k, v = jnp.split(qkv, 3, axis=-1)
    q = q.reshape(b, s, N_HEADS, HEAD_DIM) * HEAD_DIM ** -0.5
    k = k.reshape(b, s, N_HEADS, HEAD_DIM)
    v = v.reshape(b, s, N_HEADS, HEAD_DIM)
    key_idx, bias_idx = neighbourhood_indices(rows)
    kn = key_idx.shape[-1]
    nblk = s // NA_Q_BLOCK
    qb = q.reshape(b, nblk, NA_Q_BLOCK, N_HEADS, HEAD_DIM).swapaxes(0, 1)
    kib = key_idx.reshape(nblk, NA_Q_BLOCK, kn)
    bib = bias_idx.reshape(nblk, NA_Q_BLOCK, kn)
    table = rel_bias.reshape(N_HEADS, NA_BIAS_H * NA_BIAS_W)

    def block(args):
        qi, ki, bi = args
        kg = jnp.take(k, ki, axis=1)
        vg = jnp.take(v, ki, axis=1)
        sc = jnp.einsum('bqhd,bqnhd->bhqn', qi, kg, preferred_element_type=jnp.float32)
        sc = sc + jnp.take(table, bi, axis=1).astype(jnp.float32)
        p = jax.nn.softmax(sc, axis=-1).astype(v.dtype)
        return jnp.einsum('bhqn,bqnhd->bqhd', p, vg)

    o = lax.map(block, (qb, kib, bib))
    o = o.swapaxes(0, 1).reshape(b, s, N_HEADS * HEAD_DIM)
    return o @ w_o


def sq_relu_mlp(h, w_in, w_out):
    u = jax.nn.relu(h @ w_in)
    return (u * u) @ w_out


def trunk(x, norm_mix, norm_mlp, norm_final, a_w_qkv, a_q_norm, a_k_norm, a_w_o,
          b_w_qkv, b_rel_bias, b_w_o, mlp_w_in, mlp_w_out):
    for i in range(DEPTH):
        j = i // 2
        h = rms_norm(x, norm_mix[i])
        if i % 2 == 0:
            x = x + global_axial_gqa(h, a_w_qkv[j], a_q_norm[j], a_k_norm[j], a_w_o[j])
        else:
            x = x + neighbourhood_attention(h, b_w_qkv[j], b_rel_bias[j], b_w_o[j])
        x = x + sq_relu_mlp(rms_norm(x, norm_mlp[i]), mlp_w_in[i], mlp_w_out[i])
    return rms_norm(x, norm_final)


def setup_inputs(seed: int = 0) -> dict:
    key = jax.random.key(seed)
    ks = jax.random.split(key, 16)
    f32 = jnp.float32
    hd = N_HEADS * HEAD_DIM
    a_qkv_w = (N_HEADS + 2 * A_KV_HEADS) * HEAD_DIM
    nrm = lambda k, shape, scale: jax.random.normal(k, shape, f32) * scale
    return {
        "x_prompt": jax.random.normal(ks[0], (BATCH, SEQ, D_MODEL), f32),
        "x_sample": jax.random.normal(ks[1], (DEC_BATCH, DEC_SEQ, D_MODEL), f32),
        "norm_mix": 1.0 + nrm(ks[2], (DEPTH, D_MODEL), 0.05),
        "norm_mlp": 1.0 + nrm(ks[3], (DEPTH, D_MODEL), 0.05),
        "norm_final": 1.0 + nrm(ks[4], (D_MODEL,), 0.05),
        "a_w_qkv": nrm(ks[5], (N_A_LAYERS, D_MODEL, a_qkv_w), D_MODEL ** -0.5),
        "a_q_norm": 1.0 + nrm(ks[6], (N_A_LAYERS, HEAD_DIM), 0.05),
        "a_k_norm": 1.0 + nrm(ks[7], (N_A_LAYERS, HEAD_DIM), 0.05),
        "a_w_o": nrm(ks[8], (N_A_LAYERS, hd, D_MODEL), hd ** -0.5),
        "b_w_qkv": nrm(ks[9], (N_B_LAYERS, D_MODEL, 3 * hd), D_MODEL ** -0.5),
        "b_rel_bias": nrm(ks[10], (N_B_LAYERS, N_HEADS, NA_BIAS_H, NA_BIAS_W), 0.1),
        "b_w_o": nrm(ks[11], (N_B_LAYERS, hd, D_MODEL), hd ** -0.5),
        "mlp_w_in": nrm(ks[12], (DEPTH, D_MODEL, D_FF), D_MODEL ** -0.5),
        "mlp_w_out": nrm(ks[13], (DEPTH, D_FF, D_MODEL), D_FF ** -0.5),
    }


def reference(x_prompt, x_sample, norm_mix, norm_mlp, norm_final, a_w_qkv, a_q_norm, a_k_norm,
              a_w_o, b_w_qkv, b_rel_bias, b_w_o, mlp_w_in, mlp_w_out):
    y_prompt = trunk(x_prompt, norm_mix, norm_mlp, norm_final, a_w_qkv, a_q_norm, a_k_norm, a_w_o,
                     b_w_qkv, b_rel_bias, b_w_o, mlp_w_in, mlp_w_out)
    y_sample = trunk(x_sample, norm_mix, norm_mlp, norm_final, a_w_qkv, a_q_norm, a_k_norm, a_w_o,
                     b_w_qkv, b_rel_bias, b_w_o, mlp_w_in, mlp_w_out)
    return (y_prompt, y_sample)
```

```python
import math
from contextlib import ExitStack

import numpy as np
import ml_dtypes

import concourse.bass as bass
import concourse.mybir as mybir
from concourse.bass_utils import run_bass_kernel_spmd

F32 = mybir.dt.float32
BF16 = mybir.dt.bfloat16
AF = mybir.ActivationFunctionType
ALU = mybir.AluOpType
AX = mybir.AxisListType

D = 1024
NH = 16
HD = 64
NKV = 4
GW = 64
DFF = 4096
EPS = 1e-6
NEG = -200.0

ENG = ("pe", "act", "dve", "pool", "sp")
BLOCK_ATTR = {"pe": "tensor", "act": "scalar", "dve": "vector", "pool": "gpsimd", "sp": "sync"}


class DSem:
    def __init__(self, sem):
        self.sem = sem
        self.cnt = 0


class Buf:
    def __init__(self, t=None):
        self.t = t
        self.w = {}
        self.r = {}


def _merge(dst, tok):
    if tok is None:
        return
    sem, val = tok
    k = id(sem)
    if k not in dst or dst[k][1] < val:
        dst[k] = (sem, val)


class Sched:
    def __init__(self, nc, stack):
        self.nc = nc
        self.stack = stack
        self.streams = {e: [] for e in ENG}
        self.esem = {e: stack.enter_context(nc.semaphore("es_" + e)) for e in ENG if e != "sp"}
        self.ecnt = {e: 0 for e in ENG}
        self.waited = {e: {} for e in ENG}
        self.nds = 0
        self.nops = 0

    def dsem(self):
        self.nds += 1
        return DSem(self.stack.enter_context(self.nc.semaphore("ds%d" % self.nds)))

    def add(self, eng, fn, reads=(), writes=(), deps=(), sig=True, dma=None):
        need = {}
        for b in reads:
            for tok in b.w.values():
                _merge(need, tok)
        for b in writes:
            for tok in b.w.values():
                _merge(need, tok)
            for tok in b.r.values():
                _merge(need, tok)
        for tok in deps:
            _merge(need, tok)
        waits = []
        wd = self.waited[eng]
        for k, (sem, val) in need.items():
            if wd.get(k, 0) >= val:
                continue
            wd[k] = val
            waits.append((sem, val))
        if dma is not None:
            dma.cnt += 16
            tok = (dma.sem, dma.cnt)
            inc = (dma.sem, 16)
        elif sig:
            self.ecnt[eng] += 1
            tok = (self.esem[eng], self.ecnt[eng])
            inc = (self.esem[eng], 1)
        else:
            tok = None
            inc = None
        self.streams[eng].append((fn, waits, inc))
        self.nops += 1
        if tok is not None:
            for b in reads:
                _merge(b.r, tok)
            for b in writes:
                b.w = {}
                b.r = {}
                _merge(b.w, tok)
        return tok

    def flush(self, block):
        for eng in ENG:
            ops = self.streams[eng]
            if not ops:
                continue

            def body(e, ops=ops):
                for fn, waits, inc in ops:
                    for sem, val in waits:
                        e.wait_ge(sem, val)
                    ins = fn(e)
                    if inc is not None:
                        ins.then_inc(inc[0], inc[1])

            getattr(block, BLOCK_ATTR[eng])(body)
            self.streams[eng] = []


def build(ROWS=128, debug=False):
    S = ROWS * GW
    NT = S // 128
    NST = S // 512
    nc = bass.Bass("TRN2", target_bir_lowering=False)
    es = ExitStack()
    with es:
        sc = Sched(nc, es)

        def din(name, shape, dt=F32):
            return nc.dram_tensor(name, list(shape), dt, kind="ExternalInput").ap()

        def dscr(name, shape, dt):
            return nc.dram_tensor(name, list(shape), dt,
                                  kind="ExternalOutput" if debug else "Internal").ap()

        x_d = din("x", [S, D])
        rope_d = din("rope", [S, 128])
        ident_d = din("ident", [128, 128], BF16)
        norm_mix_d = din("norm_mix", [2, D])
        norm_mlp_d = din("norm_mlp", [2, D])
        norm_final_d = din("norm_final", [D])
        a_wqkv_d = din("a_w_qkv", [D, 1536])
        a_qn_d = din("a_q_norm", [HD])
        a_kn_d = din("a_k_norm", [HD])
        a_wo_d = din("a_w_o", [D, D])
        b_wqkv_d = din("b_w_qkv", [D, 3072])
        b_bias_d = din("b_rel_bias", [NH, 15, 31])
        b_wo_d = din("b_w_o", [D, D])
        w_in_d = din("mlp_w_in", [2, D, DFF])
        w_out_d = din("mlp_w_out", [2, DFF, D])
        y_d = nc.dram_tensor("y", [S, D], F32, kind="ExternalOutput").ap()

        qT_d = dscr("qT_s", [128, 8, S], BF16)
        oT_d = dscr("oT_s", [128, 8, S], BF16)
        x1_d = dscr("x1_s", [S, D], F32)
        kT1_d = dscr("kT1_s", [128, 8, S], BF16)
        v1_d = dscr("v1_s", [S, NH * 65], BF16)
        trev_d = dscr("trev_s", [NH, 15, 31], F32)
        cful_d = dscr("cful_s", [128, NH * 14 * 64], F32)
        cint_d = dscr("cint_s", [128, NH * 10 * 64], F32)

        def sb(stack, name, shape, dt):
            return stack.enter_context(nc.sbuf_tensor(name, list(shape), dt))

        def ps(stack, name, shape, dt=F32):
            return stack.enter_context(nc.psum_tensor(name, list(shape), dt))

        ident = sb(es, "ident_sb", [128, 128], BF16)
        ones_f = sb(es, "ones_f", [128, 64], F32)
        negh = sb(es, "negh", [128, 32], F32)
        ones_b = sb(es, "ones_b", [128, 64], BF16)
        b_ident = Buf()
        b_const = Buf()
        ds_c = sc.dsem()
        sc.add("sp", lambda e: e.dma_start(out=ident[:], in_=ident_d), writes=[b_ident], dma=ds_c)
        sc.add("pool", lambda e: e.memset(ones_f[:], 1.0), writes=[b_const])
        sc.add("pool", lambda e: e.memset(negh[:], -0.5), writes=[b_const])
        sc.add("pool", lambda e: e.memset(ones_b[:], 1.0), writes=[b_const])

        epsb = sb(es, "epsb", [128, 1], F32)
        sc.add("pool", lambda e: e.memset(epsb[:], EPS), writes=[b_const])

        def rstd_ops(ssq_ap, tmp_ap, out_ap, nh_ap, inv_n, b_in, b_tmp, b_out):
            sc.add("act", lambda e: e.activation(out=tmp_ap, in_=ssq_ap, func=AF.Sqrt, scale=inv_n, bias=epsb[:, 0:1]),
                   reads=[b_in, b_const], writes=[b_tmp])
            sc.add("dve", lambda e: e.reciprocal(out=out_ap, in_=tmp_ap), reads=[b_tmp], writes=[b_out])

        def group0(eng, fns, reads=(), writes=(), per_deps=None):
            n = len(fns)
            for i, fn in enumerate(fns):
                pd = list(per_deps[i]) if per_deps is not None else []
                if i == n - 1:
                    sc.add(eng, fn, reads=reads, writes=writes, deps=pd)
                elif i == 0:
                    deps = [t for b in reads for t in b.w.values()]
                    deps += [t for b in writes for t in list(b.w.values()) + list(b.r.values())]
                    sc.add(eng, fn, deps=deps + pd, sig=False)
                else:
                    sc.add(eng, fn, deps=pd, sig=False)

        def set_w0(buf, dsem):
            buf.w = {}
            buf.r = {}
            _merge(buf.w, (dsem.sem, dsem.cnt))

        with ExitStack() as st, nc.Block("E0") as blk:
            tb0 = sb(st, "tb0", [16, 15, 31], F32)
            tr1 = sb(st, "tr1", [16, 15, 31], F32)
            tr2 = sb(st, "tr2", [16, 15, 31], F32)
            negt = sb(st, "negt", [128, 2048], F32)
            b_tb0 = Buf(); b_tr1 = Buf(); b_tr2 = Buf(); b_negt = Buf()
            ds_e0 = sc.dsem(); ds_fill = sc.dsem(); ds_tr = sc.dsem(); ds_sc = sc.dsem()
            sc.add("sp", lambda e: e.dma_start(out=tb0[:], in_=b_bias_d), writes=[b_tb0], dma=ds_e0)
            for i in range(31):
                sc.add("dve", lambda e, i=i: e.tensor_copy(out=tr1[:, :, i:i + 1], in_=tb0[:, :, 30 - i:31 - i]),
                       reads=[b_tb0] if i == 0 else [], writes=[b_tr1] if i == 30 else [], sig=(i == 30))
            for u in range(15):
                sc.add("dve", lambda e, u=u: e.tensor_copy(out=tr2[:, u:u + 1, :], in_=tr1[:, 14 - u:15 - u, :]),
                       reads=[b_tr1] if u == 0 else [], writes=[b_tr2] if u == 14 else [], sig=(u == 14))
            sc.add("sp", lambda e: e.dma_start(out=trev_d, in_=tr2[:]), reads=[b_tr2], dma=ds_tr)
            sc.add("pool", lambda e: e.memset(negt[:], NEG), writes=[b_negt])
            for k in range(7):
                sc.add("sp", lambda e, k=k: e.dma_start(out=cful_d[:, k * 2048:(k + 1) * 2048], in_=negt[:]),
                       reads=[b_negt], dma=ds_fill)
            for k in range(5):
                sc.add("sp", lambda e, k=k: e.dma_start(out=cint_d[:, k * 2048:(k + 1) * 2048], in_=negt[:]),
                       reads=[b_negt], dma=ds_fill)
            pre = [(ds_tr.sem, ds_tr.cnt), (ds_fill.sem, ds_fill.cnt)]
            scatter_jobs = []
            cfv = cful_d.rearrange("p (h r c) -> p h r c", h=NH, r=14)
            civ = cint_d.rearrange("p (h r c) -> p h r c", h=NH, r=10)
            for a in range(2):
                for kc in range(64):
                    cs = [c for c in range(64) if min(max(c - 8, 0), 48) <= kc <= min(max(c - 8, 0), 48) + 15]
                    clo, chi = cs[0], cs[-1]
                    assert cs == list(range(clo, chi + 1))
                    s0, s1 = clo - kc + 15, chi - kc + 16
                    p = a * 64 + kc
                    scatter_jobs.append(lambda e, p=p, a=a, clo=clo, chi=chi, s0=s0, s1=s1: e.dma_start(
                        out=cfv[p, :, :, clo:chi + 1], in_=trev_d[:, 1 - a:15 - a, s0:s1]))
                    scatter_jobs.append(lambda e, p=p, a=a, clo=clo, chi=chi, s0=s0, s1=s1: e.dma_start(
                        out=civ[p, :, 1 + a:9 + a, clo:chi + 1], in_=trev_d[:, 4:12, s0:s1]))
            sc.add("pool", lambda e: e.memset(negh[:, 31:32], -0.5), deps=pre)
            sc.flush(blk)

        with ExitStack() as l0:
            KT = sb(l0, "KT", [128, 2, S], BF16)
            VP = sb(l0, "VP", [128, NT, NKV, 65], BF16)
            b_KT = [Buf() for _ in range(NT)]
            b_VP = [Buf() for _ in range(NT)]
            b_VPones = Buf()
            sc.add("pool", lambda e: e.memset(VP[:, :, :, 64:65], 1.0), writes=[b_VPones])

            with ExitStack() as st, nc.Block("A0") as blk:
                wqkv = sb(st, "wqkv0", [128, 8, 1536], BF16)
                gcol = sb(st, "gcol0", [128, 8], F32)
                gq = sb(st, "gq", [128, 4, 64], F32)
                xs = [sb(st, "xs%d" % i, [128, D], F32) for i in range(3)]
                rp = [sb(st, "rp%d" % i, [128, 128], F32) for i in range(3)]
                junk = sb(st, "junk", [128, D], BF16)
                ssq = sb(st, "ssq", [128, NT], F32)
                tmp1 = sb(st, "tmp1", [128, NT], F32)
                rstd = sb(st, "rstd", [128, NT], F32)
                xn = [sb(st, "xn%d" % i, [128, D], BF16) for i in range(2)]
                hT = [sb(st, "hT%d" % i, [128, 8, 128], BF16) for i in range(2)]
                sqb = sb(st, "sqb", [128, 20, 64], F32)
                raw = sb(st, "raw", [128, 20, 64], F32)
                raw_b = sb(st, "raw_b", [128, 20, 64], F32)
                ssqh = sb(st, "ssqh", [128, 20], F32)
                tmph = sb(st, "tmph", [128, 20], F32)
                rstdh = sb(st, "rstdh", [128, 20], F32)
                tabs = [sb(st, "tabs%d" % i, [128, 4, 64], F32) for i in range(2)]
                ra = sb(st, "ra", [128, 20, 64], F32)
                rb = sb(st, "rb", [128, 20, 64], F32)
                rc = sb(st, "rc", [128, 20, 64], F32)
                qk = [sb(st, "qk%d" % i, [128, 20, 64], BF16) for i in range(2)]
                qTs = [sb(st, "qTs%d" % i, [128, 8, 128], BF16) for i in range(2)]
                tpx = ps(st, "tpx", [128, 8, 128], BF16)
                tpq = ps(st, "tpq", [128, 8, 128], BF16)
                tpk = ps(st, "tpk", [128, 2, 128], BF16)
                pqkv = ps(st, "pqkv", [128, 1536], F32)

                b_w = Buf(); b_gain = Buf(); b_gq = Buf()
                b_xs = [Buf() for _ in range(3)]; b_rp = [Buf() for _ in range(3)]
                ds_rp = [sc.dsem() for _ in range(3)]
                b_ssq = Buf(); b_tmp1 = Buf(); b_rstd = Buf()
                b_xn = [Buf() for _ in range(2)]; b_hT = [Buf() for _ in range(2)]
                b_sqb = Buf(); b_raw = Buf(); b_ssqh = Buf(); b_tmph = Buf(); b_rstdh = Buf()
                b_tabs = [Buf() for _ in range(2)]
                b_ra = Buf(); b_rb = Buf(); b_rc = Buf()
                b_qk = [Buf() for _ in range(2)]; b_qTs = [Buf() for _ in range(2)]
                b_tpx = Buf(); b_tpq = Buf(); b_tpk = Buf(); b_pqkv = Buf()
                ds_w = sc.dsem(); ds_g = sc.dsem(); ds_gq = sc.dsem()
                ds_x = [sc.dsem() for _ in range(3)]; ds_q = [sc.dsem() for _ in range(2)]

                wv = a_wqkv_d.rearrange("(c p) m -> p c m", p=128)
                for c in range(8):
                    sc.add("pool", lambda e, c=c: e.dma_start(out=wqkv[:, c, :], in_=wv[:, c, :]),
                           writes=[b_w] if c == 7 else [], dma=ds_w)
                b_w.w = {}; _merge(b_w.w, (ds_w.sem, ds_w.cnt))
                sc.add("sp", lambda e: e.dma_start(out=gcol[:], in_=norm_mix_d[0].rearrange("(c p) -> p c", p=128),
                                                   allow_slow_non_contiguous=True),
                       writes=[b_gain], dma=ds_g)
                for c in range(8):
                    sc.add("dve", lambda e, c=c: e.tensor_scalar(out=wqkv[:, c, :], in0=wqkv[:, c, :],
                                                                 scalar1=gcol[:, c:c + 1], scalar2=None, op0=ALU.mult),
                           reads=[b_w, b_gain], writes=[b_w] if c == 0 else [],
                           deps=list(b_w.w.values()) if c > 0 else [])
                b_w.w = {}; _merge(b_w.w, (sc.esem["dve"], sc.ecnt["dve"]))
                for gi, gd in ((0, a_qn_d), (2, a_kn_d)):
                    sc.add("sp", lambda e, gi=gi, gd=gd: e.dma_start(
                        out=gq[:, gi, :], in_=gd[None, :].broadcast_to([128, 64])), dma=ds_gq)
                    for a in range(2):
                        for hf in range(2):
                            o0 = a * 32 + hf * 16
                            s0 = a * 32 + (1 - hf) * 16
                            sc.add("sp", lambda e, gi=gi, gd=gd, o0=o0, s0=s0: e.dma_start(
                                out=gq[:, gi + 1, o0:o0 + 16],
                                in_=gd[None, s0:s0 + 16].broadcast_to([128, 16])), dma=ds_gq)
                b_gq.w = {}; _merge(b_gq.w, (ds_gq.sem, ds_gq.cnt))
                sc.add("pool", lambda e: e.tensor_scalar(out=gq[:, 0:2, :], in0=gq[:, 0:2, :], scalar1=0.125,
                                                         scalar2=None, op0=ALU.mult),
                       reads=[b_gq], writes=[b_gq])

                xv = x_d.rearrange("(t p) d -> t p d", p=128)
                rv = rope_d.rearrange("(t p) d -> t p d", p=128)
                qTv = qT_d
                raw2 = [raw, raw_b]
                b_raw2 = [b_raw, Buf()]

                def load_x(t):
                    if t >= NT:
                        return
                    x3 = t % 3
                    sc.add("sp", lambda e: e.dma_start(out=xs[x3][:], in_=xv[t]), writes=[b_xs[x3]], dma=ds_x[x3])
                    sc.add("sp", lambda e: e.dma_start(out=rp[x3][:], in_=rv[t]), writes=[b_rp[x3]], dma=ds_rp[x3])

                def stage1(t):
                    sl = t % 2
                    x3 = t % 3
                    sc.add("act", lambda e: e.activation(out=junk[:], in_=xs[x3][:], func=AF.Square,
                                                         accum_out=ssq[:, t:t + 1]),
                           reads=[b_xs[x3]], writes=[b_ssq])
                    yield
                    rstd_ops(ssq[:, t:t + 1], tmp1[:, t:t + 1], rstd[:, t:t + 1], None, 1.0 / D,
                             b_ssq, b_tmp1, b_rstd)
                    yield
                    sc.add("act", lambda e: e.activation(out=xn[sl][:], in_=xs[x3][:], func=AF.Copy,
                                                         scale=rstd[:, t:t + 1]),
                           reads=[b_xs[x3], b_rstd], writes=[b_xn[sl]])
                    yield
                    group0("pe", [lambda e, c=c: e.transpose(out=tpx[:, c, :], in_=xn[sl][:, c * 128:(c + 1) * 128],
                                                             identity=ident[:]) for c in range(8)],
                           reads=[b_xn[sl], b_ident], writes=[b_tpx])
                    yield
                    sc.add("act", lambda e: e.copy(out=hT[sl][:], in_=tpx[:]), reads=[b_tpx], writes=[b_hT[sl]])
                    yield
                    group0("pe", [lambda e, c=c, h3=h3: e.matmul(
                        pqkv[:, h3 * 512:(h3 + 1) * 512], lhsT=hT[sl][:, c, :],
                        rhs=wqkv[:, c, h3 * 512:(h3 + 1) * 512], start=(c == 0), stop=(c == 7))
                        for h3 in range(3) for c in range(8)],
                        reads=[b_hT[sl], b_w], writes=[b_pqkv])
                    yield
                    pq3 = pqkv[:, 0:1280].rearrange("p (h d) -> p h d", d=64)
                    sc.add("act", lambda e: e.copy(out=raw2[sl][:], in_=pq3), reads=[b_pqkv], writes=[b_raw2[sl]])
                    yield
                    sc.add("dve", lambda e: e.tensor_copy(out=VP[:, t, :, 0:64],
                                                          in_=pqkv[:, 1280:1536].rearrange("p (g d) -> p g d", d=64)),
                           reads=[b_pqkv], writes=[b_VP[t]])
                    yield

                def stage2(t):
                    sl = t % 2
                    rw = raw2[sl]
                    b_rw = b_raw2[sl]
                    sc.add("act", lambda e: e.activation(out=sqb[:], in_=rw[:], func=AF.Square),
                           reads=[b_rw], writes=[b_sqb])
                    yield
                    sc.add("dve", lambda e: e.tensor_reduce(out=ssqh[:], in_=sqb[:], axis=AX.X, op=ALU.add),
                           reads=[b_sqb], writes=[b_ssqh])
                    yield
                    rstd_ops(ssqh[:], tmph[:], rstdh[:], None, 1.0 / HD, b_ssqh, b_tmph, b_rstdh)
                    yield
                    tb = tabs[sl]
                    x3 = t % 3
                    sc.add("pool", lambda e: e.tensor_tensor(
                        out=tb[:, 0:4:2, :], in0=gq[:, 0:4:2, :],
                        in1=rp[x3][:, None, 0:64].broadcast_to([128, 2, 64]), op=ALU.mult),
                        reads=[b_rp[x3], b_gq], writes=[b_tabs[sl]])
                    yield
                    sc.add("pool", lambda e: e.tensor_tensor(
                        out=tb[:, 1:4:2, :], in0=gq[:, 1:4:2, :],
                        in1=rp[x3][:, None, 64:128].broadcast_to([128, 2, 64]), op=ALU.mult),
                        reads=[b_rp[x3], b_gq], writes=[b_tabs[sl]])
                    yield
                    sc.add("dve", lambda e: e.tensor_tensor(
                        out=ra[:, 0:16, :], in0=rw[:, 0:16, :],
                        in1=tb[:, 0:1, :].broadcast_to([128, 16, 64]), op=ALU.mult),
                        reads=[b_rw, b_tabs[sl]], writes=[b_ra])
                    yield
                    sc.add("dve", lambda e: e.tensor_tensor(
                        out=ra[:, 16:20, :], in0=rw[:, 16:20, :],
                        in1=tb[:, 2:3, :].broadcast_to([128, 4, 64]), op=ALU.mult),
                        reads=[b_rw, b_tabs[sl]], writes=[b_ra])
                    yield
                    raw5 = rw[:].rearrange("p h (a f i) -> p h a f i", a=2, f=2)
                    rb5 = rb[:].rearrange("p h (a f i) -> p h a f i", a=2, f=2)
                    for (h0, h1, ti) in ((0, 16, 1), (16, 20, 3)):
                        tb5 = tb[:, ti, :].rearrange("p (a f i) -> p a f i", a=2, f=2)
                        for hf in range(2):
                            sc.add("pool", lambda e, h0=h0, h1=h1, hf=hf, tb5=tb5: e.tensor_tensor(
                                out=rb5[:, h0:h1, :, hf, :], in0=raw5[:, h0:h1, :, 1 - hf, :],
                                in1=tb5[:, None, :, hf, :].broadcast_to([128, h1 - h0, 2, 16]), op=ALU.mult),
                                reads=[b_rw, b_tabs[sl]], writes=[b_rb])
                    sc.add("dve", lambda e: e.tensor_tensor(out=rc[:], in0=ra[:], in1=rb[:], op=ALU.add),
                           reads=[b_ra, b_rb], writes=[b_rc])
                    yield
                    sc.add("dve", lambda e: e.tensor_tensor(
                        out=qk[sl][:], in0=rc[:], in1=rstdh[:, :, None].broadcast_to([128, 20, 64]), op=ALU.mult),
                        reads=[b_rc, b_rstdh], writes=[b_qk[sl]])
                    yield
                    fns = []
                    for cc in range(8):
                        h0 = 8 * (cc // 4) + (cc % 4)
                        fns.append(lambda e, cc=cc, h0=h0: e.transpose(out=tpq[0:64, cc, :], in_=qk[sl][:, h0, :],
                                                                       identity=ident[:]))
                        fns.append(lambda e, cc=cc, h0=h0: e.transpose(out=tpq[64:128, cc, :], in_=qk[sl][:, h0 + 4, :],
                                                                       identity=ident[:]))
                    fns += [lambda e, cc=cc: e.transpose(out=tpk[:, cc, :], in_=qk[sl][:, 16 + 2 * cc:18 + 2 * cc, :],
                                                         identity=ident[:]) for cc in range(2)]
                    group0("pe", fns, reads=[b_qk[sl]], writes=[b_tpq, b_tpk])
                    yield
                    sc.add("act", lambda e: e.copy(out=qTs[sl][:], in_=tpq[:]), reads=[b_tpq], writes=[b_qTs[sl]])
                    yield
                    sc.add("dve", lambda e: e.tensor_copy(out=KT[:, :, t * 128:(t + 1) * 128], in_=tpk[:]),
                           reads=[b_tpk], writes=[b_KT[t]])
                    yield
                    sc.add("sp", lambda e: e.dma_start(out=qTv[:, :, t * 128:(t + 1) * 128], in_=qTs[sl][:]),
                           reads=[b_qTs[sl]], dma=ds_q[sl])
                    yield

                def run_zip(*gens):
                    gens = [g for g in gens if g is not None]
                    while gens:
                        for g in list(gens):
                            try:
                                next(g)
                            except StopIteration:
                                gens.remove(g)

                load_x(0)
                load_x(1)
                load_x(2)
                run_zip(stage1(0))
                for t in range(NT):
                    run_zip(stage1(t + 1) if t + 1 < NT else None, stage2(t))
                    load_x(t + 3)
                fin = [(d.sem, d.cnt) for d in ds_q]
                sc.add("pool", lambda e: e.memset(negh[:, 31:32], -0.5), deps=fin)
                sc.flush(blk)

            with ExitStack() as st, nc.Block("B0") as blk:
                NQS = 3
                NPT = 3
                qsA = [sb(st, "qsA%d" % i, [128, 512], BF16) for i in range(NQS)]
                qsB = [sb(st, "qsB%d" % i, [128, 512], BF16) for i in range(NQS)]
                PT = [sb(st, "PT%d" % i, [128, 1024], BF16) for i in range(NPT)]
                Osb = sb(st, "Osb", [65, 2, 512], F32)
                rcp = sb(st, "rcp", [65, 2, 512], F32)
                rhi = sb(st, "rhi", [65, 2, 512], BF16)
                rlo = sb(st, "rlo", [65, 2, 512], BF16)
                ost = [sb(st, "ost%d" % i, [64, 2, 512], BF16) for i in range(2)]
                NSS = 3
                Sps = [ps(st, "Sps%d" % i, [128, 1024], F32) for i in range(NSS)]
                Ops = ps(st, "Ops", [128, 1024], F32)
                b_qs = [Buf() for _ in range(NQS)]; b_PT = [Buf() for _ in range(NPT)]
                b_S = [Buf() for _ in range(3)]; b_O = Buf(); b_Osb = Buf(); b_rcp = Buf()
                b_rhl = Buf(); b_rhl2 = Buf(); b_ost = [Buf() for _ in range(2)]
                ds_qs = [sc.dsem() for _ in range(NQS)]; ds_o = [sc.dsem() for _ in range(2)]

                units = [(s, cc, kt) for s in range(NST) for cc in range(8) for kt in range(NT)]
                NU = len(units)
                pairs = [(s, cc) for s in range(NST) for cc in range(8)]

                for i in range(NQS):
                    sc.add("pool", lambda e, i=i: e.memset(qsA[i][64:128, :], 0.0), writes=[b_qs[i]])
                    sc.add("pool", lambda e, i=i: e.memset(qsB[i][0:64, :], 0.0), writes=[b_qs[i]])

                def load_q(pi):
                    s, cc = pairs[pi]
                    sl = pi % NQS
                    sc.add("sp", lambda e, s=s, cc=cc, sl=sl: e.dma_start(
                        out=qsA[sl][0:64, :], in_=qT_d[0:64, cc, s * 512:(s + 1) * 512]),
                        writes=[b_qs[sl]], dma=ds_qs[sl])
                    sc.add("sp", lambda e, s=s, cc=cc, sl=sl: e.dma_start(
                        out=qsB[sl][64:128, :], in_=qT_d[64:128, cc, s * 512:(s + 1) * 512]),
                        dma=ds_qs[sl])
                    set_w0(b_qs[sl], ds_qs[sl])

                def qk_mm(u):
                    s, cc, kt = units[u]
                    pi = s * 8 + cc
                    sl = pi % NQS
                    ss = u % NSS
                    P_ = cc // 4
                    sc.add("pe", lambda e, ss=ss, sl=sl, P_=P_, kt=kt: e.matmul(
                        Sps[ss][:, 0:512], lhsT=KT[:, P_, kt * 128:(kt + 1) * 128], rhs=qsA[sl][:],
                        start=True, stop=True),
                        deps=list(b_qs[sl].w.values()) + list(b_S[ss].r.values()) + list(b_S[ss].w.values()),
                        sig=False)
                    sc.add("pe", lambda e, ss=ss, sl=sl, P_=P_, kt=kt: e.matmul(
                        Sps[ss][:, 512:1024], lhsT=KT[:, P_, kt * 128:(kt + 1) * 128], rhs=qsB[sl][:],
                        start=True, stop=True),
                        reads=[b_qs[sl]], writes=[b_S[ss]])

                def exp_op(u):
                    ss = u % NSS
                    pp = u % NPT
                    sc.add("act", lambda e, ss=ss, pp=pp: e.activation(out=PT[pp][:], in_=Sps[ss][:], func=AF.Exp),
                           reads=[b_S[ss]], writes=[b_PT[pp]])

                b_Ob = [Buf(), Buf()]
                b_Osb2 = [Buf(), Buf()]
                b_ost2 = [[Buf(), Buf()] for _ in range(2)]
                ds_o2 = [[sc.dsem(), sc.dsem()] for _ in range(2)]

                def pv_mm(u):
                    s, cc, kt = units[u]
                    pp = u % NPT
                    P_ = cc // 4
                    first = (kt == 0)
                    last = (kt == NT - 1)
                    sc.add("pe", lambda e, pp=pp, P_=P_, kt=kt: e.matmul(
                        Ops[0:65, 0:512], lhsT=VP[:, kt, 2 * P_, :], rhs=PT[pp][:, 0:512],
                        start=(kt == 0), stop=(kt == NT - 1)),
                        deps=list(b_PT[pp].w.values()) + (list(b_Ob[0].r.values()) if first else []), sig=False)
                    sc.add("pe", lambda e, pp=pp, P_=P_, kt=kt: e.matmul(
                        Ops[0:65, 512:1024], lhsT=VP[:, kt, 2 * P_ + 1, :], rhs=PT[pp][:, 512:1024],
                        start=(kt == 0), stop=(kt == NT - 1)),
                        reads=[b_PT[pp]], writes=[b_Ob[0], b_Ob[1]] if last else [],
                        deps=list(b_Ob[1].r.values()) if first else [])

                def pair_end_a(pi):
                    for a in range(2):
                        sc.add("dve", lambda e, a=a: e.tensor_copy(out=Osb[:, a, :], in_=Ops[0:65, a * 512:(a + 1) * 512]),
                               reads=[b_Ob[a]], writes=[b_Osb2[a]])
                    sc.add("dve", lambda e: e.reciprocal(out=rcp[64:65, :, :], in_=Osb[64:65, :, :]),
                           reads=[b_Osb2[0], b_Osb2[1]], writes=[b_rcp])
                    sc.add("dve", lambda e: e.tensor_copy(out=rhi[64:65, :, :], in_=rcp[64:65, :, :]),
                           reads=[b_rcp], writes=[b_rhl])
                    sc.add("dve", lambda e: e.tensor_tensor(out=rlo[64:65, :, :], in0=rcp[64:65, :, :],
                                                            in1=rhi[64:65, :, :], op=ALU.subtract),
                           reads=[b_rcp, b_rhl], writes=[b_rhl2])

                def pair_end_b(pi, x, a):
                    Rps = Sps[x][0:64, a * 512:(a + 1) * 512]
                    b_R = b_S[x]
                    s, cc = pairs[pi]
                    P_, j_ = cc // 4, cc % 4
                    h = (8 * P_ + j_, 8 * P_ + 4 + j_)[a]
                    os_ = pi % 2
                    fns = [lambda e: e.matmul(Rps, lhsT=ones_b[64:65, 0:64], rhs=rhi[64:65, a, :], start=True, stop=False),
                           lambda e: e.matmul(Rps, lhsT=ones_b[64:65, 0:64], rhs=rlo[64:65, a, :], start=False, stop=True)]
                    group0("pe", fns, reads=[b_rhl, b_rhl2, b_const], writes=[b_R])
                    sc.add("dve", lambda e: e.tensor_tensor(out=ost[os_][:, a, :], in0=Osb[0:64, a, :], in1=Rps,
                                                            op=ALU.mult),
                           reads=[b_Osb2[a], b_R], writes=[b_ost2[os_][a]])
                    p0 = (h % 2) * 64
                    sc.add("sp", lambda e: e.dma_start(
                        out=oT_d[p0:p0 + 64, h // 2, s * 512:(s + 1) * 512], in_=ost[os_][:, a, :]),
                        reads=[b_ost2[os_][a]], dma=ds_o2[os_][a])

                LAG = min(14, NT - 2)
                per_pair = -(-len(scatter_jobs) // len(pairs))
                pend = {}
                tail = []
                load_q(0)
                if len(pairs) > 1:
                    load_q(1)
                qk_mm(0)
                exp_op(0)
                for u0 in (1, 2):
                    if u0 < NU:
                        qk_mm(u0)
                for u in range(NU):
                    s, cc, kt = units[u]
                    pi = s * 8 + cc
                    if u + 1 < NU:
                        exp_op(u + 1)
                    if kt == 1:
                        for _ in range(per_pair):
                            if scatter_jobs:
                                sc.add("sp", scatter_jobs.pop(0), deps=pre, dma=ds_sc)
                    if u in pend:
                        pi_b, a_b = pend.pop(u)
                        pair_end_b(pi_b, u % NSS, a_b)
                    if u + 3 < NU:
                        s2, cc2, kt2 = units[u + 3]
                        if kt2 == 0 and (s2 * 8 + cc2) + 1 < len(pairs):
                            load_q(s2 * 8 + cc2 + 1)
                        qk_mm(u + 3)
                    pv_mm(u)
                    if kt == NT - 1:
                        pair_end_a(pi)
                        for a_b in range(2):
                            k_ = u + LAG + a_b
                            if k_ <= NU - 1:
                                pend[k_] = (pi, a_b)
                            else:
                                tail.append((pi, a_b))
                for (pi_b, a_b) in tail:
                    pair_end_b(pi_b, 0, a_b)
                while scatter_jobs:
                    sc.add("sp", scatter_jobs.pop(0), deps=pre, dma=ds_sc)
                etab_done = [(ds_sc.sem, ds_sc.cnt)]
                fin = [(d.sem, d.cnt) for dd in ds_o2 for d in dd]
                sc.add("pool", lambda e: e.memset(negh[:, 31:32], -0.5), deps=fin)
                sc.flush(blk)

        def group(eng, fns, reads=(), writes=(), per_deps=None):
            n = len(fns)
            for i, fn in enumerate(fns):
                pd = list(per_deps[i]) if per_deps is not None else []
                if i == n - 1:
                    sc.add(eng, fn, reads=reads, writes=writes, deps=pd)
                elif i == 0:
                    deps = [t for b in reads for t in b.w.values()]
                    deps += [t for b in writes for t in list(b.w.values()) + list(b.r.values())]
                    sc.add(eng, fn, deps=deps + pd, sig=False)
                else:
                    sc.add(eng, fn, deps=pd, sig=False)

        def set_w(buf, dsem):
            buf.w = {}
            buf.r = {}
            _merge(buf.w, (dsem.sem, dsem.cnt))

        def norm_front(xt_ap, b_x, gain_t, b_g, ssq_c, tmp_c, rstd_c, bs, xn_t, b_xn_, tp_t, b_tp_, dst_ap, b_dst,
                       junk_t):
            b_ssq_, b_tmp_, b_rstd_ = bs
            sc.add("act", lambda e: e.activation(out=junk_t[:], in_=xt_ap, func=AF.Square, accum_out=ssq_c),
                   reads=[b_x], writes=[b_ssq_])
            rstd_ops(ssq_c, tmp_c, rstd_c, negh[:, 0:1], 1.0 / D, b_ssq_, b_tmp_, b_rstd_)
            sc.add("dve", lambda e: e.scalar_tensor_tensor(out=xn_t[:], in0=xt_ap, scalar=rstd_c, in1=gain_t[:],
                                                           op0=ALU.mult, op1=ALU.mult),
                   reads=[b_x, b_rstd_, b_g], writes=[b_xn_])
            group("pe", [lambda e, c=c: e.transpose(out=tp_t[:, c, :], in_=xn_t[:, c * 128:(c + 1) * 128],
                                                    identity=ident[:]) for c in range(8)],
                  reads=[b_xn_, b_ident], writes=[b_tp_])
            sc.add("act", lambda e: e.copy(out=dst_ap, in_=tp_t[:]), reads=[b_tp_], writes=[b_dst])

        def run_lag(g0, g1, lag):
            alive0 = alive1 = True
            for _ in range(lag):
                try:
                    next(g0)
                except StopIteration:
                    alive0 = False
            while alive0 or alive1:
                if alive0:
                    try:
                        next(g0)
                    except StopIteration:
                        alive0 = False
                if alive1:
                    try:
                        next(g1)
                    except StopIteration:
                        alive1 = False

        def norm_front_gen(xt_ap, b_x, gain_t, b_g, ssq_c, tmp_c, rstd_c, bs, xn_t, b_xn_, tp_t, b_tp_, dst_ap, b_dst,
                           junk_t, b_junk):
            b_ssq_, b_tmp_, b_rstd_ = bs
            sc.add("act", lambda e: e.activation(out=junk_t[:], in_=xt_ap, func=AF.Square, accum_out=ssq_c),
                   reads=[b_x], writes=[b_ssq_, b_junk])
            yield
            rstd_ops(ssq_c, tmp_c, rstd_c, None, 1.0 / D, b_ssq_, b_tmp_, b_rstd_)
            yield
            sc.add("dve", lambda e: e.scalar_tensor_tensor(out=xn_t[:], in0=xt_ap, scalar=rstd_c, in1=gain_t[:],
                                                           op0=ALU.mult, op1=ALU.mult),
                   reads=[b_x, b_rstd_, b_g], writes=[b_xn_])
            yield
            group("pe", [lambda e, c=c: e.transpose(out=tp_t[:, c, :], in_=xn_t[:, c * 128:(c + 1) * 128],
                                                    identity=ident[:]) for c in range(8)],
                  reads=[b_xn_, b_ident], writes=[b_tp_])
            yield
            sc.add("act", lambda e: e.copy(out=dst_ap, in_=tp_t[:]), reads=[b_tp_], writes=[b_dst])
            yield

        def post_phase(L, src_d, wo_d, dst_d, final):
            NSP = S // 256
            with ExitStack() as st, nc.Block("P%d" % L) as blk:
                Wo = sb(st, "Wo%d" % L, [128, 8, D], BF16)
                Win = sb(st, "Win%d" % L, [128, 8, DFF], BF16)
                Wout = sb(st, "Wout%d" % L, [128, 32, D], BF16)
                gm = sb(st, "gm%d" % L, [128, D], F32)
                gf = sb(st, "gf%d" % L, [128, D], F32) if final else None
                xm = [sb(st, "xm%d_%d" % (L, i), [128, 2, D], F32) for i in range(2)]
                oTs = [sb(st, "oTs%d_%d" % (L, i), [128, 8, 256], BF16) for i in range(2)]
                xn_ = [sb(st, "pxn%d_%d" % (L, i), [128, D], BF16) for i in range(2)]
                h2T = sb(st, "h2T%d" % L, [128, 8, 256], BF16)
                uT = sb(st, "uT%d" % L, [128, 32, 256], BF16)
                rsb = [sb(st, "rsb%d_%d" % (L, i), [128, 256], F32) for i in range(2)]
                junk_ = sb(st, "pjunk%d" % L, [128, D], BF16)
                stt = sb(st, "pstat%d" % L, [128, 6, NT], F32)
                yps = [ps(st, "yps%d_%d" % (L, i), [128, D], F32) for i in range(2)]
                tp_ = ps(st, "ptp%d" % L, [128, 8, 128], BF16)
                ups = [ps(st, "ups%d_%d" % (L, i), [128, 512], F32) for i in range(3)]
                b_Wo = Buf(); b_Win = Buf(); b_Wout = Buf(); b_gm = Buf(); b_gf = Buf()
                b_xm = [[Buf() for _ in range(2)] for _ in range(2)]
                b_oTs = [Buf() for _ in range(2)]; b_pxn = [Buf() for _ in range(2)]
                b_h2T = [Buf() for _ in range(2)]; b_uT = [Buf() for _ in range(32)]
                b_rsb = [Buf() for _ in range(2)]; b_st = [Buf() for _ in range(6)]
                b_yps = [Buf() for _ in range(2)]; b_tp = Buf(); b_ups = [Buf() for _ in range(3)]
                ds_wo = sc.dsem(); ds_win = sc.dsem(); ds_wout = sc.dsem(); ds_g = sc.dsem()
                ds_xl = [sc.dsem() for _ in range(2)]; ds_ol = [sc.dsem() for _ in range(2)]
                ds_st = [[sc.dsem() for _ in range(2)] for _ in range(2)]

                wov = wo_d.rearrange("(c p) m -> p c m", p=128)
                for c in range(8):
                    sc.add("pool", lambda e, c=c: e.dma_start(out=Wo[:, c, :], in_=wov[:, c, :]), dma=ds_wo)
                set_w(b_Wo, ds_wo)
                sc.add("sp", lambda e: e.dma_start(out=gm[:], in_=norm_mlp_d[L:L + 1, :].broadcast_to([128, D])), dma=ds_g)
                set_w(b_gm, ds_g)
                if final:
                    sc.add("sp", lambda e: e.dma_start(out=gf[:], in_=norm_final_d[None, :].broadcast_to([128, D])),
                           dma=ds_g)
                    set_w(b_gf, ds_g); set_w(b_gm, ds_g)
                winv = w_in_d[L].rearrange("(c p) m -> p c m", p=128)
                for c in range(8):
                    for hh in range(2):
                        sc.add("pool", lambda e, c=c, hh=hh: e.dma_start(
                            out=Win[:, c, hh * 2048:(hh + 1) * 2048], in_=winv[:, c, hh * 2048:(hh + 1) * 2048]),
                            dma=ds_win)
                set_w(b_Win, ds_win)
                woutv = w_out_d[L].rearrange("(c p) m -> p c m", p=128)
                for c in range(32):
                    sc.add("pool", lambda e, c=c: e.dma_start(out=Wout[:, c, :], in_=woutv[:, c, :]), dma=ds_wout)
                set_w(b_Wout, ds_wout)

                srcv = src_d.rearrange("(n i p) d -> n p i d", p=128, i=2)
                dstv = dst_d.rearrange("(n i p) d -> n p i d", p=128, i=2)

                def loads(n):
                    sl = n % 2
                    sc.add("sp", lambda e, n=n, sl=sl: e.dma_start(out=xm[sl][:], in_=srcv[n]),
                           writes=[b_xm[sl][0], b_xm[sl][1]], dma=ds_xl[sl])
                    sc.add("sp", lambda e, n=n, sl=sl: e.dma_start(out=oTs[sl][:], in_=oT_d[:, :, n * 256:(n + 1) * 256]),
                           writes=[b_oTs[sl]], dma=ds_ol[sl])

                b_stt = [[Buf() for _ in range(3)] for _ in range(2)]
                b_junkP = Buf()

                def tile_front(n, i):
                    sl = n % 2
                    t = 2 * n + i
                    fns = [lambda e, c=c, hf=hf: e.matmul(
                        yps[i][:, hf * 512:(hf + 1) * 512], lhsT=oTs[sl][:, c, i * 128:(i + 1) * 128],
                        rhs=Wo[:, c, hf * 512:(hf + 1) * 512], start=(c == 0), stop=(c == 7))
                        for hf in range(2) for c in range(8)]
                    group("pe", fns, reads=[b_oTs[sl], b_Wo], writes=[b_yps[i]])
                    yield
                    sc.add("dve", lambda e: e.tensor_tensor(out=xm[sl][:, i, :], in0=yps[i][:],
                                                            in1=xm[sl][:, i, :], op=ALU.add),
                           reads=[b_yps[i]], writes=[b_xm[sl][i]])
                    yield
                    yield from norm_front_gen(
                        xm[sl][:, i, :], b_xm[sl][i], gm, b_gm, stt[:, 0, t:t + 1], stt[:, 1, t:t + 1],
                        stt[:, 2, t:t + 1], (b_stt[i][0], b_stt[i][1], b_stt[i][2]), xn_[i], b_pxn[i], tp_, b_tp,
                        h2T[:, :, i * 128:(i + 1) * 128], b_h2T[i], junk_, b_junkP)

                loads(0)
                for n in range(NSP):
                    sl = n % 2
                    if n + 1 < NSP:
                        loads(n + 1)
                    run_lag(tile_front(n, 0), tile_front(n, 1), 2)
                    for m in range(32):
                        us = m % 3
                        fns = [lambda e, c=c, m=m, us=us: e.matmul(
                            ups[us][:, 0:256], lhsT=Win[:, c, m * 128:(m + 1) * 128], rhs=h2T[:, c, :],
                            start=(c == 0), stop=(c == 7)) for c in range(8)]
                        group("pe", fns, reads=[b_h2T[0], b_h2T[1], b_Win], writes=[b_ups[us]])
                        rs_ = m % 2
                        sc.add("act", lambda e, us=us, rs_=rs_: e.activation(out=rsb[rs_][:], in_=ups[us][:, 0:256],
                                                                           func=AF.Relu),
                               reads=[b_ups[us]], writes=[b_rsb[rs_]])
                        sc.add("dve", lambda e, us=us, rs_=rs_, m=m: e.tensor_tensor(
                            out=uT[:, m, :], in0=ups[us][:, 0:256], in1=rsb[rs_][:], op=ALU.mult),
                            reads=[b_ups[us], b_rsb[rs_]], writes=[b_uT[m]])
                    for i in range(2):
                        t = 2 * n + i
                        fns = [lambda e, k=k, hf=hf, i=i: e.matmul(
                            yps[i][:, hf * 512:(hf + 1) * 512], lhsT=uT[:, k, i * 128:(i + 1) * 128],
                            rhs=Wout[:, k, hf * 512:(hf + 1) * 512], start=(k == 0), stop=(k == 31))
                            for hf in range(2) for k in range(32)]
                        pdeps = [list(b_uT[k].w.values()) for hf in range(2) for k in range(32)]
                        group("pe", fns, reads=[b_Wout] + (b_uT if i == 1 else []), writes=[b_yps[i]], per_deps=pdeps)
                        sc.add("dve", lambda e, i=i, sl=sl: e.tensor_tensor(out=xm[sl][:, i, :], in0=yps[i][:],
                                                                          in1=xm[sl][:, i, :], op=ALU.add),
                               reads=[b_yps[i]], writes=[b_xm[sl][i]])
                        if final:
                            sc.add("act", lambda e, i=i, sl=sl, t=t: e.activation(
                                out=junk_[:], in_=xm[sl][:, i, :], func=AF.Square, accum_out=stt[:, 3, t:t + 1]),
                                reads=[b_xm[sl][i]], writes=[b_st[3]])
                            rstd_ops(stt[:, 3, t:t + 1], stt[:, 4, t:t + 1], stt[:, 5, t:t + 1], negh[:, 0:1], 1.0 / D,
                                     b_st[3], b_st[4], b_st[5])
                            sc.add("dve", lambda e, i=i, sl=sl, t=t: e.scalar_tensor_tensor(
                                out=xm[sl][:, i, :], in0=xm[sl][:, i, :], scalar=stt[:, 5, t:t + 1], in1=gf[:],
                                op0=ALU.mult, op1=ALU.mult),
                                reads=[b_st[5], b_gf], writes=[b_xm[sl][i]])
                        sc.add("sp", lambda e, i=i, sl=sl, n=n: e.dma_start(out=dstv[n][:, i, :], in_=xm[sl][:, i, :]),
                               reads=[b_xm[sl][i]], dma=ds_st[sl][i])
                fin = [(d.sem, d.cnt) for dd in ds_st for d in dd]
                sc.add("pool", lambda e: e.memset(negh[:, 31:32], -0.5), deps=fin)
                sc.flush(blk)

        post_phase(0, x_d, a_wo_d, x1_d, False)

        with ExitStack() as st, nc.Block("A1") as blk:
            W1 = sb(st, "W1", [128, 8, 3072], BF16)
            g1 = sb(st, "g1", [128, D], F32)
            xs1 = [sb(st, "xs1_%d" % i, [128, D], F32) for i in range(2)]
            xn1 = [sb(st, "xn1_%d" % i, [128, D], BF16) for i in range(2)]
            h1T = sb(st, "h1T", [128, 8, 512], BF16)
            qkst = [sb(st, "qkst%d" % i, [128, 16, 512], BF16) for i in range(2)]
            vst = [sb(st, "vst%d" % i, [128, 4, NH, 65], BF16) for i in range(2)]
            junk1 = sb(st, "junk1", [128, D], BF16)
            st1 = sb(st, "st1", [128, 3, NT], F32)
            tp1 = ps(st, "tp1", [128, 8, 128], BF16)
            qkps = [ps(st, "qkps%d" % i, [128, 512], F32) for i in range(2)]
            vps = [ps(st, "vps%d" % i, [128, D], F32) for i in range(2)]
            b_W1 = Buf(); b_g1 = Buf(); b_xs1 = [Buf() for _ in range(2)]; b_xn1 = [Buf() for _ in range(2)]
            b_h1T = [Buf() for _ in range(4)]; b_qkst = [Buf() for _ in range(2)]; b_vst = [Buf() for _ in range(2)]
            b_s1 = [Buf() for _ in range(3)]; b_tp1 = Buf(); b_qkps = [Buf() for _ in range(2)]
            b_vps = [Buf() for _ in range(2)]
            ds_w1 = sc.dsem(); ds_g1 = sc.dsem(); ds_x1 = [sc.dsem() for _ in range(2)]
            ds_s1 = [sc.dsem() for _ in range(2)]
            ds_s1v = [sc.dsem() for _ in range(2)]
            w1v = b_wqkv_d.rearrange("(c p) m -> p c m", p=128)
            for c in range(8):
                for hh in range(2):
                    sc.add("pool", lambda e, c=c, hh=hh: e.dma_start(
                        out=W1[:, c, hh * 1536:(hh + 1) * 1536], in_=w1v[:, c, hh * 1536:(hh + 1) * 1536]), dma=ds_w1)
            set_w(b_W1, ds_w1)
            sc.add("sp", lambda e: e.dma_start(out=g1[:], in_=norm_mix_d[1:2, :].broadcast_to([128, D])), dma=ds_g1)
            set_w(b_g1, ds_g1)
            for i in range(2):
                sc.add("pool", lambda e, i=i: e.memset(vst[i][:, :, :, 64:65], 1.0), writes=[b_vst[i]])
            x1v = x1_d.rearrange("(t p) d -> t p d", p=128)
            v1v = v1_d.rearrange("(t p) f -> p t f", p=128)
            b_s1t = [[Buf() for _ in range(3)] for _ in range(2)]
            b_junk1 = Buf()

            def a1_front(s, i):
                t = 4 * s + i
                sl = t % 2
                sc.add("sp", lambda e: e.dma_start(out=xs1[sl][:], in_=x1v[t]), writes=[b_xs1[sl]], dma=ds_x1[sl])
                yield
                yield from norm_front_gen(
                    xs1[sl][:], b_xs1[sl], g1, b_g1, st1[:, 0, t:t + 1], st1[:, 1, t:t + 1],
                    st1[:, 2, t:t + 1], (b_s1t[sl][0], b_s1t[sl][1], b_s1t[sl][2]), xn1[sl], b_xn1[sl], tp1, b_tp1,
                    h1T[:, :, i * 128:(i + 1) * 128], b_h1T[i], junk1, b_junk1)

            for s in range(NST):
                ss = s % 2
                for i0 in (0, 2):
                    run_lag(a1_front(s, i0), a1_front(s, i0 + 1), 2)
                for mc in range(16):
                    qs_ = mc % 2
                    fns = [lambda e, c=c, mc=mc, qs_=qs_: e.matmul(
                        qkps[qs_][:], lhsT=W1[:, c, mc * 128:(mc + 1) * 128], rhs=h1T[:, c, :],
                        start=(c == 0), stop=(c == 7)) for c in range(8)]
                    group("pe", fns, reads=b_h1T + [b_W1], writes=[b_qkps[qs_]])
                    scale = 0.125 if mc < 8 else 1.0
                    sc.add("act", lambda e, mc=mc, qs_=qs_, ss=ss, scale=scale: e.mul(
                        out=qkst[ss][:, mc, :], in_=qkps[qs_][:], mul=scale),
                        reads=[b_qkps[qs_]], writes=[b_qkst[ss]] if mc == 0 else [],
                        deps=list(b_qkst[ss].w.values()) if mc > 0 else [])
                b_qkst[ss].w = {}; _merge(b_qkst[ss].w, (sc.esem["act"], sc.ecnt["act"]))
                for i in range(4):
                    vs_ = i % 2
                    fns = [lambda e, c=c, hf=hf, i=i, vs_=vs_: e.matmul(
                        vps[vs_][:, hf * 512:(hf + 1) * 512], lhsT=h1T[:, c, i * 128:(i + 1) * 128],
                        rhs=W1[:, c, 2048 + hf * 512:2048 + (hf + 1) * 512], start=(c == 0), stop=(c == 7))
                        for hf in range(2) for c in range(8)]
                    group("pe", fns, reads=[b_h1T[i], b_W1], writes=[b_vps[vs_]])
                    sc.add("dve", lambda e, i=i, vs_=vs_, ss=ss: e.tensor_copy(
                        out=vst[ss][:, i, :, 0:64], in_=vps[vs_][:].rearrange("p (h d) -> p h d", d=64)),
                        reads=[b_vps[vs_]], writes=[b_vst[ss]] if i == 0 else [],
                        deps=list(b_vst[ss].w.values()) if i > 0 else [])
                b_vst[ss].w = {}; _merge(b_vst[ss].w, (sc.esem["dve"], sc.ecnt["dve"]))
                sc.add("sp", lambda e, s=s, ss=ss: e.dma_start(out=qT_d[:, :, s * 512:(s + 1) * 512], in_=qkst[ss][:, 0:8, :]),
                       reads=[b_qkst[ss]], dma=ds_s1[ss])
                sc.add("sp", lambda e, s=s, ss=ss: e.dma_start(out=kT1_d[:, :, s * 512:(s + 1) * 512], in_=qkst[ss][:, 8:16, :]),
                       reads=[b_qkst[ss]], dma=ds_s1[ss])
                sc.add("sp", lambda e, s=s, ss=ss: e.dma_start(
                    out=v1v[:, 4 * s:4 * s + 4, :], in_=vst[ss][:].rearrange("p i h e -> p i (h e)")),
                    reads=[b_vst[ss]], dma=ds_s1v[ss])
            fin = [(d.sem, d.cnt) for d in ds_s1 + ds_s1v]
            sc.add("pool", lambda e: e.memset(negh[:, 31:32], -0.5), deps=fin)
            sc.flush(blk)

        with ExitStack() as st, nc.Block("B1") as blk:
            NK = 7
            Eful = sb(st, "Eful", [128, NH, 14, 64], BF16)
            Eint = sb(st, "Eint", [128, NH, 10, 64], BF16)
            stg = sb(st, "Estg", [128, NH * 14 * 64], F32)
            q1s = [sb(st, "q1s%d" % i, [128, 8, 128], BF16) for i in range(2)]
            k1r = [sb(st, "k1r%d" % i, [128, 8, 128], BF16) for i in range(NK)]
            v1r = [sb(st, "v1r%d" % i, [128, NH, 65], BF16) for i in range(NK)]
            P0 = [sb(st, "P0_%d" % i, [128, 1024], BF16) for i in range(2)]
            PTn = [sb(st, "PTn%d" % i, [128, 1024], BF16) for i in range(3)]
            rc8 = sb(st, "rc8", [128, 2, 4], F32)
            o1tm = [sb(st, "o1tm%d" % i, [128, NH, 64], BF16) for i in range(2)]
            o1Ts = [sb(st, "o1Ts%d" % i, [128, 8, 128], BF16) for i in range(2)]
            S1 = [ps(st, "S1_%d" % i, [128, 1024], F32) for i in range(2)]
            O1 = ps(st, "O1", [128, 2, 512], F32)
            tpo = ps(st, "tpo", [128, 8, 128], BF16)
            b_Ef = Buf(); b_Ei = Buf(); b_stg = Buf()
            b_q1s = [Buf() for _ in range(2)]; b_k1r = [Buf() for _ in range(NK)]; b_v1r = [Buf() for _ in range(NK)]
            b_P0 = [Buf() for _ in range(2)]; b_PTn = [Buf() for _ in range(3)]; b_rc8 = Buf()
            b_o1tm = [Buf() for _ in range(2)]; b_o1Ts = [Buf() for _ in range(2)]
            b_S1 = [Buf() for _ in range(2)]; b_O1 = Buf(); b_tpo = Buf()
            ds_e = sc.dsem(); ds_q1 = [sc.dsem() for _ in range(2)]; ds_k1 = [sc.dsem() for _ in range(NK)]
            ds_o1 = [sc.dsem() for _ in range(2)]
            sc.add("sp", lambda e: e.dma_start(out=stg[:], in_=cful_d), writes=[b_stg], deps=etab_done, dma=ds_e)
            sc.add("act", lambda e: e.activation(out=Eful[:].rearrange("p h r c -> p (h r c)"), in_=stg[:], func=AF.Exp),
                   reads=[b_stg], writes=[b_Ef])
            sc.add("sp", lambda e: e.dma_start(out=stg[:, 0:NH * 10 * 64], in_=cint_d), writes=[b_stg], dma=ds_e)
            sc.add("act", lambda e: e.activation(out=Eint[:].rearrange("p h r c -> p (h r c)"),
                                                 in_=stg[:, 0:NH * 10 * 64], func=AF.Exp),
                   reads=[b_stg], writes=[b_Ei])

            def pair_cfg(i):
                if NT >= 6 and 2 <= i <= NT - 3:
                    return [i - 2 + j for j in range(5)], Eint, b_Ei, [8 - 2 * j for j in range(5)]
                if i == 0:
                    K_, t0 = 7, 0
                elif i == 1:
                    K_, t0 = 5, 0
                elif i == NT - 2:
                    K_, t0 = 3, NT - 4
                else:
                    K_, t0 = 1, NT - 4
                return [t0 + j for j in range(4)], Eful, b_Ef, [13 - K_ - 2 * j for j in range(4)]

            loaded = set()

            def load_keys(jt):
                if jt in loaded or jt < 0 or jt >= NT:
                    return
                loaded.add(jt)
                sl = jt % NK
                sc.add("sp", lambda e, jt=jt, sl=sl: e.dma_start(out=k1r[sl][:], in_=kT1_d[:, :, jt * 128:(jt + 1) * 128]),
                       writes=[b_k1r[sl]], dma=ds_k1[sl])
                sc.add("sp", lambda e, jt=jt, sl=sl: e.dma_start(
                    out=v1r[sl][:].rearrange("p h e -> p (h e)"), in_=v1_d[jt * 128:(jt + 1) * 128, :]),
                    writes=[b_v1r[sl]], dma=ds_k1[sl])
                set_w(b_k1r[sl], ds_k1[sl]); set_w(b_v1r[sl], ds_k1[sl])

            def load_q1(i):
                sl = i % 2
                sc.add("sp", lambda e, i=i, sl=sl: e.dma_start(out=q1s[sl][:], in_=qT_d[:, :, i * 128:(i + 1) * 128]),
                       writes=[b_q1s[sl]], dma=ds_q1[sl])

            units1 = []
            for i in range(NT):
                tiles, Et, b_Et, planes = pair_cfg(i)
                for hf in range(2):
                    for j, jt in enumerate(tiles):
                        units1.append((i, hf, j, jt, len(tiles), planes[j], Et, b_Et))
            NU1 = len(units1)

            def prefetch(i):
                if i >= NT:
                    return
                load_q1(i)
                for jt in pair_cfg(i)[0]:
                    load_keys(jt)

            def qk1(u):
                i, hf, j, jt, nj, p0_, Et, b_Et = units1[u]
                ksl, qsl, ss = jt % NK, i % 2, u % 2
                fns = []
                for cl in range(4):
                    c = 4 * hf + cl
                    fns.append(lambda e, c=c, cl=cl: e.matmul(
                        S1[ss][:, cl * 128:(cl + 1) * 128], lhsT=k1r[ksl][0:64, c, :], rhs=q1s[qsl][0:64, c, :],
                        start=True, stop=True))
                    fns.append(lambda e, c=c, cl=cl: e.matmul(
                        S1[ss][:, 512 + cl * 128:512 + (cl + 1) * 128], lhsT=k1r[ksl][64:128, c, :],
                        rhs=q1s[qsl][64:128, c, :], start=True, stop=True))
                group("pe", fns, reads=[b_k1r[ksl], b_q1s[qsl]], writes=[b_S1[ss]])

            def exp1(u):
                ss = u % 2
                sc.add("act", lambda e: e.activation(out=P0[ss][:], in_=S1[ss][:], func=AF.Exp),
                       reads=[b_S1[ss]], writes=[b_P0[ss]])

            def mul1(u):
                i, hf, j, jt, nj, p0_, Et, b_Et = units1[u]
                ss, pp = u % 2, u % 3
                e_ap = Et[:, 8 * hf:8 * hf + 8, p0_:p0_ + 2, :].rearrange("p (cl par) b c -> p par cl (b c)", par=2)
                meng = "dve"
                sc.add(meng, lambda e: e.tensor_tensor(
                    out=PTn[pp][:].rearrange("p (par cl q) -> p par cl q", par=2, cl=4),
                    in0=P0[ss][:].rearrange("p (par cl q) -> p par cl q", par=2, cl=4), in1=e_ap, op=ALU.mult),
                    reads=[b_P0[ss], b_Et], writes=[b_PTn[pp]])

            def pv1(u):
                i, hf, j, jt, nj, p0_, Et, b_Et = units1[u]
                ksl, pp = jt % NK, u % 3
                fns = []
                for hh in range(8):
                    cl, par = hh // 2, hh % 2
                    h = 8 * hf + hh
                    fns.append(lambda e, hh=hh, cl=cl, par=par, h=h: e.matmul(
                        O1[:, hh // 4, (hh % 4) * 65:(hh % 4) * 65 + 65],
                        lhsT=PTn[pp][:, par * 512 + cl * 128:par * 512 + (cl + 1) * 128],
                        rhs=v1r[ksl][:, h, :], start=(j == 0 and hh % 4 == 0), stop=(j == nj - 1),
                        skip_group_check=True))
                if j == 0 or j == nj - 1:
                    group("pe", fns, reads=[b_PTn[pp], b_v1r[ksl]], writes=[b_O1])
                else:
                    group("pe", fns, reads=[b_PTn[pp], b_v1r[ksl], b_O1])

            O4 = O1[:, :, 0:260].rearrange("p b (h e) -> p b h e", e=65)

            def half_end(i, hf):
                osl = i % 2
                sc.add("dve", lambda e: e.reciprocal(out=rc8[:], in_=O4[:, :, :, 64]), reads=[b_O1], writes=[b_rc8])
                sc.add("dve", lambda e: e.tensor_tensor(
                    out=o1tm[osl][:, 8 * hf:8 * hf + 8, :].rearrange("p (b h) d -> p b h d", b=2),
                    in0=O4[:, :, :, 0:64], in1=rc8[:, :, :, None].broadcast_to([128, 2, 4, 64]), op=ALU.mult),
                    reads=[b_O1, b_rc8], writes=[b_o1tm[osl]] if hf == 0 else [],
                    deps=list(b_o1tm[osl].w.values()) if hf == 1 else [])
                if hf == 1:
                    b_o1tm[osl].w = {}; _merge(b_o1tm[osl].w, (sc.esem["dve"], sc.ecnt["dve"]))
                b_O1.r = {}; _merge(b_O1.r, (sc.esem["dve"], sc.ecnt["dve"]))

            def pair_end(i):
                osl = i % 2
                group("pe", [lambda e, c=c: e.transpose(out=tpo[:, c, :], in_=o1tm[osl][:, 2 * c:2 * c + 2, :],
                                                        identity=ident[:]) for c in range(8)],
                      reads=[b_o1tm[osl]], writes=[b_tpo])
                sc.add("act", lambda e: e.copy(out=o1Ts[osl][:], in_=tpo[:]), reads=[b_tpo], writes=[b_o1Ts[osl]])
                sc.add("sp", lambda e: e.dma_start(out=oT_d[:, :, i * 128:(i + 1) * 128], in_=o1Ts[osl][:]),
                       reads=[b_o1Ts[osl]], dma=ds_o1[osl])

            prefetch(0)
            prefetch(1)
            pend1 = {}
            qk1(0)
            exp1(0)
            mul1(0)
            if NU1 > 1:
                qk1(1)
            for u in range(NU1):
                i, hf, j, jt, nj, p0_, Et, b_Et = units1[u]
                if u + 1 < NU1:
                    exp1(u + 1)
                    mul1(u + 1)
                if u + 2 < NU1:
                    i2, hf2, j2 = units1[u + 2][0:3]
                    if hf2 == 0 and j2 == 0:
                        prefetch(i2 + 1)
                    qk1(u + 2)
                pv1(u)
                if j == nj - 1:
                    half_end(i, hf)
                    if hf == 1:
                        pend1[min(u + 2, NU1 - 1)] = i
                if u in pend1:
                    pair_end(pend1.pop(u))
            fin = [(d.sem, d.cnt) for d in ds_o1]
            sc.add("pool", lambda e: e.memset(negh[:, 31:32], -0.5), deps=fin)
            sc.flush(blk)

        post_phase(1, x1_d, b_wo_d, y_d, True)
    return nc


def _rope_table(S):
    t = np.arange(S)
    row = (t // GW).astype(np.float32)
    col = (t % GW).astype(np.float32)
    inv = (10000.0 ** (-np.arange(0, 32, 2, dtype=np.float32) / 32)).astype(np.float32)
    ang = np.concatenate([row[:, None] * inv, col[:, None] * inv], axis=-1).astype(np.float32)
    cos = np.cos(ang).astype(np.float32).reshape(S, 2, 16)
    sin = np.sin(ang).astype(np.float32).reshape(S, 2, 16)
    C = np.stack([cos, cos], axis=2).reshape(S, 64)
    Sg = np.stack([-sin, sin], axis=2).reshape(S, 64)
    return np.ascontiguousarray(np.concatenate([C, Sg], axis=1).astype(np.float32))


_NC_CACHE = {}


def kernel(x_prompt, x_sample, norm_mix, norm_mlp, norm_final, a_w_qkv, a_q_norm, a_k_norm,
           a_w_o, b_w_qkv, b_rel_bias, b_w_o, mlp_w_in, mlp_w_out):
    f = lambda a: np.ascontiguousarray(np.asarray(a, dtype=np.float32))
    xp, xsm = f(x_prompt), f(x_sample)
    seqs = [xp[i] for i in range(xp.shape[0])] + [xsm[i] for i in range(xsm.shape[0])]
    S = seqs[0].shape[0]
    ROWS = S // GW
    if ROWS not in _NC_CACHE:
        _NC_CACHE[ROWS] = build(ROWS)
    nc = _NC_CACHE[ROWS]
    shared = {
        "rope": _rope_table(S),
        "ident": np.eye(128, dtype=np.float32).astype(ml_dtypes.bfloat16),
        "norm_mix": f(norm_mix), "norm_mlp": f(norm_mlp), "norm_final": f(norm_final),
        "a_w_qkv": f(a_w_qkv)[0], "a_q_norm": f(a_q_norm)[0], "a_k_norm": f(a_k_norm)[0],
        "a_w_o": f(a_w_o)[0], "b_w_qkv": f(b_w_qkv)[0], "b_rel_bias": f(b_rel_bias)[0],
        "b_w_o": f(b_w_o)[0], "mlp_w_in": f(mlp_w_in), "mlp_w_out": f(mlp_w_out),
    }
    in_maps = [dict(shared, x=np.ascontiguousarray(sq)) for sq in seqs]
    res = run_bass_kernel_spmd(nc, in_maps, core_ids=list(range(len(seqs))))
    ys = [np.asarray(r["y"], dtype=np.float32) for r in res.results]
    nb = xp.shape[0]
    return (np.stack(ys[:nb], axis=0), np.stack(ys[nb:], axis=0))
```

```python
import math
from contextlib import ExitStack

import numpy as np
import ml_dtypes

import concourse.bass as bass
import concourse.mybir as mybir
from concourse.bass_utils import run_bass_kernel_spmd

F32 = mybir.dt.float32
BF16 = mybir.dt.bfloat16
AF = mybir.ActivationFunctionType
ALU = mybir.AluOpType
AX = mybir.AxisListType

D = 1024
NH = 16
HD = 64
NKV = 4
GW = 64
DFF = 4096
EPS = 1e-6
NEG = -200.0

ENG = ("pe", "act", "dve", "pool", "sp")
BLOCK_ATTR = {"pe": "tensor", "act": "scalar", "dve": "vector", "pool": "gpsimd", "sp": "sync"}


class DSem:
    def __init__(self, sem):
        self.sem = sem
        self.cnt = 0


class Buf:
    def __init__(self, t=None):
        self.t = t
        self.w = {}
        self.r = {}


def _merge(dst, tok):
    if tok is None:
        return
    sem, val = tok
    k = id(sem)
    if k not in dst or dst[k][1] < val:
        dst[k] = (sem, val)


class Sched:
    def __init__(self, nc, stack):
        self.nc = nc
        self.stack = stack
        self.streams = {e: [] for e in ENG}
        self.esem = {e: stack.enter_context(nc.semaphore("es_" + e)) for e in ENG if e != "sp"}
        self.ecnt = {e: 0 for e in ENG}
        self.waited = {e: {} for e in ENG}
        self.nds = 0
        self.nops = 0

    def dsem(self):
        self.nds += 1
        return DSem(self.stack.enter_context(self.nc.semaphore("ds%d" % self.nds)))

    def add(self, eng, fn, reads=(), writes=(), deps=(), sig=True, dma=None):
        need = {}
        for b in reads:
            for tok in b.w.values():
                _merge(need, tok)
        for b in writes:
            for tok in b.w.values():
                _merge(need, tok)
            for tok in b.r.values():
                _merge(need, tok)
        for tok in deps:
            _merge(need, tok)
        waits = []
        wd = self.waited[eng]
        for k, (sem, val) in need.items():
            if wd.get(k, 0) >= val:
                continue
            wd[k] = val
            waits.append((sem, val))
        if dma is not None:
            dma.cnt += 16
            tok = (dma.sem, dma.cnt)
            inc = (dma.sem, 16)
        elif sig:
            self.ecnt[eng] += 1
            tok = (self.esem[eng], self.ecnt[eng])
            inc = (self.esem[eng], 1)
        else:
            tok = None
            inc = None
        self.streams[eng].append((fn, waits, inc))
        self.nops += 1
        if tok is not None:
            for b in reads:
                _merge(b.r, tok)
            for b in writes:
                b.w = {}
                b.r = {}
                _merge(b.w, tok)
        return tok

    def flush(self, block):
        for eng in ENG:
            ops = self.streams[eng]
            if not ops:
                continue

            def body(e, ops=ops):
                for fn, waits, inc in ops:
                    for sem, val in waits:
                        e.wait_ge(sem, val)
                    ins = fn(e)
                    if inc is not None:
                        ins.then_inc(inc[0], inc[1])

            getattr(block, BLOCK_ATTR[eng])(body)
            self.streams[eng] = []


def build(ROWS=128, debug=False):
    S = ROWS * GW
    NT = S // 128
    NST = S // 512
    nc = bass.Bass("TRN2", target_bir_lowering=False)
    es = ExitStack()
    with es:
        sc = Sched(nc, es)

        def din(name, shape, dt=F32):
            return nc.dram_tensor(name, list(shape), dt, kind="ExternalInput").ap()

        def dscr(name, shape, dt):
            return nc.dram_tensor(name, list(shape), dt,
                                  kind="ExternalOutput" if debug else "Internal").ap()

        x_d = din("x", [S, D])
        rope_d = din("rope", [S, 128])
        ident_d = din("ident", [128, 128], BF16)
        norm_mix_d = din("norm_mix", [2, D])
        norm_mlp_d = din("norm_mlp", [2, D])
        norm_final_d = din("norm_final", [D])
        a_wqkv_d = din("a_w_qkv", [D, 1536])
        a_qn_d = din("a_q_norm", [HD])
        a_kn_d = din("a_k_norm", [HD])
        a_wo_d = din("a_w_o", [D, D])
        b_wqkv_d = din("b_w_qkv", [D, 3072])
        b_bias_d = din("b_rel_bias", [NH, 15, 31])
        b_wo_d = din("b_w_o", [D, D])
        w_in_d = din("mlp_w_in", [2, D, DFF])
        w_out_d = din("mlp_w_out", [2, DFF, D])
        y_d = nc.dram_tensor("y", [S, D], F32, kind="ExternalOutput").ap()

        qT_d = dscr("qT_s", [128, 8, S], BF16)
        oT_d = dscr("oT_s", [128, 8, S], BF16)
        x1_d = dscr("x1_s", [S, D], F32)
        kT1_d = dscr("kT1_s", [128, 8, S], BF16)
        v1_d = dscr("v1_s", [S, NH * 65], BF16)
        trev_d = dscr("trev_s", [NH, 15, 31], F32)
        cful_d = dscr("cful_s", [128, NH * 14 * 64], F32)
        cint_d = dscr("cint_s", [128, NH * 10 * 64], F32)

        def sb(stack, name, shape, dt):
            return stack.enter_context(nc.sbuf_tensor(name, list(shape), dt))

        def ps(stack, name, shape, dt=F32):
            return stack.enter_context(nc.psum_tensor(name, list(shape), dt))

        ident = sb(es, "ident_sb", [128, 128], BF16)
        ones_f = sb(es, "ones_f", [128, 64], F32)
        negh = sb(es, "negh", [128, 32], F32)
        ones_b = sb(es, "ones_b", [128, 64], BF16)
        b_ident = Buf()
        b_const = Buf()
        ds_c = sc.dsem()
        sc.add("sp", lambda e: e.dma_start(out=ident[:], in_=ident_d), writes=[b_ident], dma=ds_c)
        sc.add("pool", lambda e: e.memset(ones_f[:], 1.0), writes=[b_const])
        sc.add("pool", lambda e: e.memset(negh[:], -0.5), writes=[b_const])
        sc.add("pool", lambda e: e.memset(ones_b[:], 1.0), writes=[b_const])

        epsb = sb(es, "epsb", [128, 1], F32)
        sc.add("pool", lambda e: e.memset(epsb[:], EPS), writes=[b_const])

        def rstd_ops(ssq_ap, tmp_ap, out_ap, nh_ap, inv_n, b_in, b_tmp, b_out):
            sc.add("act", lambda e: e.activation(out=tmp_ap, in_=ssq_ap, func=AF.Sqrt, scale=inv_n, bias=epsb[:, 0:1]),
                   reads=[b_in, b_const], writes=[b_tmp])
            sc.add("dve", lambda e: e.reciprocal(out=out_ap, in_=tmp_ap), reads=[b_tmp], writes=[b_out])

        def group0(eng, fns, reads=(), writes=(), per_deps=None):
            n = len(fns)
            for i, fn in enumerate(fns):
                pd = list(per_deps[i]) if per_deps is not None else []
                if i == n - 1:
                    sc.add(eng, fn, reads=reads, writes=writes, deps=pd)
                elif i == 0:
                    deps = [t for b in reads for t in b.w.values()]
                    deps += [t for b in writes for t in list(b.w.values()) + list(b.r.values())]
                    sc.add(eng, fn, deps=deps + pd, sig=False)
                else:
                    sc.add(eng, fn, deps=pd, sig=False)

        def set_w0(buf, dsem):
            buf.w = {}
            buf.r = {}
            _merge(buf.w, (dsem.sem, dsem.cnt))

        with ExitStack() as st, nc.Block("E0") as blk:
            tb0 = sb(st, "tb0", [16, 15, 31], F32)
            tr1 = sb(st, "tr1", [16, 15, 31], F32)
            tr2 = sb(st, "tr2", [16, 15, 31], F32)
            negt = sb(st, "negt", [128, 2048], F32)
            b_tb0 = Buf(); b_tr1 = Buf(); b_tr2 = Buf(); b_negt = Buf()
            ds_e0 = sc.dsem(); ds_fill = sc.dsem(); ds_tr = sc.dsem(); ds_sc = sc.dsem()
            sc.add("sp", lambda e: e.dma_start(out=tb0[:], in_=b_bias_d), writes=[b_tb0], dma=ds_e0)
            for i in range(31):
                sc.add("dve", lambda e, i=i: e.tensor_copy(out=tr1[:, :, i:i + 1], in_=tb0[:, :, 30 - i:31 - i]),
                       reads=[b_tb0] if i == 0 else [], writes=[b_tr1] if i == 30 else [], sig=(i == 30))
            for u in range(15):
                sc.add("dve", lambda e, u=u: e.tensor_copy(out=tr2[:, u:u + 1, :], in_=tr1[:, 14 - u:15 - u, :]),
                       reads=[b_tr1] if u == 0 else [], writes=[b_tr2] if u == 14 else [], sig=(u == 14))
            sc.add("sp", lambda e: e.dma_start(out=trev_d, in_=tr2[:]), reads=[b_tr2], dma=ds_tr)
            sc.add("pool", lambda e: e.memset(negt[:], NEG), writes=[b_negt])
            for k in range(7):
                sc.add("sp", lambda e, k=k: e.dma_start(out=cful_d[:, k * 2048:(k + 1) * 2048], in_=negt[:]),
                       reads=[b_negt], dma=ds_fill)
            for k in range(5):
                sc.add("sp", lambda e, k=k: e.dma_start(out=cint_d[:, k * 2048:(k + 1) * 2048], in_=negt[:]),
                       reads=[b_negt], dma=ds_fill)
            pre = [(ds_tr.sem, ds_tr.cnt), (ds_fill.sem, ds_fill.cnt)]
            scatter_jobs = []
            cfv = cful_d.rearrange("p (h r c) -> p h r c", h=NH, r=14)
            civ = cint_d.rearrange("p (h r c) -> p h r c", h=NH, r=10)
            for a in range(2):
                for kc in range(64):
                    cs = [c for c in range(64) if min(max(c - 8, 0), 48) <= kc <= min(max(c - 8, 0), 48) + 15]
                    clo, chi = cs[0], cs[-1]
                    assert cs == list(range(clo, chi + 1))
                    s0, s1 = clo - kc + 15, chi - kc + 16
                    p = a * 64 + kc
                    scatter_jobs.append(lambda e, p=p, a=a, clo=clo, chi=chi, s0=s0, s1=s1: e.dma_start(
                        out=cfv[p, :, :, clo:chi + 1], in_=trev_d[:, 1 - a:15 - a, s0:s1]))
                    scatter_jobs.append(lambda e, p=p, a=a, clo=clo, chi=chi, s0=s0, s1=s1: e.dma_start(
                        out=civ[p, :, 1 + a:9 + a, clo:chi + 1], in_=trev_d[:, 4:12, s0:s1]))
            sc.add("pool", lambda e: e.memset(negh[:, 31:32], -0.5), deps=pre)
            sc.flush(blk)

        with ExitStack() as l0:
            KT = sb(l0, "KT", [128, 2, S], BF16)
            VP = sb(l0, "VP", [128, NT, NKV, 65], BF16)
            b_KT = [Buf() for _ in range(NT)]
            b_VP = [Buf() for _ in range(NT)]
            b_VPones = Buf()
            sc.add("pool", lambda e: e.memset(VP[:, :, :, 64:65], 1.0), writes=[b_VPones])

            with ExitStack() as st, nc.Block("A0") as blk:
                wqkv = sb(st, "wqkv0", [128, 8, 1536], BF16)
                gcol = sb(st, "gcol0", [128, 8], F32)
                gq = sb(st, "gq", [128, 4, 64], F32)
                xs = [sb(st, "xs%d" % i, [128, D], F32) for i in range(3)]
                rp = [sb(st, "rp%d" % i, [128, 128], F32) for i in range(3)]
                junk = sb(st, "junk", [128, D], BF16)
                ssq = sb(st, "ssq", [128, NT], F32)
                tmp1 = sb(st, "tmp1", [128, NT], F32)
                rstd = sb(st, "rstd", [128, NT], F32)
                xn = [sb(st, "xn%d" % i, [128, D], BF16) for i in range(2)]
                hT = [sb(st, "hT%d" % i, [128, 8, 128], BF16) for i in range(2)]
                sqb = sb(st, "sqb", [128, 20, 64], F32)
                raw = sb(st, "raw", [128, 20, 64], F32)
                raw_b = sb(st, "raw_b", [128, 20, 64], F32)
                ssqh = sb(st, "ssqh", [128, 20], F32)
                tmph = sb(st, "tmph", [128, 20], F32)
                rstdh = sb(st, "rstdh", [128, 20], F32)
                tabs = [sb(st, "tabs%d" % i, [128, 4, 64], F32) for i in range(2)]
                ra = sb(st, "ra", [128, 20, 64], F32)
                rb = sb(st, "rb", [128, 20, 64], F32)
                rc = sb(st, "rc", [128, 20, 64], F32)
                qk = [sb(st, "qk%d" % i, [128, 20, 64], BF16) for i in range(2)]
                qTs = [sb(st, "qTs%d" % i, [128, 8, 128], BF16) for i in range(2)]
                tpx = ps(st, "tpx", [128, 8, 128], BF16)
                tpq = ps(st, "tpq", [128, 8, 128], BF16)
                tpk = ps(st, "tpk", [128, 2, 128], BF16)
                pqkv = ps(st, "pqkv", [128, 1536], F32)

                b_w = Buf(); b_gain = Buf(); b_gq = Buf()
                b_xs = [Buf() for _ in range(3)]; b_rp = [Buf() for _ in range(3)]
                ds_rp = [sc.dsem() for _ in range(3)]
                b_ssq = Buf(); b_tmp1 = Buf(); b_rstd = Buf()
                b_xn = [Buf() for _ in range(2)]; b_hT = [Buf() for _ in range(2)]
                b_sqb = Buf(); b_raw = Buf(); b_ssqh = Buf(); b_tmph = Buf(); b_rstdh = Buf()
                b_tabs = [Buf() for _ in range(2)]
                b_ra = Buf(); b_rb = Buf(); b_rc = Buf()
                b_qk = [Buf() for _ in range(2)]; b_qTs = [Buf() for _ in range(2)]
                b_tpx = Buf(); b_tpq = Buf(); b_tpk = Buf(); b_pqkv = Buf()
                ds_w = sc.dsem(); ds_g = sc.dsem(); ds_gq = sc.dsem()
                ds_x = [sc.dsem() for _ in range(3)]; ds_q = [sc.dsem() for _ in range(2)]

                wv = a_wqkv_d.rearrange("(c p) m -> p c m", p=128)
                for c in range(8):
                    for P_ in range(2):
                        for a in range(2):
                            sc.add("pool", lambda e, c=c, P_=P_, a=a: e.dma_start(
                                out=wqkv[:, c, 512 * P_:512 * P_ + 512].rearrange("p (j a d) -> p j a d", j=4, a=2)[:, :, a, :],
                                in_=wv[:, c, 512 * P_ + 256 * a:512 * P_ + 256 * a + 256].rearrange("p (j d) -> p j d", j=4)),
                                dma=ds_w)
                    sc.add("pool", lambda e, c=c: e.dma_start(out=wqkv[:, c, 1024:1536], in_=wv[:, c, 1024:1536]),
                           writes=[b_w] if c == 7 else [], dma=ds_w)
                b_w.w = {}; _merge(b_w.w, (ds_w.sem, ds_w.cnt))
                sc.add("sp", lambda e: e.dma_start(out=gcol[:], in_=norm_mix_d[0].rearrange("(c p) -> p c", p=128),
                                                   allow_slow_non_contiguous=True),
                       writes=[b_gain], dma=ds_g)
                for c in range(8):
                    sc.add("dve", lambda e, c=c: e.tensor_scalar(out=wqkv[:, c, :], in0=wqkv[:, c, :],
                                                                 scalar1=gcol[:, c:c + 1], scalar2=None, op0=ALU.mult),
                           reads=[b_w, b_gain], writes=[b_w] if c == 0 else [],
                           deps=list(b_w.w.values()) if c > 0 else [])
                b_w.w = {}; _merge(b_w.w, (sc.esem["dve"], sc.ecnt["dve"]))
                for gi, gd in ((0, a_qn_d), (2, a_kn_d)):
                    sc.add("sp", lambda e, gi=gi, gd=gd: e.dma_start(
                        out=gq[:, gi, :], in_=gd[None, :].broadcast_to([128, 64])), dma=ds_gq)
                    for a in range(2):
                        for hf in range(2):
                            o0 = a * 32 + hf * 16
                            s0 = a * 32 + (1 - hf) * 16
                            sc.add("sp", lambda e, gi=gi, gd=gd, o0=o0, s0=s0: e.dma_start(
                                out=gq[:, gi + 1, o0:o0 + 16],
                                in_=gd[None, s0:s0 + 16].broadcast_to([128, 16])), dma=ds_gq)
                b_gq.w = {}; _merge(b_gq.w, (ds_gq.sem, ds_gq.cnt))
                sc.add("pool", lambda e: e.tensor_scalar(out=gq[:, 0:2, :], in0=gq[:, 0:2, :], scalar1=0.125,
                                                         scalar2=None, op0=ALU.mult),
                       reads=[b_gq], writes=[b_gq])

                xv = x_d.rearrange("(t p) d -> t p d", p=128)
                rv = rope_d.rearrange("(t p) d -> t p d", p=128)
                qTv = qT_d
                raw2 = [raw, raw_b]
                b_raw2 = [b_raw, Buf()]

                def load_x(t):
                    if t >= NT:
                        return
                    x3 = t % 3
                    sc.add("sp", lambda e: e.dma_start(out=xs[x3][:], in_=xv[t]), writes=[b_xs[x3]], dma=ds_x[x3])
                    sc.add("sp", lambda e: e.dma_start(out=rp[x3][:], in_=rv[t]), writes=[b_rp[x3]], dma=ds_rp[x3])

                def stage1(t):
                    sl = t % 2
                    x3 = t % 3
                    sc.add("act", lambda e: e.activation(out=junk[:], in_=xs[x3][:], func=AF.Square,
                                                         accum_out=ssq[:, t:t + 1]),
                           reads=[b_xs[x3]], writes=[b_ssq])
                    yield
                    rstd_ops(ssq[:, t:t + 1], tmp1[:, t:t + 1], rstd[:, t:t + 1], None, 1.0 / D,
                             b_ssq, b_tmp1, b_rstd)
                    yield
                    sc.add("act", lambda e: e.activation(out=xn[sl][:], in_=xs[x3][:], func=AF.Copy,
                                                         scale=rstd[:, t:t + 1]),
                           reads=[b_xs[x3], b_rstd], writes=[b_xn[sl]])
                    yield
                    group0("pe", [lambda e, c=c: e.transpose(out=tpx[:, c, :], in_=xn[sl][:, c * 128:(c + 1) * 128],
                                                             identity=ident[:]) for c in range(8)],
                           reads=[b_xn[sl], b_ident], writes=[b_tpx])
                    yield
                    sc.add("act", lambda e: e.copy(out=hT[sl][:], in_=tpx[:]), reads=[b_tpx], writes=[b_hT[sl]])
                    yield
                    group0("pe", [lambda e, c=c, h3=h3: e.matmul(
                        pqkv[:, h3 * 512:(h3 + 1) * 512], lhsT=hT[sl][:, c, :],
                        rhs=wqkv[:, c, h3 * 512:(h3 + 1) * 512], start=(c == 0), stop=(c == 7))
                        for h3 in range(3) for c in range(8)],
                        reads=[b_hT[sl], b_w], writes=[b_pqkv])
                    yield
                    pq3 = pqkv[:, 0:1280].rearrange("p (h d) -> p h d", d=64)
                    sc.add("act", lambda e: e.copy(out=raw2[sl][:], in_=pq3), reads=[b_pqkv], writes=[b_raw2[sl]])
                    yield
                    sc.add("dve", lambda e: e.tensor_copy(out=VP[:, t, :, 0:64],
                                                          in_=pqkv[:, 1280:1536].rearrange("p (g d) -> p g d", d=64)),
                           reads=[b_pqkv], writes=[b_VP[t]])
                    yield

                def stage2(t):
                    sl = t % 2
                    rw = raw2[sl]
                    b_rw = b_raw2[sl]
                    sc.add("act", lambda e: e.activation(out=sqb[:], in_=rw[:], func=AF.Square),
                           reads=[b_rw], writes=[b_sqb])
                    yield
                    sc.add("dve", lambda e: e.tensor_reduce(out=ssqh[:], in_=sqb[:], axis=AX.X, op=ALU.add),
                           reads=[b_sqb], writes=[b_ssqh])
                    yield
                    rstd_ops(ssqh[:], tmph[:], rstdh[:], None, 1.0 / HD, b_ssqh, b_tmph, b_rstdh)
                    yield
                    tb = tabs[sl]
                    x3 = t % 3
                    sc.add("pool", lambda e: e.tensor_tensor(
                        out=tb[:, 0:4:2, :], in0=gq[:, 0:4:2, :],
                        in1=rp[x3][:, None, 0:64].broadcast_to([128, 2, 64]), op=ALU.mult),
                        reads=[b_rp[x3], b_gq], writes=[b_tabs[sl]])
                    yield
                    sc.add("pool", lambda e: e.tensor_tensor(
                        out=tb[:, 1:4:2, :], in0=gq[:, 1:4:2, :],
                        in1=rp[x3][:, None, 64:128].broadcast_to([128, 2, 64]), op=ALU.mult),
                        reads=[b_rp[x3], b_gq], writes=[b_tabs[sl]])
                    yield
                    sc.add("dve", lambda e: e.tensor_tensor(
                        out=ra[:, 0:16, :], in0=rw[:, 0:16, :],
                        in1=tb[:, 0:1, :].broadcast_to([128, 16, 64]), op=ALU.mult),
                        reads=[b_rw, b_tabs[sl]], writes=[b_ra])
                    yield
                    sc.add("dve", lambda e: e.tensor_tensor(
                        out=ra[:, 16:20, :], in0=rw[:, 16:20, :],
                        in1=tb[:, 2:3, :].broadcast_to([128, 4, 64]), op=ALU.mult),
                        reads=[b_rw, b_tabs[sl]], writes=[b_ra])
                    yield
                    raw5 = rw[:].rearrange("p h (a f i) -> p h a f i", a=2, f=2)
                    rb5 = rb[:].rearrange("p h (a f i) -> p h a f i", a=2, f=2)
                    for (h0, h1, ti) in ((0, 16, 1), (16, 20, 3)):
                        tb5 = tb[:, ti, :].rearrange("p (a f i) -> p a f i", a=2, f=2)
                        for hf in range(2):
                            sc.add("pool", lambda e, h0=h0, h1=h1, hf=hf, tb5=tb5: e.tensor_tensor(
                                out=rb5[:, h0:h1, :, hf, :], in0=raw5[:, h0:h1, :, 1 - hf, :],
                                in1=tb5[:, None, :, hf, :].broadcast_to([128, h1 - h0, 2, 16]), op=ALU.mult),
                                reads=[b_rw, b_tabs[sl]], writes=[b_rb])
                    sc.add("dve", lambda e: e.tensor_tensor(out=rc[:], in0=ra[:], in1=rb[:], op=ALU.add),
                           reads=[b_ra, b_rb], writes=[b_rc])
                    yield
                    sc.add("dve", lambda e: e.tensor_tensor(
                        out=qk[sl][:], in0=rc[:], in1=rstdh[:, :, None].broadcast_to([128, 20, 64]), op=ALU.mult),
                        reads=[b_rc, b_rstdh], writes=[b_qk[sl]])
                    yield
                    fns = [lambda e, cc=cc: e.transpose(out=tpq[:, cc, :], in_=qk[sl][:, 2 * cc:2 * cc + 2, :],
                                                        identity=ident[:]) for cc in range(8)]
                    fns += [lambda e, cc=cc: e.transpose(out=tpk[:, cc, :], in_=qk[sl][:, 16 + 2 * cc:18 + 2 * cc, :],
                                                         identity=ident[:]) for cc in range(2)]
                    group0("pe", fns, reads=[b_qk[sl]], writes=[b_tpq, b_tpk])
                    yield
                    sc.add("act", lambda e: e.copy(out=qTs[sl][:], in_=tpq[:]), reads=[b_tpq], writes=[b_qTs[sl]])
                    yield
                    sc.add("dve", lambda e: e.tensor_copy(out=KT[:, :, t * 128:(t + 1) * 128], in_=tpk[:]),
                           reads=[b_tpk], writes=[b_KT[t]])
                    yield
                    sc.add("sp", lambda e: e.dma_start(out=qTv[:, :, t * 128:(t + 1) * 128], in_=qTs[sl][:]),
                           reads=[b_qTs[sl]], dma=ds_q[sl])
                    yield

                def run_zip(*gens):
                    gens = [g for g in gens if g is not None]
                    while gens:
                        for g in list(gens):
                            try:
                                next(g)
                            except StopIteration:
                                gens.remove(g)

                load_x(0)
                load_x(1)
                load_x(2)
                run_zip(stage1(0))
                for t in range(NT):
                    run_zip(stage1(t + 1) if t + 1 < NT else None, stage2(t))
                    load_x(t + 3)
                fin = [(d.sem, d.cnt) for d in ds_q]
                sc.add("pool", lambda e: e.memset(negh[:, 31:32], -0.5), deps=fin)
                sc.flush(blk)

            with ExitStack() as st, nc.Block("B0") as blk:
                NQS = 3
                NPT = 3
                qsA = [sb(st, "qsA%d" % i, [128, 512], BF16) for i in range(NQS)]
                qsB = [sb(st, "qsB%d" % i, [128, 512], BF16) for i in range(NQS)]
                PT = [sb(st, "PT%d" % i, [128, 1024], BF16) for i in range(NPT)]
                Osb = sb(st, "Osb", [65, 2, 512], F32)
                rcp = sb(st, "rcp", [65, 2, 512], F32)
                rhi = sb(st, "rhi", [65, 2, 512], BF16)
                rlo = sb(st, "rlo", [65, 2, 512], BF16)
                ost = [sb(st, "ost%d" % i, [64, 2, 512], BF16) for i in range(2)]
                NSS = 3
                Sps = [ps(st, "Sps%d" % i, [128, 1024], F32) for i in range(NSS)]
                Ops = ps(st, "Ops", [128, 1024], F32)
                b_qs = [Buf() for _ in range(NQS)]; b_PT = [Buf() for _ in range(NPT)]
                b_S = [Buf() for _ in range(3)]; b_O = Buf(); b_Osb = Buf(); b_rcp = Buf()
                b_rhl = Buf(); b_rhl2 = Buf(); b_ost = [Buf() for _ in range(2)]
                ds_qs = [sc.dsem() for _ in range(NQS)]; ds_o = [sc.dsem() for _ in range(2)]

                units = [(s, cc, kt) for s in range(NST) for cc in range(8) for kt in range(NT)]
                NU = len(units)
                pairs = [(s, cc) for s in range(NST) for cc in range(8)]

                for i in range(NQS):
                    sc.add("pool", lambda e, i=i: e.memset(qsA[i][64:128, :], 0.0), writes=[b_qs[i]])
                    sc.add("pool", lambda e, i=i: e.memset(qsB[i][0:64, :], 0.0), writes=[b_qs[i]])

                def load_q(pi):
                    s, cc = pairs[pi]
                    sl = pi % NQS
                    sc.add("sp", lambda e, s=s, cc=cc, sl=sl: e.dma_start(
                        out=qsA[sl][0:64, :], in_=qT_d[0:64, cc, s * 512:(s + 1) * 512]),
                        writes=[b_qs[sl]], dma=ds_qs[sl])
                    sc.add("sp", lambda e, s=s, cc=cc, sl=sl: e.dma_start(
                        out=qsB[sl][64:128, :], in_=qT_d[64:128, cc, s * 512:(s + 1) * 512]),
                        dma=ds_qs[sl])
                    set_w0(b_qs[sl], ds_qs[sl])

                def qk_mm(u):
                    s, cc, kt = units[u]
                    pi = s * 8 + cc
                    sl = pi % NQS
                    ss = u % NSS
                    P_ = cc // 4
                    sc.add("pe", lambda e, ss=ss, sl=sl, P_=P_, kt=kt: e.matmul(
                        Sps[ss][:, 0:512], lhsT=KT[:, P_, kt * 128:(kt + 1) * 128], rhs=qsA[sl][:],
                        start=True, stop=True),
                        deps=list(b_qs[sl].w.values()) + list(b_S[ss].r.values()) + list(b_S[ss].w.values()),
                        sig=False)
                    sc.add("pe", lambda e, ss=ss, sl=sl, P_=P_, kt=kt: e.matmul(
                        Sps[ss][:, 512:1024], lhsT=KT[:, P_, kt * 128:(kt + 1) * 128], rhs=qsB[sl][:],
                        start=True, stop=True),
                        reads=[b_qs[sl]], writes=[b_S[ss]])

                def exp_op(u):
                    ss = u % NSS
                    pp = u % NPT
                    sc.add("act", lambda e, ss=ss, pp=pp: e.activation(out=PT[pp][:], in_=Sps[ss][:], func=AF.Exp),
                           reads=[b_S[ss]], writes=[b_PT[pp]])

                def pv_mm(u):
                    s, cc, kt = units[u]
                    pp = u % NPT
                    P_ = cc // 4
                    first = (kt == 0)
                    sc.add("pe", lambda e, pp=pp, P_=P_, kt=kt: e.matmul(
                        Ops[0:65, 0:512], lhsT=VP[:, kt, 2 * P_, :], rhs=PT[pp][:, 0:512],
                        start=(kt == 0), stop=(kt == NT - 1)),
                        deps=list(b_PT[pp].w.values()) + (list(b_O.r.values()) if first else []), sig=False)
                    sc.add("pe", lambda e, pp=pp, P_=P_, kt=kt: e.matmul(
                        Ops[0:65, 512:1024], lhsT=VP[:, kt, 2 * P_ + 1, :], rhs=PT[pp][:, 512:1024],
                        start=(kt == 0), stop=(kt == NT - 1)),
                        reads=[b_PT[pp]], writes=[b_O] if kt == NT - 1 else [])

                def pair_end_a(pi):
                    sc.add("dve", lambda e: e.tensor_copy(out=Osb[:].rearrange("p a q -> p (a q)"), in_=Ops[0:65, :]),
                           reads=[b_O], writes=[b_Osb])
                    sc.add("dve", lambda e: e.reciprocal(out=rcp[64:65, :, :], in_=Osb[64:65, :, :]),
                           reads=[b_Osb], writes=[b_rcp])
                    sc.add("dve", lambda e: e.tensor_copy(out=rhi[64:65, :, :], in_=rcp[64:65, :, :]),
                           reads=[b_rcp], writes=[b_rhl])
                    sc.add("dve", lambda e: e.tensor_tensor(out=rlo[64:65, :, :], in0=rcp[64:65, :, :],
                                                            in1=rhi[64:65, :, :], op=ALU.subtract),
                           reads=[b_rcp, b_rhl], writes=[b_rhl2])

                def pair_end_b(pi, x):
                    Rps = Sps[x][0:64, :]
                    b_R = b_S[x]
                    s, cc = pairs[pi]
                    P_, j_ = cc // 4, cc % 4
                    heads = (8 * P_ + j_, 8 * P_ + 4 + j_)
                    os_ = pi % 2
                    fns = []
                    for a in range(2):
                        fns.append(lambda e, a=a: e.matmul(
                            Rps[:, a * 512:(a + 1) * 512], lhsT=ones_b[64:65, 0:64], rhs=rhi[64:65, a, :],
                            start=True, stop=False))
                        fns.append(lambda e, a=a: e.matmul(
                            Rps[:, a * 512:(a + 1) * 512], lhsT=ones_b[64:65, 0:64], rhs=rlo[64:65, a, :],
                            start=False, stop=True))
                    group0("pe", fns, reads=[b_rhl, b_rhl2, b_const], writes=[b_R])
                    sc.add("dve", lambda e, os_=os_: e.tensor_tensor(
                        out=ost[os_][:].rearrange("p a q -> p (a q)"),
                        in0=Osb[0:64, :, :].rearrange("p a q -> p (a q)"), in1=Rps, op=ALU.mult),
                        reads=[b_Osb, b_R], writes=[b_ost[os_]])
                    for a in range(2):
                        h = heads[a]
                        p0 = (h % 2) * 64
                        sc.add("sp", lambda e, a=a, h=h, p0=p0, s=s, os_=os_: e.dma_start(
                            out=oT_d[p0:p0 + 64, h // 2, s * 512:(s + 1) * 512], in_=ost[os_][:, a, :]),
                            reads=[b_ost[os_]], dma=ds_o[os_])

                LAG = min(14, NT - 2)
                per_pair = -(-len(scatter_jobs) // len(pairs))
                pend = {}
                load_q(0)
                if len(pairs) > 1:
                    load_q(1)
                qk_mm(0)
                exp_op(0)
                for u0 in (1, 2):
                    if u0 < NU:
                        qk_mm(u0)
                for u in range(NU):
                    s, cc, kt = units[u]
                    pi = s * 8 + cc
                    if u + 1 < NU:
                        exp_op(u + 1)
                    if kt == 1:
                        for _ in range(per_pair):
                            if scatter_jobs:
                                sc.add("sp", scatter_jobs.pop(0), deps=pre, dma=ds_sc)
                    if u in pend:
                        pair_end_b(pend.pop(u), u % NSS)
                    if u + 3 < NU:
                        s2, cc2, kt2 = units[u + 3]
                        if kt2 == 0 and (s2 * 8 + cc2) + 1 < len(pairs):
                            load_q(s2 * 8 + cc2 + 1)
                        qk_mm(u + 3)
                    pv_mm(u)
                    if kt == NT - 1:
                        pair_end_a(pi)
                        pend[min(u + LAG, NU - 1)] = pi
                for u_left in sorted(pend):
                    pair_end_b(pend[u_left], 0)
                while scatter_jobs:
                    sc.add("sp", scatter_jobs.pop(0), deps=pre, dma=ds_sc)
                etab_done = [(ds_sc.sem, ds_sc.cnt)]
                fin = [(d.sem, d.cnt) for d in ds_o]
                sc.add("pool", lambda e: e.memset(negh[:, 31:32], -0.5), deps=fin)
                sc.flush(blk)

        def group(eng, fns, reads=(), writes=(), per_deps=None):
            n = len(fns)
            for i, fn in enumerate(fns):
                pd = list(per_deps[i]) if per_deps is not None else []
                if i == n - 1:
                    sc.add(eng, fn, reads=reads, writes=writes, deps=pd)
                elif i == 0:
                    deps = [t for b in reads for t in b.w.values()]
                    deps += [t for b in writes for t in list(b.w.values()) + list(b.r.values())]
                    sc.add(eng, fn, deps=deps + pd, sig=False)
                else:
                    sc.add(eng, fn, deps=pd, sig=False)

        def set_w(buf, dsem):
            buf.w = {}
            buf.r = {}
            _merge(buf.w, (dsem.sem, dsem.cnt))

        def norm_front(xt_ap, b_x, gain_t, b_g, ssq_c, tmp_c, rstd_c, bs, xn_t, b_xn_, tp_t, b_tp_, dst_ap, b_dst,
                       junk_t):
            b_ssq_, b_tmp_, b_rstd_ = bs
            sc.add("act", lambda e: e.activation(out=junk_t[:], in_=xt_ap, func=AF.Square, accum_out=ssq_c),
                   reads=[b_x], writes=[b_ssq_])
            rstd_ops(ssq_c, tmp_c, rstd_c, negh[:, 0:1], 1.0 / D, b_ssq_, b_tmp_, b_rstd_)
            sc.add("dve", lambda e: e.scalar_tensor_tensor(out=xn_t[:], in0=xt_ap, scalar=rstd_c, in1=gain_t[:],
                                                           op0=ALU.mult, op1=ALU.mult),
                   reads=[b_x, b_rstd_, b_g], writes=[b_xn_])
            group("pe", [lambda e, c=c: e.transpose(out=tp_t[:, c, :], in_=xn_t[:, c * 128:(c + 1) * 128],
                                                    identity=ident[:]) for c in range(8)],
                  reads=[b_xn_, b_ident], writes=[b_tp_])
            sc.add("act", lambda e: e.copy(out=dst_ap, in_=tp_t[:]), reads=[b_tp_], writes=[b_dst])

        def run_lag(g0, g1, lag):
            alive0 = alive1 = True
            for _ in range(lag):
                try:
                    next(g0)
                except StopIteration:
                    alive0 = False
            while alive0 or alive1:
                if alive0:
                    try:
                        next(g0)
                    except StopIteration:
                        alive0 = False
                if alive1:
                    try:
                        next(g1)
                    except StopIteration:
                        alive1 = False

        def norm_front_gen(xt_ap, b_x, gain_t, b_g, ssq_c, tmp_c, rstd_c, bs, xn_t, b_xn_, tp_t, b_tp_, dst_ap, b_dst,
                           junk_t, b_junk):
            b_ssq_, b_tmp_, b_rstd_ = bs
            sc.add("act", lambda e: e.activation(out=junk_t[:], in_=xt_ap, func=AF.Square, accum_out=ssq_c),
                   reads=[b_x], writes=[b_ssq_, b_junk])
            yield
            rstd_ops(ssq_c, tmp_c, rstd_c, None, 1.0 / D, b_ssq_, b_tmp_, b_rstd_)
            yield
            sc.add("dve", lambda e: e.scalar_tensor_tensor(out=xn_t[:], in0=xt_ap, scalar=rstd_c, in1=gain_t[:],
                                                           op0=ALU.mult, op1=ALU.mult),
                   reads=[b_x, b_rstd_, b_g], writes=[b_xn_])
            yield
            group("pe", [lambda e, c=c: e.transpose(out=tp_t[:, c, :], in_=xn_t[:, c * 128:(c + 1) * 128],
                                                    identity=ident[:]) for c in range(8)],
                  reads=[b_xn_, b_ident], writes=[b_tp_])
            yield
            sc.add("act", lambda e: e.copy(out=dst_ap, in_=tp_t[:]), reads=[b_tp_], writes=[b_dst])
            yield

        def post_phase(L, src_d, wo_d, dst_d, final):
            NSP = S // 256
            with ExitStack() as st, nc.Block("P%d" % L) as blk:
                Wo = sb(st, "Wo%d" % L, [128, 8, D], BF16)
                Win = sb(st, "Win%d" % L, [128, 8, DFF], BF16)
                Wout = sb(st, "Wout%d" % L, [128, 32, D], BF16)
                gm = sb(st, "gm%d" % L, [128, D], F32)
                gf = sb(st, "gf%d" % L, [128, D], F32) if final else None
                xm = [sb(st, "xm%d_%d" % (L, i), [128, 2, D], F32) for i in range(2)]
                oTs = [sb(st, "oTs%d_%d" % (L, i), [128, 8, 256], BF16) for i in range(2)]
                xn_ = [sb(st, "pxn%d_%d" % (L, i), [128, D], BF16) for i in range(2)]
                h2T = sb(st, "h2T%d" % L, [128, 8, 256], BF16)
                uT = sb(st, "uT%d" % L, [128, 32, 256], BF16)
                rsb = [sb(st, "rsb%d_%d" % (L, i), [128, 256], F32) for i in range(2)]
                junk_ = sb(st, "pjunk%d" % L, [128, D], BF16)
                stt = sb(st, "pstat%d" % L, [128, 6, NT], F32)
                yps = [ps(st, "yps%d_%d" % (L, i), [128, D], F32) for i in range(2)]
                tp_ = ps(st, "ptp%d" % L, [128, 8, 128], BF16)
                ups = [ps(st, "ups%d_%d" % (L, i), [128, 512], F32) for i in range(3)]
                b_Wo = Buf(); b_Win = Buf(); b_Wout = Buf(); b_gm = Buf(); b_gf = Buf()
                b_xm = [[Buf() for _ in range(2)] for _ in range(2)]
                b_oTs = [Buf() for _ in range(2)]; b_pxn = [Buf() for _ in range(2)]
                b_h2T = [Buf() for _ in range(2)]; b_uT = [Buf() for _ in range(32)]
                b_rsb = [Buf() for _ in range(2)]; b_st = [Buf() for _ in range(6)]
                b_yps = [Buf() for _ in range(2)]; b_tp = Buf(); b_ups = [Buf() for _ in range(3)]
                ds_wo = sc.dsem(); ds_win = sc.dsem(); ds_wout = sc.dsem(); ds_g = sc.dsem()
                ds_xl = [sc.dsem() for _ in range(2)]; ds_ol = [sc.dsem() for _ in range(2)]
                ds_st = [[sc.dsem() for _ in range(2)] for _ in range(2)]

                wov = wo_d.rearrange("(c p) m -> p c m", p=128)
                for c in range(8):
                    sc.add("pool", lambda e, c=c: e.dma_start(out=Wo[:, c, :], in_=wov[:, c, :]), dma=ds_wo)
                set_w(b_Wo, ds_wo)
                sc.add("sp", lambda e: e.dma_start(out=gm[:], in_=norm_mlp_d[L:L + 1, :].broadcast_to([128, D])), dma=ds_g)
                set_w(b_gm, ds_g)
                if final:
                    sc.add("sp", lambda e: e.dma_start(out=gf[:], in_=norm_final_d[None, :].broadcast_to([128, D])),
                           dma=ds_g)
                    set_w(b_gf, ds_g); set_w(b_gm, ds_g)
                winv = w_in_d[L].rearrange("(c p) m -> p c m", p=128)
                for c in range(8):
                    for hh in range(2):
                        sc.add("pool", lambda e, c=c, hh=hh: e.dma_start(
                            out=Win[:, c, hh * 2048:(hh + 1) * 2048], in_=winv[:, c, hh * 2048:(hh + 1) * 2048]),
                            dma=ds_win)
                set_w(b_Win, ds_win)
                woutv = w_out_d[L].rearrange("(c p) m -> p c m", p=128)
                for c in range(32):
                    sc.add("pool", lambda e, c=c: e.dma_start(out=Wout[:, c, :], in_=woutv[:, c, :]), dma=ds_wout)
                set_w(b_Wout, ds_wout)

                srcv = src_d.rearrange("(n i p) d -> n p i d", p=128, i=2)
                dstv = dst_d.rearrange("(n i p) d -> n p i d", p=128, i=2)

                def loads(n):
                    sl = n % 2
                    sc.add("sp", lambda e, n=n, sl=sl: e.dma_start(out=xm[sl][:], in_=srcv[n]),
                           writes=[b_xm[sl][0], b_xm[sl][1]], dma=ds_xl[sl])
                    sc.add("sp", lambda e, n=n, sl=sl: e.dma_start(out=oTs[sl][:], in_=oT_d[:, :, n * 256:(n + 1) * 256]),
                           writes=[b_oTs[sl]], dma=ds_ol[sl])

                b_stt = [[Buf() for _ in range(3)] for _ in range(2)]
                b_junkP = Buf()

                def tile_front(n, i):
                    sl = n % 2
                    t = 2 * n + i
                    fns = [lambda e, c=c, hf=hf: e.matmul(
                        yps[i][:, hf * 512:(hf + 1) * 512], lhsT=oTs[sl][:, c, i * 128:(i + 1) * 128],
                        rhs=Wo[:, c, hf * 512:(hf + 1) * 512], start=(c == 0), stop=(c == 7))
                        for hf in range(2) for c in range(8)]
                    group("pe", fns, reads=[b_oTs[sl], b_Wo], writes=[b_yps[i]])
                    yield
                    sc.add("dve", lambda e: e.tensor_tensor(out=xm[sl][:, i, :], in0=yps[i][:],
                                                            in1=xm[sl][:, i, :], op=ALU.add),
                           reads=[b_yps[i]], writes=[b_xm[sl][i]])
                    yield
                    yield from norm_front_gen(
                        xm[sl][:, i, :], b_xm[sl][i], gm, b_gm, stt[:, 0, t:t + 1], stt[:, 1, t:t + 1],
                        stt[:, 2, t:t + 1], (b_stt[i][0], b_stt[i][1], b_stt[i][2]), xn_[i], b_pxn[i], tp_, b_tp,
                        h2T[:, :, i * 128:(i + 1) * 128], b_h2T[i], junk_, b_junkP)

                loads(0)
                for n in range(NSP):
                    sl = n % 2
                    if n + 1 < NSP:
                        loads(n + 1)
                    run_lag(tile_front(n, 0), tile_front(n, 1), 2)
                    for m in range(32):
                        us = m % 3
                        fns = [lambda e, c=c, m=m, us=us: e.matmul(
                            ups[us][:, 0:256], lhsT=Win[:, c, m * 128:(m + 1) * 128], rhs=h2T[:, c, :],
                            start=(c == 0), stop=(c == 7)) for c in range(8)]
                        group("pe", fns, reads=[b_h2T[0], b_h2T[1], b_Win], writes=[b_ups[us]])
                        rs_ = m % 2
                        sc.add("act", lambda e, us=us, rs_=rs_: e.activation(out=rsb[rs_][:], in_=ups[us][:, 0:256],
                                                                           func=AF.Relu),
                               reads=[b_ups[us]], writes=[b_rsb[rs_]])
                        sc.add("dve", lambda e, us=us, rs_=rs_, m=m: e.tensor_tensor(
                            out=uT[:, m, :], in0=ups[us][:, 0:256], in1=rsb[rs_][:], op=ALU.mult),
                            reads=[b_ups[us], b_rsb[rs_]], writes=[b_uT[m]])
                    for i in range(2):
                        t = 2 * n + i
                        fns = [lambda e, k=k, hf=hf, i=i: e.matmul(
                            yps[i][:, hf * 512:(hf + 1) * 512], lhsT=uT[:, k, i * 128:(i + 1) * 128],
                            rhs=Wout[:, k, hf * 512:(hf + 1) * 512], start=(k == 0), stop=(k == 31))
                            for hf in range(2) for k in range(32)]
                        pdeps = [list(b_uT[k].w.values()) for hf in range(2) for k in range(32)]
                        group("pe", fns, reads=[b_Wout] + (b_uT if i == 1 else []), writes=[b_yps[i]], per_deps=pdeps)
                        sc.add("dve", lambda e, i=i, sl=sl: e.tensor_tensor(out=xm[sl][:, i, :], in0=yps[i][:],
                                                                          in1=xm[sl][:, i, :], op=ALU.add),
                               reads=[b_yps[i]], writes=[b_xm[sl][i]])
                        if final:
                            sc.add("act", lambda e, i=i, sl=sl, t=t: e.activation(
                                out=junk_[:], in_=xm[sl][:, i, :], func=AF.Square, accum_out=stt[:, 3, t:t + 1]),
                                reads=[b_xm[sl][i]], writes=[b_st[3]])
                            rstd_ops(stt[:, 3, t:t + 1], stt[:, 4, t:t + 1], stt[:, 5, t:t + 1], negh[:, 0:1], 1.0 / D,
                                     b_st[3], b_st[4], b_st[5])
                            sc.add("dve", lambda e, i=i, sl=sl, t=t: e.scalar_tensor_tensor(
                                out=xm[sl][:, i, :], in0=xm[sl][:, i, :], scalar=stt[:, 5, t:t + 1], in1=gf[:],
                                op0=ALU.mult, op1=ALU.mult),
                                reads=[b_st[5], b_gf], writes=[b_xm[sl][i]])
                        sc.add("sp", lambda e, i=i, sl=sl, n=n: e.dma_start(out=dstv[n][:, i, :], in_=xm[sl][:, i, :]),
                               reads=[b_xm[sl][i]], dma=ds_st[sl][i])
                fin = [(d.sem, d.cnt) for dd in ds_st for d in dd]
                sc.add("pool", lambda e: e.memset(negh[:, 31:32], -0.5), deps=fin)
                sc.flush(blk)

        post_phase(0, x_d, a_wo_d, x1_d, False)

        with ExitStack() as st, nc.Block("A1") as blk:
            W1 = sb(st, "W1", [128, 8, 3072], BF16)
            g1 = sb(st, "g1", [128, D], F32)
            xs1 = [sb(st, "xs1_%d" % i, [128, D], F32) for i in range(2)]
            xn1 = [sb(st, "xn1_%d" % i, [128, D], BF16) for i in range(2)]
            h1T = sb(st, "h1T", [128, 8, 512], BF16)
            qkst = [sb(st, "qkst%d" % i, [128, 16, 512], BF16) for i in range(2)]
            vst = [sb(st, "vst%d" % i, [128, 4, NH, 65], BF16) for i in range(2)]
            junk1 = sb(st, "junk1", [128, D], BF16)
            st1 = sb(st, "st1", [128, 3, NT], F32)
            tp1 = ps(st, "tp1", [128, 8, 128], BF16)
            qkps = [ps(st, "qkps%d" % i, [128, 512], F32) for i in range(2)]
            vps = [ps(st, "vps%d" % i, [128, D], F32) for i in range(2)]
            b_W1 = Buf(); b_g1 = Buf(); b_xs1 = [Buf() for _ in range(2)]; b_xn1 = [Buf() for _ in range(2)]
            b_h1T = [Buf() for _ in range(4)]; b_qkst = [Buf() for _ in range(2)]; b_vst = [Buf() for _ in range(2)]
            b_s1 = [Buf() for _ in range(3)]; b_tp1 = Buf(); b_qkps = [Buf() for _ in range(2)]
            b_vps = [Buf() for _ in range(2)]
            ds_w1 = sc.dsem(); ds_g1 = sc.dsem(); ds_x1 = [sc.dsem() for _ in range(2)]
            ds_s1 = [sc.dsem() for _ in range(2)]
            ds_s1v = [sc.dsem() for _ in range(2)]
            w1v = b_wqkv_d.rearrange("(c p) m -> p c m", p=128)
            for c in range(8):
                for hh in range(2):
                    sc.add("pool", lambda e, c=c, hh=hh: e.dma_start(
                        out=W1[:, c, hh * 1536:(hh + 1) * 1536], in_=w1v[:, c, hh * 1536:(hh + 1) * 1536]), dma=ds_w1)
            set_w(b_W1, ds_w1)
            sc.add("sp", lambda e: e.dma_start(out=g1[:], in_=norm_mix_d[1:2, :].broadcast_to([128, D])), dma=ds_g1)
            set_w(b_g1, ds_g1)
            for i in range(2):
                sc.add("pool", lambda e, i=i: e.memset(vst[i][:, :, :, 64:65], 1.0), writes=[b_vst[i]])
            x1v = x1_d.rearrange("(t p) d -> t p d", p=128)
            v1v = v1_d.rearrange("(t p) f -> p t f", p=128)
            b_s1t = [[Buf() for _ in range(3)] for _ in range(2)]
            b_junk1 = Buf()

            def a1_front(s, i):
                t = 4 * s + i
                sl = t % 2
                sc.add("sp", lambda e: e.dma_start(out=xs1[sl][:], in_=x1v[t]), writes=[b_xs1[sl]], dma=ds_x1[sl])
                yield
                yield from norm_front_gen(
                    xs1[sl][:], b_xs1[sl], g1, b_g1, st1[:, 0, t:t + 1], st1[:, 1, t:t + 1],
                    st1[:, 2, t:t + 1], (b_s1t[sl][0], b_s1t[sl][1], b_s1t[sl][2]), xn1[sl], b_xn1[sl], tp1, b_tp1,
                    h1T[:, :, i * 128:(i + 1) * 128], b_h1T[i], junk1, b_junk1)

            for s in range(NST):
                ss = s % 2
                for i0 in (0, 2):
                    run_lag(a1_front(s, i0), a1_front(s, i0 + 1), 2)
                for mc in range(16):
                    qs_ = mc % 2
                    fns = [lambda e, c=c, mc=mc, qs_=qs_: e.matmul(
                        qkps[qs_][:], lhsT=W1[:, c, mc * 128:(mc + 1) * 128], rhs=h1T[:, c, :],
                        start=(c == 0), stop=(c == 7)) for c in range(8)]
                    group("pe", fns, reads=b_h1T + [b_W1], writes=[b_qkps[qs_]])
                    scale = 0.125 if mc < 8 else 1.0
                    sc.add("act", lambda e, mc=mc, qs_=qs_, ss=ss, scale=scale: e.mul(
                        out=qkst[ss][:, mc, :], in_=qkps[qs_][:], mul=scale),
                        reads=[b_qkps[qs_]], writes=[b_qkst[ss]] if mc == 0 else [],
                        deps=list(b_qkst[ss].w.values()) if mc > 0 else [])
                b_qkst[ss].w = {}; _merge(b_qkst[ss].w, (sc.esem["act"], sc.ecnt["act"]))
                for i in range(4):
                    vs_ = i % 2
                    fns = [lambda e, c=c, hf=hf, i=i, vs_=vs_: e.matmul(
                        vps[vs_][:, hf * 512:(hf + 1) * 512], lhsT=h1T[:, c, i * 128:(i + 1) * 128],
                        rhs=W1[:, c, 2048 + hf * 512:2048 + (hf + 1) * 512], start=(c == 0), stop=(c == 7))
                        for hf in range(2) for c in range(8)]
                    group("pe", fns, reads=[b_h1T[i], b_W1], writes=[b_vps[vs_]])
                    sc.add("dve", lambda e, i=i, vs_=vs_, ss=ss: e.tensor_copy(
                        out=vst[ss][:, i, :, 0:64], in_=vps[vs_][:].rearrange("p (h d) -> p h d", d=64)),
                        reads=[b_vps[vs_]], writes=[b_vst[ss]] if i == 0 else [],
                        deps=list(b_vst[ss].w.values()) if i > 0 else [])
                b_vst[ss].w = {}; _merge(b_vst[ss].w, (sc.esem["dve"], sc.ecnt["dve"]))
                sc.add("sp", lambda e, s=s, ss=ss: e.dma_start(out=qT_d[:, :, s * 512:(s + 1) * 512], in_=qkst[ss][:, 0:8, :]),
                       reads=[b_qkst[ss]], dma=ds_s1[ss])
                sc.add("sp", lambda e, s=s, ss=ss: e.dma_start(out=kT1_d[:, :, s * 512:(s + 1) * 512], in_=qkst[ss][:, 8:16, :]),
                       reads=[b_qkst[ss]], dma=ds_s1[ss])
                sc.add("sp", lambda e, s=s, ss=ss: e.dma_start(
                    out=v1v[:, 4 * s:4 * s + 4, :], in_=vst[ss][:].rearrange("p i h e -> p i (h e)")),
                    reads=[b_vst[ss]], dma=ds_s1v[ss])
            fin = [(d.sem, d.cnt) for d in ds_s1 + ds_s1v]
            sc.add("pool", lambda e: e.memset(negh[:, 31:32], -0.5), deps=fin)
            sc.flush(blk)

        with ExitStack() as st, nc.Block("B1") as blk:
            NK = 7
            Eful = sb(st, "Eful", [128, NH, 14, 64], BF16)
            Eint = sb(st, "Eint", [128, NH, 10, 64], BF16)
            stg = sb(st, "Estg", [128, NH * 14 * 64], F32)
            q1s = [sb(st, "q1s%d" % i, [128, 8, 128], BF16) for i in range(2)]
            k1r = [sb(st, "k1r%d" % i, [128, 8, 128], BF16) for i in range(NK)]
            v1r = [sb(st, "v1r%d" % i, [128, NH, 65], BF16) for i in range(NK)]
            P0 = [sb(st, "P0_%d" % i, [128, 1024], BF16) for i in range(2)]
            PTn = [sb(st, "PTn%d" % i, [128, 1024], BF16) for i in range(3)]
            rc8 = sb(st, "rc8", [128, 2, 4], F32)
            o1tm = [sb(st, "o1tm%d" % i, [128, NH, 64], BF16) for i in range(2)]
            o1Ts = [sb(st, "o1Ts%d" % i, [128, 8, 128], BF16) for i in range(2)]
            S1 = [ps(st, "S1_%d" % i, [128, 1024], F32) for i in range(2)]
            O1 = ps(st, "O1", [128, 2, 512], F32)
            tpo = ps(st, "tpo", [128, 8, 128], BF16)
            b_Ef = Buf(); b_Ei = Buf(); b_stg = Buf()
            b_q1s = [Buf() for _ in range(2)]; b_k1r = [Buf() for _ in range(NK)]; b_v1r = [Buf() for _ in range(NK)]
            b_P0 = [Buf() for _ in range(2)]; b_PTn = [Buf() for _ in range(3)]; b_rc8 = Buf()
            b_o1tm = [Buf() for _ in range(2)]; b_o1Ts = [Buf() for _ in range(2)]
            b_S1 = [Buf() for _ in range(2)]; b_O1 = Buf(); b_tpo = Buf()
            ds_e = sc.dsem(); ds_q1 = [sc.dsem() for _ in range(2)]; ds_k1 = [sc.dsem() for _ in range(NK)]
            ds_o1 = [sc.dsem() for _ in range(2)]
            sc.add("sp", lambda e: e.dma_start(out=stg[:], in_=cful_d), writes=[b_stg], deps=etab_done, dma=ds_e)
            sc.add("act", lambda e: e.activation(out=Eful[:].rearrange("p h r c -> p (h r c)"), in_=stg[:], func=AF.Exp),
                   reads=[b_stg], writes=[b_Ef])
            sc.add("sp", lambda e: e.dma_start(out=stg[:, 0:NH * 10 * 64], in_=cint_d), writes=[b_stg], dma=ds_e)
            sc.add("act", lambda e: e.activation(out=Eint[:].rearrange("p h r c -> p (h r c)"),
                                                 in_=stg[:, 0:NH * 10 * 64], func=AF.Exp),
                   reads=[b_stg], writes=[b_Ei])

            def pair_cfg(i):
                if NT >= 6 and 2 <= i <= NT - 3:
                    return [i - 2 + j for j in range(5)], Eint, b_Ei, [8 - 2 * j for j in range(5)]
                if i == 0:
                    K_, t0 = 7, 0
                elif i == 1:
                    K_, t0 = 5, 0
                elif i == NT - 2:
                    K_, t0 = 3, NT - 4
                else:
                    K_, t0 = 1, NT - 4
                return [t0 + j for j in range(4)], Eful, b_Ef, [13 - K_ - 2 * j for j in range(4)]

            loaded = set()

            def load_keys(jt):
                if jt in loaded or jt < 0 or jt >= NT:
                    return
                loaded.add(jt)
                sl = jt % NK
                sc.add("sp", lambda e, jt=jt, sl=sl: e.dma_start(out=k1r[sl][:], in_=kT1_d[:, :, jt * 128:(jt + 1) * 128]),
                       writes=[b_k1r[sl]], dma=ds_k1[sl])
                sc.add("sp", lambda e, jt=jt, sl=sl: e.dma_start(
                    out=v1r[sl][:].rearrange("p h e -> p (h e)"), in_=v1_d[jt * 128:(jt + 1) * 128, :]),
                    writes=[b_v1r[sl]], dma=ds_k1[sl])
                set_w(b_k1r[sl], ds_k1[sl]); set_w(b_v1r[sl], ds_k1[sl])

            def load_q1(i):
                sl = i % 2
                sc.add("sp", lambda e, i=i, sl=sl: e.dma_start(out=q1s[sl][:], in_=qT_d[:, :, i * 128:(i + 1) * 128]),
                       writes=[b_q1s[sl]], dma=ds_q1[sl])

            units1 = []
            for i in range(NT):
                tiles, Et, b_Et, planes = pair_cfg(i)
                for hf in range(2):
                    for j, jt in enumerate(tiles):
                        units1.append((i, hf, j, jt, len(tiles), planes[j], Et, b_Et))
            NU1 = len(units1)

            def prefetch(i):
                if i >= NT:
                    return
                load_q1(i)
                for jt in pair_cfg(i)[0]:
                    load_keys(jt)

            def qk1(u):
                i, hf, j, jt, nj, p0_, Et, b_Et = units1[u]
                ksl, qsl, ss = jt % NK, i % 2, u % 2
                fns = []
                for cl in range(4):
                    c = 4 * hf + cl
                    fns.append(lambda e, c=c, cl=cl: e.matmul(
                        S1[ss][:, cl * 128:(cl + 1) * 128], lhsT=k1r[ksl][0:64, c, :], rhs=q1s[qsl][0:64, c, :],
                        start=True, stop=True))
                    fns.append(lambda e, c=c, cl=cl: e.matmul(
                        S1[ss][:, 512 + cl * 128:512 + (cl + 1) * 128], lhsT=k1r[ksl][64:128, c, :],
                        rhs=q1s[qsl][64:128, c, :], start=True, stop=True))
                group("pe", fns, reads=[b_k1r[ksl], b_q1s[qsl]], writes=[b_S1[ss]])

            def exp1(u):
                ss = u % 2
                sc.add("act", lambda e: e.activation(out=P0[ss][:], in_=S1[ss][:], func=AF.Exp),
                       reads=[b_S1[ss]], writes=[b_P0[ss]])

            def mul1(u):
                i, hf, j, jt, nj, p0_, Et, b_Et = units1[u]
                ss, pp = u % 2, u % 3
                e_ap = Et[:, 8 * hf:8 * hf + 8, p0_:p0_ + 2, :].rearrange("p (cl par) b c -> p par cl (b c)", par=2)
                meng = "dve"
                sc.add(meng, lambda e: e.tensor_tensor(
                    out=PTn[pp][:].rearrange("p (par cl q) -> p par cl q", par=2, cl=4),
                    in0=P0[ss][:].rearrange("p (par cl q) -> p par cl q", par=2, cl=4), in1=e_ap, op=ALU.mult),
                    reads=[b_P0[ss], b_Et], writes=[b_PTn[pp]])

            def pv1(u):
                i, hf, j, jt, nj, p0_, Et, b_Et = units1[u]
                ksl, pp = jt % NK, u % 3
                fns = []
                for hh in range(8):
                    cl, par = hh // 2, hh % 2
                    h = 8 * hf + hh
                    fns.append(lambda e, hh=hh, cl=cl, par=par, h=h: e.matmul(
                        O1[:, hh // 4, (hh % 4) * 65:(hh % 4) * 65 + 65],
                        lhsT=PTn[pp][:, par * 512 + cl * 128:par * 512 + (cl + 1) * 128],
                        rhs=v1r[ksl][:, h, :], start=(j == 0 and hh % 4 == 0), stop=(j == nj - 1),
                        skip_group_check=True))
                if j == 0 or j == nj - 1:
                    group("pe", fns, reads=[b_PTn[pp], b_v1r[ksl]], writes=[b_O1])
                else:
                    group("pe", fns, reads=[b_PTn[pp], b_v1r[ksl], b_O1])

            O4 = O1[:, :, 0:260].rearrange("p b (h e) -> p b h e", e=65)

            def half_end(i, hf):
                osl = i % 2
                sc.add("dve", lambda e: e.reciprocal(out=rc8[:], in_=O4[:, :, :, 64]), reads=[b_O1], writes=[b_rc8])
                sc.add("dve", lambda e: e.tensor_tensor(
                    out=o1tm[osl][:, 8 * hf:8 * hf + 8, :].rearrange("p (b h) d -> p b h d", b=2),
                    in0=O4[:, :, :, 0:64], in1=rc8[:, :, :, None].broadcast_to([128, 2, 4, 64]), op=ALU.mult),
                    reads=[b_O1, b_rc8], writes=[b_o1tm[osl]] if hf == 0 else [],
                    deps=list(b_o1tm[osl].w.values()) if hf == 1 else [])
                if hf == 1:
                    b_o1tm[osl].w = {}; _merge(b_o1tm[osl].w, (sc.esem["dve"], sc.ecnt["dve"]))
                b_O1.r = {}; _merge(b_O1.r, (sc.esem["dve"], sc.ecnt["dve"]))

            def pair_end(i):
                osl = i % 2
                group("pe", [lambda e, c=c: e.transpose(out=tpo[:, c, :], in_=o1tm[osl][:, 2 * c:2 * c + 2, :],
                                                        identity=ident[:]) for c in range(8)],
                      reads=[b_o1tm[osl]], writes=[b_tpo])
                sc.add("act", lambda e: e.copy(out=o1Ts[osl][:], in_=tpo[:]), reads=[b_tpo], writes=[b_o1Ts[osl]])
                sc.add("sp", lambda e: e.dma_start(out=oT_d[:, :, i * 128:(i + 1) * 128], in_=o1Ts[osl][:]),
                       reads=[b_o1Ts[osl]], dma=ds_o1[osl])

            prefetch(0)
            prefetch(1)
            pend1 = {}
            qk1(0)
            exp1(0)
            mul1(0)
            if NU1 > 1:
                qk1(1)
            for u in range(NU1):
                i, hf, j, jt, nj, p0_, Et, b_Et = units1[u]
                if u + 1 < NU1:
                    exp1(u + 1)
                    mul1(u + 1)
                if u + 2 < NU1:
                    i2, hf2, j2 = units1[u + 2][0:3]
                    if hf2 == 0 and j2 == 0:
                        prefetch(i2 + 1)
                    qk1(u + 2)
                pv1(u)
                if j == nj - 1:
                    half_end(i, hf)
                    if hf == 1:
                        pend1[min(u + 2, NU1 - 1)] = i
                if u in pend1:
                    pair_end(pend1.pop(u))
            fin = [(d.sem, d.cnt) for d in ds_o1]
            sc.add("pool", lambda e: e.memset(negh[:, 31:32], -0.5), deps=fin)
            sc.flush(blk)

        post_phase(1, x1_d, b_wo_d, y_d, True)
    return nc


def _rope_table(S):
    t = np.arange(S)
    row = (t // GW).astype(np.float32)
    col = (t % GW).astype(np.float32)
    inv = (10000.0 ** (-np.arange(0, 32, 2, dtype=np.float32) / 32)).astype(np.float32)
    ang = np.concatenate([row[:, None] * inv, col[:, None] * inv], axis=-1).astype(np.float32)
    cos = np.cos(ang).astype(np.float32).reshape(S, 2, 16)
    sin = np.sin(ang).astype(np.float32).reshape(S, 2, 16)
    C = np.stack([cos, cos], axis=2).reshape(S, 64)
    Sg = np.stack([-sin, sin], axis=2).reshape(S, 64)
    return np.ascontiguousarray(np.concatenate([C, Sg], axis=1).astype(np.float32))


_NC_CACHE = {}


def kernel(x_prompt, x_sample, norm_mix, norm_mlp, norm_final, a_w_qkv, a_q_norm, a_k_norm,
           a_w_o, b_w_qkv, b_rel_bias, b_w_o, mlp_w_in, mlp_w_out):
    f = lambda a: np.ascontiguousarray(np.asarray(a, dtype=np.float32))
    xp, xsm = f(x_prompt), f(x_sample)
    seqs = [xp[i] for i in range(xp.shape[0])] + [xsm[i] for i in range(xsm.shape[0])]
    S = seqs[0].shape[0]
    ROWS = S // GW
    if ROWS not in _NC_CACHE:
        _NC_CACHE[ROWS] = build(ROWS)
    nc = _NC_CACHE[ROWS]
    shared = {
        "rope": _rope_table(S),
        "ident": np.eye(128, dtype=np.float32).astype(ml_dtypes.bfloat16),
        "norm_mix": f(norm_mix), "norm_mlp": f(norm_mlp), "norm_final": f(norm_final),
        "a_w_qkv": f(a_w_qkv)[0], "a_q_norm": f(a_q_norm)[0], "a_k_norm": f(a_k_norm)[0],
        "a_w_o": f(a_w_o)[0], "b_w_qkv": f(b_w_qkv)[0], "b_rel_bias": f(b_rel_bias)[0],
        "b_w_o": f(b_w_o)[0], "mlp_w_in": f(mlp_w_in), "mlp_w_out": f(mlp_w_out),
    }
    in_maps = [dict(shared, x=np.ascontiguousarray(sq)) for sq in seqs]
    res = run_bass_kernel_spmd(nc, in_maps, core_ids=list(range(len(seqs))))
    ys = [np.asarray(r["y"], dtype=np.float32) for r in res.results]
    nb = xp.shape[0]
    return (np.stack(ys[:nb], axis=0), np.stack(ys[nb:], axis=0))
```

```python
import math
from contextlib import ExitStack

import numpy as np
import ml_dtypes

import concourse.bass as bass
import concourse.mybir as mybir
from concourse.bass_utils import run_bass_kernel_spmd

F32 = mybir.dt.float32
BF16 = mybir.dt.bfloat16
AF = mybir.ActivationFunctionType
ALU = mybir.AluOpType
AX = mybir.AxisListType

D = 1024
NH = 16
HD = 64
NKV = 4
GW = 64
DFF = 4096
EPS = 1e-6
NEG = -200.0

ENG = ("pe", "act", "dve", "pool", "sp")
BLOCK_ATTR = {"pe": "tensor", "act": "scalar", "dve": "vector", "pool": "gpsimd", "sp": "sync"}


class DSem:
    def __init__(self, sem):
        self.sem = sem
        self.cnt = 0


class Buf:
    def __init__(self, t=None):
        self.t = t
        self.w = {}
        self.r = {}


def _merge(dst, tok):
    if tok is None:
        return
    sem, val = tok
    k = id(sem)
    if k not in dst or dst[k][1] < val:
        dst[k] = (sem, val)


class Sched:
    def __init__(self, nc, stack):
        self.nc = nc
        self.stack = stack
        self.streams = {e: [] for e in ENG}
        self.esem = {e: stack.enter_context(nc.semaphore("es_" + e)) for e in ENG if e != "sp"}
        self.ecnt = {e: 0 for e in ENG}
        self.waited = {e: {} for e in ENG}
        self.nds = 0
        self.nops = 0

    def dsem(self):
        self.nds += 1
        return DSem(self.stack.enter_context(self.nc.semaphore("ds%d" % self.nds)))

    def add(self, eng, fn, reads=(), writes=(), deps=(), sig=True, dma=None):
        need = {}
        for b in reads:
            for tok in b.w.values():
                _merge(need, tok)
        for b in writes:
            for tok in b.w.values():
                _merge(need, tok)
            for tok in b.r.values():
                _merge(need, tok)
        for tok in deps:
            _merge(need, tok)
        waits = []
        wd = self.waited[eng]
        for k, (sem, val) in need.items():
            if wd.get(k, 0) >= val:
                continue
            wd[k] = val
            waits.append((sem, val))
        if dma is not None:
            dma.cnt += 16
            tok = (dma.sem, dma.cnt)
            inc = (dma.sem, 16)
        elif sig:
            self.ecnt[eng] += 1
            tok = (self.esem[eng], self.ecnt[eng])
            inc = (self.esem[eng], 1)
        else:
            tok = None
            inc = None
        self.streams[eng].append((fn, waits, inc))
        self.nops += 1
        if tok is not None:
            for b in reads:
                _merge(b.r, tok)
            for b in writes:
                b.w = {}
                b.r = {}
                _merge(b.w, tok)
        return tok

    def flush(self, block):
        for eng in ENG:
            ops = self.streams[eng]
            if not ops:
                continue

            def body(e, ops=ops):
                for fn, waits, inc in ops:
                    for sem, val in waits:
                        e.wait_ge(sem, val)
                    ins = fn(e)
                    if inc is not None:
                        ins.then_inc(inc[0], inc[1])

            getattr(block, BLOCK_ATTR[eng])(body)
            self.streams[eng] = []


def build(ROWS=128, debug=False):
    S = ROWS * GW
    NT = S // 128
    NST = S // 512
    nc = bass.Bass("TRN2", target_bir_lowering=False)
    es = ExitStack()
    with es:
        sc = Sched(nc, es)

        def din(name, shape, dt=F32):
            return nc.dram_tensor(name, list(shape), dt, kind="ExternalInput").ap()

        def dscr(name, shape, dt):
            return nc.dram_tensor(name, list(shape), dt,
                                  kind="ExternalOutput" if debug else "Internal").ap()

        x_d = din("x", [S, D])
        rope_d = din("rope", [S, 128])
        ident_d = din("ident", [128, 128], BF16)
        norm_mix_d = din("norm_mix", [2, D])
        norm_mlp_d = din("norm_mlp", [2, D])
        norm_final_d = din("norm_final", [D])
        a_wqkv_d = din("a_w_qkv", [D, 1536])
        a_qn_d = din("a_q_norm", [HD])
        a_kn_d = din("a_k_norm", [HD])
        a_wo_d = din("a_w_o", [D, D])
        b_wqkv_d = din("b_w_qkv", [D, 3072])
        b_bias_d = din("b_rel_bias", [NH, 15, 31])
        b_wo_d = din("b_w_o", [D, D])
        w_in_d = din("mlp_w_in", [2, D, DFF])
        w_out_d = din("mlp_w_out", [2, DFF, D])
        y_d = nc.dram_tensor("y", [S, D], F32, kind="ExternalOutput").ap()

        qT_d = dscr("qT_s", [128, 8, S], BF16)
        oT_d = dscr("oT_s", [128, 8, S], BF16)
        x1_d = dscr("x1_s", [S, D], F32)
        kT1_d = dscr("kT1_s", [128, 8, S], BF16)
        v1_d = dscr("v1_s", [S, NH * 65], BF16)
        trev_d = dscr("trev_s", [NH, 15, 31], F32)
        cful_d = dscr("cful_s", [128, NH * 14 * 64], F32)
        cint_d = dscr("cint_s", [128, NH * 10 * 64], F32)

        def sb(stack, name, shape, dt):
            return stack.enter_context(nc.sbuf_tensor(name, list(shape), dt))

        def ps(stack, name, shape, dt=F32):
            return stack.enter_context(nc.psum_tensor(name, list(shape), dt))

        ident = sb(es, "ident_sb", [128, 128], BF16)
        ones_f = sb(es, "ones_f", [128, 64], F32)
        negh = sb(es, "negh", [128, 32], F32)
        ones_b = sb(es, "ones_b", [128, 64], BF16)
        b_ident = Buf()
        b_const = Buf()
        ds_c = sc.dsem()
        sc.add("sp", lambda e: e.dma_start(out=ident[:], in_=ident_d), writes=[b_ident], dma=ds_c)
        sc.add("pool", lambda e: e.memset(ones_f[:], 1.0), writes=[b_const])
        sc.add("pool", lambda e: e.memset(negh[:], -0.5), writes=[b_const])
        sc.add("pool", lambda e: e.memset(ones_b[:], 1.0), writes=[b_const])

        epsb = sb(es, "epsb", [128, 1], F32)
        sc.add("pool", lambda e: e.memset(epsb[:], EPS), writes=[b_const])

        def rstd_ops(ssq_ap, tmp_ap, out_ap, nh_ap, inv_n, b_in, b_tmp, b_out):
            sc.add("act", lambda e: e.activation(out=tmp_ap, in_=ssq_ap, func=AF.Sqrt, scale=inv_n, bias=epsb[:, 0:1]),
                   reads=[b_in, b_const], writes=[b_tmp])
            sc.add("dve", lambda e: e.reciprocal(out=out_ap, in_=tmp_ap), reads=[b_tmp], writes=[b_out])

        def group0(eng, fns, reads=(), writes=(), per_deps=None):
            n = len(fns)
            for i, fn in enumerate(fns):
                pd = list(per_deps[i]) if per_deps is not None else []
                if i == n - 1:
                    sc.add(eng, fn, reads=reads, writes=writes, deps=pd)
                elif i == 0:
                    deps = [t for b in reads for t in b.w.values()]
                    deps += [t for b in writes for t in list(b.w.values()) + list(b.r.values())]
                    sc.add(eng, fn, deps=deps + pd, sig=False)
                else:
                    sc.add(eng, fn, deps=pd, sig=False)

        def set_w0(buf, dsem):
            buf.w = {}
            buf.r = {}
            _merge(buf.w, (dsem.sem, dsem.cnt))

        with ExitStack() as st, nc.Block("E0") as blk:
            tb0 = sb(st, "tb0", [16, 15, 31], F32)
            tr1 = sb(st, "tr1", [16, 15, 31], F32)
            tr2 = sb(st, "tr2", [16, 15, 31], F32)
            negt = sb(st, "negt", [128, 2048], F32)
            b_tb0 = Buf(); b_tr1 = Buf(); b_tr2 = Buf(); b_negt = Buf()
            ds_e0 = sc.dsem(); ds_fill = sc.dsem(); ds_tr = sc.dsem(); ds_sc = sc.dsem()
            sc.add("sp", lambda e: e.dma_start(out=tb0[:], in_=b_bias_d), writes=[b_tb0], dma=ds_e0)
            for i in range(31):
                sc.add("dve", lambda e, i=i: e.tensor_copy(out=tr1[:, :, i:i + 1], in_=tb0[:, :, 30 - i:31 - i]),
                       reads=[b_tb0] if i == 0 else [], writes=[b_tr1] if i == 30 else [], sig=(i == 30))
            for u in range(15):
                sc.add("dve", lambda e, u=u: e.tensor_copy(out=tr2[:, u:u + 1, :], in_=tr1[:, 14 - u:15 - u, :]),
                       reads=[b_tr1] if u == 0 else [], writes=[b_tr2] if u == 14 else [], sig=(u == 14))
            sc.add("sp", lambda e: e.dma_start(out=trev_d, in_=tr2[:]), reads=[b_tr2], dma=ds_tr)
            sc.add("pool", lambda e: e.memset(negt[:], NEG), writes=[b_negt])
            for k in range(7):
                sc.add("sp", lambda e, k=k: e.dma_start(out=cful_d[:, k * 2048:(k + 1) * 2048], in_=negt[:]),
                       reads=[b_negt], dma=ds_fill)
            for k in range(5):
                sc.add("sp", lambda e, k=k: e.dma_start(out=cint_d[:, k * 2048:(k + 1) * 2048], in_=negt[:]),
                       reads=[b_negt], dma=ds_fill)
            pre = [(ds_tr.sem, ds_tr.cnt), (ds_fill.sem, ds_fill.cnt)]
            scatter_jobs = []
            cfv = cful_d.rearrange("p (h r c) -> p h r c", h=NH, r=14)
            civ = cint_d.rearrange("p (h r c) -> p h r c", h=NH, r=10)
            for a in range(2):
                for kc in range(64):
                    cs = [c for c in range(64) if min(max(c - 8, 0), 48) <= kc <= min(max(c - 8, 0), 48) + 15]
                    clo, chi = cs[0], cs[-1]
                    assert cs == list(range(clo, chi + 1))
                    s0, s1 = clo - kc + 15, chi - kc + 16
                    p = a * 64 + kc
                    scatter_jobs.append(lambda e, p=p, a=a, clo=clo, chi=chi, s0=s0, s1=s1: e.dma_start(
                        out=cfv[p, :, :, clo:chi + 1], in_=trev_d[:, 1 - a:15 - a, s0:s1]))
                    scatter_jobs.append(lambda e, p=p, a=a, clo=clo, chi=chi, s0=s0, s1=s1: e.dma_start(
                        out=civ[p, :, 1 + a:9 + a, clo:chi + 1], in_=trev_d[:, 4:12, s0:s1]))
            sc.add("pool", lambda e: e.memset(negh[:, 31:32], -0.5), deps=pre)
            sc.flush(blk)

        with ExitStack() as l0:
            KT = sb(l0, "KT", [128, 2, S], BF16)
            VP = sb(l0, "VP", [128, NT, NKV, 65], BF16)
            b_KT = [Buf() for _ in range(NT)]
            b_VP = [Buf() for _ in range(NT)]
            b_VPones = Buf()
            sc.add("pool", lambda e: e.memset(VP[:, :, :, 64:65], 1.0), writes=[b_VPones])

            with ExitStack() as st, nc.Block("A0") as blk:
                wqkv = sb(st, "wqkv0", [128, 8, 1536], BF16)
                gcol = sb(st, "gcol0", [128, 8], F32)
                gq = sb(st, "gq", [128, 4, 64], F32)
                xs = [sb(st, "xs%d" % i, [128, D], F32) for i in range(3)]
                rp = [sb(st, "rp%d" % i, [128, 128], F32) for i in range(3)]
                junk = sb(st, "junk", [128, D], BF16)
                ssq = sb(st, "ssq", [128, NT], F32)
                tmp1 = sb(st, "tmp1", [128, NT], F32)
                rstd = sb(st, "rstd", [128, NT], F32)
                xn = [sb(st, "xn%d" % i, [128, D], BF16) for i in range(2)]
                hT = [sb(st, "hT%d" % i, [128, 8, 128], BF16) for i in range(2)]
                sqb = sb(st, "sqb", [128, 20, 64], F32)
                raw = sb(st, "raw", [128, 20, 64], F32)
                raw_b = sb(st, "raw_b", [128, 20, 64], F32)
                ssqh = sb(st, "ssqh", [128, 20], F32)
                tmph = sb(st, "tmph", [128, 20], F32)
                rstdh = sb(st, "rstdh", [128, 20], F32)
                tabs = [sb(st, "tabs%d" % i, [128, 4, 64], F32) for i in range(2)]
                ra = sb(st, "ra", [128, 20, 64], F32)
                rb = sb(st, "rb", [128, 20, 64], F32)
                rc = sb(st, "rc", [128, 20, 64], F32)
                qk = [sb(st, "qk%d" % i, [128, 20, 64], BF16) for i in range(2)]
                qTs = [sb(st, "qTs%d" % i, [128, 8, 128], BF16) for i in range(2)]
                tpx = ps(st, "tpx", [128, 8, 128], BF16)
                tpq = ps(st, "tpq", [128, 8, 128], BF16)
                tpk = ps(st, "tpk", [128, 2, 128], BF16)
                pqkv = ps(st, "pqkv", [128, 1536], F32)

                b_w = Buf(); b_gain = Buf(); b_gq = Buf()
                b_xs = [Buf() for _ in range(3)]; b_rp = [Buf() for _ in range(3)]
                ds_rp = [sc.dsem() for _ in range(3)]
                b_ssq = Buf(); b_tmp1 = Buf(); b_rstd = Buf()
                b_xn = [Buf() for _ in range(2)]; b_hT = [Buf() for _ in range(2)]
                b_sqb = Buf(); b_raw = Buf(); b_ssqh = Buf(); b_tmph = Buf(); b_rstdh = Buf()
                b_tabs = [Buf() for _ in range(2)]
                b_ra = Buf(); b_rb = Buf(); b_rc = Buf()
                b_qk = [Buf() for _ in range(2)]; b_qTs = [Buf() for _ in range(2)]
                b_tpx = Buf(); b_tpq = Buf(); b_tpk = Buf(); b_pqkv = Buf()
                ds_w = sc.dsem(); ds_g = sc.dsem(); ds_gq = sc.dsem()
                ds_x = [sc.dsem() for _ in range(3)]; ds_q = [sc.dsem() for _ in range(2)]

                wv = a_wqkv_d.rearrange("(c p) m -> p c m", p=128)
                for c in range(8):
                    for P_ in range(2):
                        for a in range(2):
                            sc.add("pool", lambda e, c=c, P_=P_, a=a: e.dma_start(
                                out=wqkv[:, c, 512 * P_:512 * P_ + 512].rearrange("p (j a d) -> p j a d", j=4, a=2)[:, :, a, :],
                                in_=wv[:, c, 512 * P_ + 256 * a:512 * P_ + 256 * a + 256].rearrange("p (j d) -> p j d", j=4)),
                                dma=ds_w)
                    sc.add("pool", lambda e, c=c: e.dma_start(out=wqkv[:, c, 1024:1536], in_=wv[:, c, 1024:1536]),
                           writes=[b_w] if c == 7 else [], dma=ds_w)
                b_w.w = {}; _merge(b_w.w, (ds_w.sem, ds_w.cnt))
                sc.add("sp", lambda e: e.dma_start(out=gcol[:], in_=norm_mix_d[0].rearrange("(c p) -> p c", p=128),
                                                   allow_slow_non_contiguous=True),
                       writes=[b_gain], dma=ds_g)
                for c in range(8):
                    sc.add("dve", lambda e, c=c: e.tensor_scalar(out=wqkv[:, c, :], in0=wqkv[:, c, :],
                                                                 scalar1=gcol[:, c:c + 1], scalar2=None, op0=ALU.mult),
                           reads=[b_w, b_gain], writes=[b_w] if c == 0 else [],
                           deps=list(b_w.w.values()) if c > 0 else [])
                b_w.w = {}; _merge(b_w.w, (sc.esem["dve"], sc.ecnt["dve"]))
                for gi, gd in ((0, a_qn_d), (2, a_kn_d)):
                    sc.add("sp", lambda e, gi=gi, gd=gd: e.dma_start(
                        out=gq[:, gi, :], in_=gd[None, :].broadcast_to([128, 64])), dma=ds_gq)
                    for a in range(2):
                        for hf in range(2):
                            o0 = a * 32 + hf * 16
                            s0 = a * 32 + (1 - hf) * 16
                            sc.add("sp", lambda e, gi=gi, gd=gd, o0=o0, s0=s0: e.dma_start(
                                out=gq[:, gi + 1, o0:o0 + 16],
                                in_=gd[None, s0:s0 + 16].broadcast_to([128, 16])), dma=ds_gq)
                b_gq.w = {}; _merge(b_gq.w, (ds_gq.sem, ds_gq.cnt))
                sc.add("pool", lambda e: e.tensor_scalar(out=gq[:, 0:2, :], in0=gq[:, 0:2, :], scalar1=0.125,
                                                         scalar2=None, op0=ALU.mult),
                       reads=[b_gq], writes=[b_gq])

                xv = x_d.rearrange("(t p) d -> t p d", p=128)
                rv = rope_d.rearrange("(t p) d -> t p d", p=128)
                qTv = qT_d
                raw2 = [raw, raw_b]
                b_raw2 = [b_raw, Buf()]

                def load_x(t):
                    if t >= NT:
                        return
                    x3 = t % 3
                    sc.add("sp", lambda e: e.dma_start(out=xs[x3][:], in_=xv[t]), writes=[b_xs[x3]], dma=ds_x[x3])
                    sc.add("sp", lambda e: e.dma_start(out=rp[x3][:], in_=rv[t]), writes=[b_rp[x3]], dma=ds_rp[x3])

                def stage1(t):
                    sl = t % 2
                    x3 = t % 3
                    sc.add("act", lambda e: e.activation(out=junk[:], in_=xs[x3][:], func=AF.Square,
                                                         accum_out=ssq[:, t:t + 1]),
                           reads=[b_xs[x3]], writes=[b_ssq])
                    yield
                    rstd_ops(ssq[:, t:t + 1], tmp1[:, t:t + 1], rstd[:, t:t + 1], None, 1.0 / D,
                             b_ssq, b_tmp1, b_rstd)
                    yield
                    sc.add("act", lambda e: e.activation(out=xn[sl][:], in_=xs[x3][:], func=AF.Copy,
                                                         scale=rstd[:, t:t + 1]),
                           reads=[b_xs[x3], b_rstd], writes=[b_xn[sl]])
                    yield
                    group0("pe", [lambda e, c=c: e.transpose(out=tpx[:, c, :], in_=xn[sl][:, c * 128:(c + 1) * 128],
                                                             identity=ident[:]) for c in range(8)],
                           reads=[b_xn[sl], b_ident], writes=[b_tpx])
                    yield
                    sc.add("act", lambda e: e.copy(out=hT[sl][:], in_=tpx[:]), reads=[b_tpx], writes=[b_hT[sl]])
                    yield
                    group0("pe", [lambda e, c=c, h3=h3: e.matmul(
                        pqkv[:, h3 * 512:(h3 + 1) * 512], lhsT=hT[sl][:, c, :],
                        rhs=wqkv[:, c, h3 * 512:(h3 + 1) * 512], start=(c == 0), stop=(c == 7))
                        for h3 in range(3) for c in range(8)],
                        reads=[b_hT[sl], b_w], writes=[b_pqkv])
                    yield
                    pq3 = pqkv[:, 0:1280].rearrange("p (h d) -> p h d", d=64)
                    sc.add("act", lambda e: e.copy(out=raw2[sl][:], in_=pq3), reads=[b_pqkv], writes=[b_raw2[sl]])
                    yield
                    sc.add("dve", lambda e: e.tensor_copy(out=VP[:, t, :, 0:64],
                                                          in_=pqkv[:, 1280:1536].rearrange("p (g d) -> p g d", d=64)),
                           reads=[b_pqkv], writes=[b_VP[t]])
                    yield

                def stage2(t):
                    sl = t % 2
                    rw = raw2[sl]
                    b_rw = b_raw2[sl]
                    sc.add("act", lambda e: e.activation(out=sqb[:], in_=rw[:], func=AF.Square),
                           reads=[b_rw], writes=[b_sqb])
                    yield
                    sc.add("dve", lambda e: e.tensor_reduce(out=ssqh[:], in_=sqb[:], axis=AX.X, op=ALU.add),
                           reads=[b_sqb], writes=[b_ssqh])
                    yield
                    rstd_ops(ssqh[:], tmph[:], rstdh[:], None, 1.0 / HD, b_ssqh, b_tmph, b_rstdh)
                    yield
                    tb = tabs[sl]
                    x3 = t % 3
                    sc.add("pool", lambda e: e.tensor_tensor(
                        out=tb[:, 0:4:2, :], in0=gq[:, 0:4:2, :],
                        in1=rp[x3][:, None, 0:64].broadcast_to([128, 2, 64]), op=ALU.mult),
                        reads=[b_rp[x3], b_gq], writes=[b_tabs[sl]])
                    yield
                    sc.add("pool", lambda e: e.tensor_tensor(
                        out=tb[:, 1:4:2, :], in0=gq[:, 1:4:2, :],
                        in1=rp[x3][:, None, 64:128].broadcast_to([128, 2, 64]), op=ALU.mult),
                        reads=[b_rp[x3], b_gq], writes=[b_tabs[sl]])
                    yield
                    sc.add("dve", lambda e: e.tensor_tensor(
                        out=ra[:, 0:16, :], in0=rw[:, 0:16, :],
                        in1=tb[:, 0:1, :].broadcast_to([128, 16, 64]), op=ALU.mult),
                        reads=[b_rw, b_tabs[sl]], writes=[b_ra])
                    yield
                    sc.add("dve", lambda e: e.tensor_tensor(
                        out=ra[:, 16:20, :], in0=rw[:, 16:20, :],
                        in1=tb[:, 2:3, :].broadcast_to([128, 4, 64]), op=ALU.mult),
                        reads=[b_rw, b_tabs[sl]], writes=[b_ra])
                    yield
                    raw5 = rw[:].rearrange("p h (a f i) -> p h a f i", a=2, f=2)
                    rb5 = rb[:].rearrange("p h (a f i) -> p h a f i", a=2, f=2)
                    for (h0, h1, ti) in ((0, 16, 1), (16, 20, 3)):
                        tb5 = tb[:, ti, :].rearrange("p (a f i) -> p a f i", a=2, f=2)
                        for hf in range(2):
                            sc.add("pool", lambda e, h0=h0, h1=h1, hf=hf, tb5=tb5: e.tensor_tensor(
                                out=rb5[:, h0:h1, :, hf, :], in0=raw5[:, h0:h1, :, 1 - hf, :],
                                in1=tb5[:, None, :, hf, :].broadcast_to([128, h1 - h0, 2, 16]), op=ALU.mult),
                                reads=[b_rw, b_tabs[sl]], writes=[b_rb])
                    sc.add("dve", lambda e: e.tensor_tensor(out=rc[:], in0=ra[:], in1=rb[:], op=ALU.add),
                           reads=[b_ra, b_rb], writes=[b_rc])
                    yield
                    sc.add("dve", lambda e: e.tensor_tensor(
                        out=qk[sl][:], in0=rc[:], in1=rstdh[:, :, None].broadcast_to([128, 20, 64]), op=ALU.mult),
                        reads=[b_rc, b_rstdh], writes=[b_qk[sl]])
                    yield
                    fns = [lambda e, cc=cc: e.transpose(out=tpq[:, cc, :], in_=qk[sl][:, 2 * cc:2 * cc + 2, :],
                                                        identity=ident[:]) for cc in range(8)]
                    fns += [lambda e, cc=cc: e.transpose(out=tpk[:, cc, :], in_=qk[sl][:, 16 + 2 * cc:18 + 2 * cc, :],
                                                         identity=ident[:]) for cc in range(2)]
                    group0("pe", fns, reads=[b_qk[sl]], writes=[b_tpq, b_tpk])
                    yield
                    sc.add("act", lambda e: e.copy(out=qTs[sl][:], in_=tpq[:]), reads=[b_tpq], writes=[b_qTs[sl]])
                    yield
                    sc.add("dve", lambda e: e.tensor_copy(out=KT[:, :, t * 128:(t + 1) * 128], in_=tpk[:]),
                           reads=[b_tpk], writes=[b_KT[t]])
                    yield
                    sc.add("sp", lambda e: e.dma_start(out=qTv[:, :, t * 128:(t + 1) * 128], in_=qTs[sl][:]),
                           reads=[b_qTs[sl]], dma=ds_q[sl])
                    yield

                def run_zip(*gens):
                    gens = [g for g in gens if g is not None]
                    while gens:
                        for g in list(gens):
                            try:
                                next(g)
                            except StopIteration:
                                gens.remove(g)

                load_x(0)
                load_x(1)
                load_x(2)
                run_zip(stage1(0))
                for t in range(NT):
                    run_zip(stage1(t + 1) if t + 1 < NT else None, stage2(t))
                    load_x(t + 3)
                fin = [(d.sem, d.cnt) for d in ds_q]
                sc.add("pool", lambda e: e.memset(negh[:, 31:32], -0.5), deps=fin)
                sc.flush(blk)

            with ExitStack() as st, nc.Block("B0") as blk:
                NQS = 3
                NPT = 3
                qsA = [sb(st, "qsA%d" % i, [128, 512], BF16) for i in range(NQS)]
                qsB = [sb(st, "qsB%d" % i, [128, 512], BF16) for i in range(NQS)]
                PT = [sb(st, "PT%d" % i, [128, 1024], BF16) for i in range(NPT)]
                Osb = sb(st, "Osb", [65, 2, 512], F32)
                rcp = sb(st, "rcp", [65, 2, 512], F32)
                rhi = sb(st, "rhi", [65, 2, 512], BF16)
                rlo = sb(st, "rlo", [65, 2, 512], BF16)
                ost = [sb(st, "ost%d" % i, [64, 2, 512], BF16) for i in range(2)]
                NSS = 3
                Sps = [ps(st, "Sps%d" % i, [128, 1024], F32) for i in range(NSS)]
                Ops = ps(st, "Ops", [128, 1024], F32)
                b_qs = [Buf() for _ in range(NQS)]; b_PT = [Buf() for _ in range(NPT)]
                b_S = [Buf() for _ in range(3)]; b_O = Buf(); b_Osb = Buf(); b_rcp = Buf()
                b_rhl = Buf(); b_rhl2 = Buf(); b_ost = [Buf() for _ in range(2)]
                ds_qs = [sc.dsem() for _ in range(NQS)]; ds_o = [sc.dsem() for _ in range(2)]

                units = [(s, cc, kt) for s in range(NST) for cc in range(8) for kt in range(NT)]
                NU = len(units)
                pairs = [(s, cc) for s in range(NST) for cc in range(8)]

                for i in range(NQS):
                    sc.add("pool", lambda e, i=i: e.memset(qsA[i][64:128, :], 0.0), writes=[b_qs[i]])
                    sc.add("pool", lambda e, i=i: e.memset(qsB[i][0:64, :], 0.0), writes=[b_qs[i]])

                def load_q(pi):
                    s, cc = pairs[pi]
                    sl = pi % NQS
                    sc.add("sp", lambda e, s=s, cc=cc, sl=sl: e.dma_start(
                        out=qsA[sl][0:64, :], in_=qT_d[0:64, cc, s * 512:(s + 1) * 512]),
                        writes=[b_qs[sl]], dma=ds_qs[sl])
                    sc.add("sp", lambda e, s=s, cc=cc, sl=sl: e.dma_start(
                        out=qsB[sl][64:128, :], in_=qT_d[64:128, cc, s * 512:(s + 1) * 512]),
                        dma=ds_qs[sl])
                    set_w0(b_qs[sl], ds_qs[sl])

                def qk_mm(u):
                    s, cc, kt = units[u]
                    pi = s * 8 + cc
                    sl = pi % NQS
                    ss = u % NSS
                    P_ = cc // 4
                    sc.add("pe", lambda e, ss=ss, sl=sl, P_=P_, kt=kt: e.matmul(
                        Sps[ss][:, 0:512], lhsT=KT[:, P_, kt * 128:(kt + 1) * 128], rhs=qsA[sl][:],
                        start=True, stop=True),
                        deps=list(b_qs[sl].w.values()) + list(b_S[ss].r.values()) + list(b_S[ss].w.values()),
                        sig=False)
                    sc.add("pe", lambda e, ss=ss, sl=sl, P_=P_, kt=kt: e.matmul(
                        Sps[ss][:, 512:1024], lhsT=KT[:, P_, kt * 128:(kt + 1) * 128], rhs=qsB[sl][:],
                        start=True, stop=True),
                        reads=[b_qs[sl]], writes=[b_S[ss]])

                def exp_op(u):
                    ss = u % NSS
                    pp = u % NPT
                    sc.add("act", lambda e, ss=ss, pp=pp: e.activation(out=PT[pp][:], in_=Sps[ss][:], func=AF.Exp),
                           reads=[b_S[ss]], writes=[b_PT[pp]])

                def pv_mm(u):
                    s, cc, kt = units[u]
                    pp = u % NPT
                    P_ = cc // 4
                    first = (kt == 0)
                    sc.add("pe", lambda e, pp=pp, P_=P_, kt=kt: e.matmul(
                        Ops[0:65, 0:512], lhsT=VP[:, kt, 2 * P_, :], rhs=PT[pp][:, 0:512],
                        start=(kt == 0), stop=(kt == NT - 1)),
                        deps=list(b_PT[pp].w.values()) + (list(b_O.r.values()) if first else []), sig=False)
                    sc.add("pe", lambda e, pp=pp, P_=P_, kt=kt: e.matmul(
                        Ops[0:65, 512:1024], lhsT=VP[:, kt, 2 * P_ + 1, :], rhs=PT[pp][:, 512:1024],
                        start=(kt == 0), stop=(kt == NT - 1)),
                        reads=[b_PT[pp]], writes=[b_O] if kt == NT - 1 else [])

                def pair_end_a(pi):
                    sc.add("dve", lambda e: e.tensor_copy(out=Osb[:].rearrange("p a q -> p (a q)"), in_=Ops[0:65, :]),
                           reads=[b_O], writes=[b_Osb])
                    sc.add("dve", lambda e: e.reciprocal(out=rcp[64:65, :, :], in_=Osb[64:65, :, :]),
                           reads=[b_Osb], writes=[b_rcp])
                    sc.add("dve", lambda e: e.tensor_copy(out=rhi[64:65, :, :], in_=rcp[64:65, :, :]),
                           reads=[b_rcp], writes=[b_rhl])
                    sc.add("dve", lambda e: e.tensor_tensor(out=rlo[64:65, :, :], in0=rcp[64:65, :, :],
                                                            in1=rhi[64:65, :, :], op=ALU.subtract),
                           reads=[b_rcp, b_rhl], writes=[b_rhl2])

                def pair_end_b(pi, x):
                    Rps = Sps[x][0:64, :]
                    b_R = b_S[x]
                    s, cc = pairs[pi]
                    P_, j_ = cc // 4, cc % 4
                    heads = (8 * P_ + j_, 8 * P_ + 4 + j_)
                    os_ = pi % 2
                    fns = []
                    for a in range(2):
                        fns.append(lambda e, a=a: e.matmul(
                            Rps[:, a * 512:(a + 1) * 512], lhsT=ones_b[64:65, 0:64], rhs=rhi[64:65, a, :],
                            start=True, stop=False))
                        fns.append(lambda e, a=a: e.matmul(
                            Rps[:, a * 512:(a + 1) * 512], lhsT=ones_b[64:65, 0:64], rhs=rlo[64:65, a, :],
                            start=False, stop=True))
                    group0("pe", fns, reads=[b_rhl, b_rhl2, b_const], writes=[b_R])
                    sc.add("dve", lambda e, os_=os_: e.tensor_tensor(
                        out=ost[os_][:].rearrange("p a q -> p (a q)"),
                        in0=Osb[0:64, :, :].rearrange("p a q -> p (a q)"), in1=Rps, op=ALU.mult),
                        reads=[b_Osb, b_R], writes=[b_ost[os_]])
                    for a in range(2):
                        h = heads[a]
                        p0 = (h % 2) * 64
                        sc.add("sp", lambda e, a=a, h=h, p0=p0, s=s, os_=os_: e.dma_start(
                            out=oT_d[p0:p0 + 64, h // 2, s * 512:(s + 1) * 512], in_=ost[os_][:, a, :]),
                            reads=[b_ost[os_]], dma=ds_o[os_])

                LAG = min(14, NT - 2)
                per_pair = -(-len(scatter_jobs) // len(pairs))
                pend = {}
                load_q(0)
                if len(pairs) > 1:
                    load_q(1)
                qk_mm(0)
                exp_op(0)
                for u0 in (1, 2):
                    if u0 < NU:
                        qk_mm(u0)
                for u in range(NU):
                    s, cc, kt = units[u]
                    pi = s * 8 + cc
                    if u + 1 < NU:
                        exp_op(u + 1)
                    if kt == 1:
                        for _ in range(per_pair):
                            if scatter_jobs:
                                sc.add("sp", scatter_jobs.pop(0), deps=pre, dma=ds_sc)
                    if u in pend:
                        pair_end_b(pend.pop(u), u % NSS)
                    if u + 3 < NU:
                        s2, cc2, kt2 = units[u + 3]
                        if kt2 == 0 and (s2 * 8 + cc2) + 1 < len(pairs):
                            load_q(s2 * 8 + cc2 + 1)
                        qk_mm(u + 3)
                    pv_mm(u)
                    if kt == NT - 1:
                        pair_end_a(pi)
                        pend[min(u + LAG, NU - 1)] = pi
                for u_left in sorted(pend):
                    pair_end_b(pend[u_left], 0)
                while scatter_jobs:
                    sc.add("sp", scatter_jobs.pop(0), deps=pre, dma=ds_sc)
                etab_done = [(ds_sc.sem, ds_sc.cnt)]
                fin = [(d.sem, d.cnt) for d in ds_o]
                sc.add("pool", lambda e: e.memset(negh[:, 31:32], -0.5), deps=fin)
                sc.flush(blk)

        def group(eng, fns, reads=(), writes=(), per_deps=None):
            n = len(fns)
            for i, fn in enumerate(fns):
                pd = list(per_deps[i]) if per_deps is not None else []
                if i == n - 1:
                    sc.add(eng, fn, reads=reads, writes=writes, deps=pd)
                elif i == 0:
                    deps = [t for b in reads for t in b.w.values()]
                    deps += [t for b in writes for t in list(b.w.values()) + list(b.r.values())]
                    sc.add(eng, fn, deps=deps + pd, sig=False)
                else:
                    sc.add(eng, fn, deps=pd, sig=False)

        def set_w(buf, dsem):
            buf.w = {}
            buf.r = {}
            _merge(buf.w, (dsem.sem, dsem.cnt))

        def norm_front(xt_ap, b_x, gain_t, b_g, ssq_c, tmp_c, rstd_c, bs, xn_t, b_xn_, tp_t, b_tp_, dst_ap, b_dst,
                       junk_t):
            b_ssq_, b_tmp_, b_rstd_ = bs
            sc.add("act", lambda e: e.activation(out=junk_t[:], in_=xt_ap, func=AF.Square, accum_out=ssq_c),
                   reads=[b_x], writes=[b_ssq_])
            rstd_ops(ssq_c, tmp_c, rstd_c, negh[:, 0:1], 1.0 / D, b_ssq_, b_tmp_, b_rstd_)
            sc.add("dve", lambda e: e.scalar_tensor_tensor(out=xn_t[:], in0=xt_ap, scalar=rstd_c, in1=gain_t[:],
                                                           op0=ALU.mult, op1=ALU.mult),
                   reads=[b_x, b_rstd_, b_g], writes=[b_xn_])
            group("pe", [lambda e, c=c: e.transpose(out=tp_t[:, c, :], in_=xn_t[:, c * 128:(c + 1) * 128],
                                                    identity=ident[:]) for c in range(8)],
                  reads=[b_xn_, b_ident], writes=[b_tp_])
            sc.add("act", lambda e: e.copy(out=dst_ap, in_=tp_t[:]), reads=[b_tp_], writes=[b_dst])

        def run_sparse0(ga, gb, k):
            a_alive = ga is not None
            b_alive = True
            while a_alive or b_alive:
                for _ in range(k):
                    if b_alive:
                        try:
                            next(gb)
                        except StopIteration:
                            b_alive = False
                if a_alive:
                    try:
                        next(ga)
                    except StopIteration:
                        a_alive = False

        def run_lag(g0, g1, lag):
            alive0 = alive1 = True
            for _ in range(lag):
                try:
                    next(g0)
                except StopIteration:
                    alive0 = False
            while alive0 or alive1:
                if alive0:
                    try:
                        next(g0)
                    except StopIteration:
                        alive0 = False
                if alive1:
                    try:
                        next(g1)
                    except StopIteration:
                        alive1 = False

        def norm_front_gen(xt_ap, b_x, gain_t, b_g, ssq_c, tmp_c, rstd_c, bs, xn_t, b_xn_, tp_t, b_tp_, dst_ap, b_dst,
                           junk_t, b_junk):
            b_ssq_, b_tmp_, b_rstd_ = bs
            sc.add("act", lambda e: e.activation(out=junk_t[:], in_=xt_ap, func=AF.Square, accum_out=ssq_c),
                   reads=[b_x], writes=[b_ssq_, b_junk])
            yield
            rstd_ops(ssq_c, tmp_c, rstd_c, None, 1.0 / D, b_ssq_, b_tmp_, b_rstd_)
            yield
            sc.add("dve", lambda e: e.scalar_tensor_tensor(out=xn_t[:], in0=xt_ap, scalar=rstd_c, in1=gain_t[:],
                                                           op0=ALU.mult, op1=ALU.mult),
                   reads=[b_x, b_rstd_, b_g], writes=[b_xn_])
            yield
            group("pe", [lambda e, c=c: e.transpose(out=tp_t[:, c, :], in_=xn_t[:, c * 128:(c + 1) * 128],
                                                    identity=ident[:]) for c in range(8)],
                  reads=[b_xn_, b_ident], writes=[b_tp_])
            yield
            sc.add("act", lambda e: e.copy(out=dst_ap, in_=tp_t[:]), reads=[b_tp_], writes=[b_dst])
            yield

        def post_phase(L, src_d, wo_d, dst_d, final):
            NSP = S // 256
            with ExitStack() as st, nc.Block("P%d" % L) as blk:
                Wo = sb(st, "Wo%d" % L, [128, 8, D], BF16)
                Win = sb(st, "Win%d" % L, [128, 8, DFF], BF16)
                Wout = sb(st, "Wout%d" % L, [128, 32, D], BF16)
                gm = sb(st, "gm%d" % L, [128, D], F32)
                gf = sb(st, "gf%d" % L, [128, D], F32) if final else None
                xm = [sb(st, "xm%d_%d" % (L, i), [128, 2, D], F32) for i in range(2)]
                oTs = [sb(st, "oTs%d_%d" % (L, i), [128, 8, 256], BF16) for i in range(2)]
                xn_ = [sb(st, "pxn%d_%d" % (L, i), [128, D], BF16) for i in range(2)]
                h2T = sb(st, "h2T%d" % L, [128, 8, 256], BF16)
                uT = sb(st, "uT%d" % L, [128, 32, 256], BF16)
                rsb = [sb(st, "rsb%d_%d" % (L, i), [128, 256], F32) for i in range(2)]
                junk_ = sb(st, "pjunk%d" % L, [128, D], BF16)
                stt = sb(st, "pstat%d" % L, [128, 6, NT], F32)
                yps = [ps(st, "yps%d_%d" % (L, i), [128, D], F32) for i in range(2)]
                tp_ = ps(st, "ptp%d" % L, [128, 8, 128], BF16)
                ups = [ps(st, "ups%d_%d" % (L, i), [128, 512], F32) for i in range(3)]
                b_Wo = Buf(); b_Win = Buf(); b_Wout = Buf(); b_gm = Buf(); b_gf = Buf()
                b_xm = [[Buf() for _ in range(2)] for _ in range(2)]
                b_oTs = [Buf() for _ in range(2)]; b_pxn = [Buf() for _ in range(2)]
                b_h2T = [Buf() for _ in range(2)]; b_uT = [Buf() for _ in range(32)]
                b_rsb = [Buf() for _ in range(2)]; b_st = [Buf() for _ in range(6)]
                b_yps = [Buf() for _ in range(2)]; b_tp = Buf(); b_ups = [Buf() for _ in range(3)]
                ds_wo = sc.dsem(); ds_win = sc.dsem(); ds_wout = sc.dsem(); ds_g = sc.dsem()
                ds_xl = [sc.dsem() for _ in range(2)]; ds_ol = [sc.dsem() for _ in range(2)]
                ds_st = [[sc.dsem() for _ in range(2)] for _ in range(2)]

                wov = wo_d.rearrange("(c p) m -> p c m", p=128)
                for c in range(8):
                    sc.add("pool", lambda e, c=c: e.dma_start(out=Wo[:, c, :], in_=wov[:, c, :]), dma=ds_wo)
                set_w(b_Wo, ds_wo)
                sc.add("sp", lambda e: e.dma_start(out=gm[:], in_=norm_mlp_d[L:L + 1, :].broadcast_to([128, D])), dma=ds_g)
                set_w(b_gm, ds_g)
                if final:
                    sc.add("sp", lambda e: e.dma_start(out=gf[:], in_=norm_final_d[None, :].broadcast_to([128, D])),
                           dma=ds_g)
                    set_w(b_gf, ds_g); set_w(b_gm, ds_g)
                winv = w_in_d[L].rearrange("(c p) m -> p c m", p=128)
                for c in range(8):
                    for hh in range(2):
                        sc.add("pool", lambda e, c=c, hh=hh: e.dma_start(
                            out=Win[:, c, hh * 2048:(hh + 1) * 2048], in_=winv[:, c, hh * 2048:(hh + 1) * 2048]),
                            dma=ds_win)
                set_w(b_Win, ds_win)
                woutv = w_out_d[L].rearrange("(c p) m -> p c m", p=128)
                for c in range(32):
                    sc.add("pool", lambda e, c=c: e.dma_start(out=Wout[:, c, :], in_=woutv[:, c, :]), dma=ds_wout)
                set_w(b_Wout, ds_wout)

                srcv = src_d.rearrange("(n i p) d -> n p i d", p=128, i=2)
                dstv = dst_d.rearrange("(n i p) d -> n p i d", p=128, i=2)

                def loads(n):
                    sl = n % 2
                    sc.add("sp", lambda e, n=n, sl=sl: e.dma_start(out=xm[sl][:], in_=srcv[n]),
                           writes=[b_xm[sl][0], b_xm[sl][1]], dma=ds_xl[sl])
                    sc.add("sp", lambda e, n=n, sl=sl: e.dma_start(out=oTs[sl][:], in_=oT_d[:, :, n * 256:(n + 1) * 256]),
                           writes=[b_oTs[sl]], dma=ds_ol[sl])

                b_stt = [[Buf() for _ in range(3)] for _ in range(2)]
                b_junkP = Buf()

                def tile_front(n, i):
                    sl = n % 2
                    t = 2 * n + i
                    fns = [lambda e, c=c, hf=hf: e.matmul(
                        yps[i][:, hf * 512:(hf + 1) * 512], lhsT=oTs[sl][:, c, i * 128:(i + 1) * 128],
                        rhs=Wo[:, c, hf * 512:(hf + 1) * 512], start=(c == 0), stop=(c == 7))
                        for hf in range(2) for c in range(8)]
                    group("pe", fns, reads=[b_oTs[sl], b_Wo], writes=[b_yps[i]])
                    yield
                    sc.add("dve", lambda e: e.tensor_tensor(out=xm[sl][:, i, :], in0=yps[i][:],
                                                            in1=xm[sl][:, i, :], op=ALU.add),
                           reads=[b_yps[i]], writes=[b_xm[sl][i]])
                    yield
                    yield from norm_front_gen(
                        xm[sl][:, i, :], b_xm[sl][i], gm, b_gm, stt[:, 0, t:t + 1], stt[:, 1, t:t + 1],
                        stt[:, 2, t:t + 1], (b_stt[i][0], b_stt[i][1], b_stt[i][2]), xn_[i], b_pxn[i], tp_, b_tp,
                        h2T[:, :, i * 128:(i + 1) * 128], b_h2T[i], junk_, b_junkP)

                loads(0)
                for n in range(NSP):
                    sl = n % 2
                    if n + 1 < NSP:
                        loads(n + 1)
                    run_lag(tile_front(n, 0), tile_front(n, 1), 2)
                    for m in range(32):
                        us = m % 3
                        fns = [lambda e, c=c, m=m, us=us: e.matmul(
                            ups[us][:, 0:256], lhsT=Win[:, c, m * 128:(m + 1) * 128], rhs=h2T[:, c, :],
                            start=(c == 0), stop=(c == 7)) for c in range(8)]
                        group("pe", fns, reads=[b_h2T[0], b_h2T[1], b_Win], writes=[b_ups[us]])
                        rs_ = m % 2
                        sc.add("act", lambda e, us=us, rs_=rs_: e.activation(out=rsb[rs_][:], in_=ups[us][:, 0:256],
                                                                           func=AF.Relu),
                               reads=[b_ups[us]], writes=[b_rsb[rs_]])
                        sc.add("dve", lambda e, us=us, rs_=rs_, m=m: e.tensor_tensor(
                            out=uT[:, m, :], in0=ups[us][:, 0:256], in1=rsb[rs_][:], op=ALU.mult),
                            reads=[b_ups[us], b_rsb[rs_]], writes=[b_uT[m]])
                    for i in range(2):
                        t = 2 * n + i
                        fns = [lambda e, k=k, hf=hf, i=i: e.matmul(
                            yps[i][:, hf * 512:(hf + 1) * 512], lhsT=uT[:, k, i * 128:(i + 1) * 128],
                            rhs=Wout[:, k, hf * 512:(hf + 1) * 512], start=(k == 0), stop=(k == 31))
                            for hf in range(2) for k in range(32)]
                        pdeps = [list(b_uT[k].w.values()) for hf in range(2) for k in range(32)]
                        group("pe", fns, reads=[b_Wout] + (b_uT if i == 1 else []), writes=[b_yps[i]], per_deps=pdeps)
                        sc.add("dve", lambda e, i=i, sl=sl: e.tensor_tensor(out=xm[sl][:, i, :], in0=yps[i][:],
                                                                          in1=xm[sl][:, i, :], op=ALU.add),
                               reads=[b_yps[i]], writes=[b_xm[sl][i]])
                        if final:
                            sc.add("act", lambda e, i=i, sl=sl, t=t: e.activation(
                                out=junk_[:], in_=xm[sl][:, i, :], func=AF.Square, accum_out=stt[:, 3, t:t + 1]),
                                reads=[b_xm[sl][i]], writes=[b_st[3]])
                            rstd_ops(stt[:, 3, t:t + 1], stt[:, 4, t:t + 1], stt[:, 5, t:t + 1], negh[:, 0:1], 1.0 / D,
                                     b_st[3], b_st[4], b_st[5])
                            sc.add("dve", lambda e, i=i, sl=sl, t=t: e.scalar_tensor_tensor(
                                out=xm[sl][:, i, :], in0=xm[sl][:, i, :], scalar=stt[:, 5, t:t + 1], in1=gf[:],
                                op0=ALU.mult, op1=ALU.mult),
                                reads=[b_st[5], b_gf], writes=[b_xm[sl][i]])
                        sc.add("sp", lambda e, i=i, sl=sl, n=n: e.dma_start(out=dstv[n][:, i, :], in_=xm[sl][:, i, :]),
                               reads=[b_xm[sl][i]], dma=ds_st[sl][i])
                fin = [(d.sem, d.cnt) for dd in ds_st for d in dd]
                sc.add("pool", lambda e: e.memset(negh[:, 31:32], -0.5), deps=fin)
                sc.flush(blk)

        post_phase(0, x_d, a_wo_d, x1_d, False)

        with ExitStack() as st, nc.Block("A1") as blk:
            W1 = sb(st, "W1", [128, 8, 3072], BF16)
            g1 = sb(st, "g1", [128, D], F32)
            xs1 = [sb(st, "xs1_%d" % i, [128, D], F32) for i in range(2)]
            xn1 = [sb(st, "xn1_%d" % i, [128, D], BF16) for i in range(2)]
            h1T = sb(st, "h1T", [128, 8, 512], BF16)
            h1T_b = sb(st, "h1T_b", [128, 8, 512], BF16)
            qkst = [sb(st, "qkst%d" % i, [128, 16, 512], BF16) for i in range(2)]
            vst = [sb(st, "vst%d" % i, [128, 4, NH, 65], BF16) for i in range(2)]
            junk1 = sb(st, "junk1", [128, D], BF16)
            st1 = sb(st, "st1", [128, 3, NT], F32)
            tp1 = ps(st, "tp1", [128, 8, 128], BF16)
            qkps = [ps(st, "qkps%d" % i, [128, 512], F32) for i in range(2)]
            vps = [ps(st, "vps%d" % i, [128, D], F32) for i in range(2)]
            b_W1 = Buf(); b_g1 = Buf(); b_xs1 = [Buf() for _ in range(2)]; b_xn1 = [Buf() for _ in range(2)]
            b_h1T = [Buf() for _ in range(4)]; b_qkst = [Buf() for _ in range(2)]; b_vst = [Buf() for _ in range(2)]
            b_s1 = [Buf() for _ in range(3)]; b_tp1 = Buf(); b_qkps = [Buf() for _ in range(2)]
            b_vps = [Buf() for _ in range(2)]
            ds_w1 = sc.dsem(); ds_g1 = sc.dsem(); ds_x1 = [sc.dsem() for _ in range(2)]
            ds_s1 = [sc.dsem() for _ in range(2)]
            ds_s1v = [sc.dsem() for _ in range(2)]
            w1v = b_wqkv_d.rearrange("(c p) m -> p c m", p=128)
            for c in range(8):
                for hh in range(2):
                    sc.add("pool", lambda e, c=c, hh=hh: e.dma_start(
                        out=W1[:, c, hh * 1536:(hh + 1) * 1536], in_=w1v[:, c, hh * 1536:(hh + 1) * 1536]), dma=ds_w1)
            set_w(b_W1, ds_w1)
            sc.add("sp", lambda e: e.dma_start(out=g1[:], in_=norm_mix_d[1:2, :].broadcast_to([128, D])), dma=ds_g1)
            set_w(b_g1, ds_g1)
            for i in range(2):
                sc.add("pool", lambda e, i=i: e.memset(vst[i][:, :, :, 64:65], 1.0), writes=[b_vst[i]])
            x1v = x1_d.rearrange("(t p) d -> t p d", p=128)
            v1v = v1_d.rearrange("(t p) f -> p t f", p=128)
            b_s1t = [[Buf() for _ in range(3)] for _ in range(2)]
            b_junk1 = Buf()

            h1T2 = [h1T, h1T_b]
            b_h1T2 = [b_h1T, [Buf() for _ in range(4)]]

            def a1_front(s, i):
                t = 4 * s + i
                sl = t % 2
                hb = s % 2
                sc.add("sp", lambda e: e.dma_start(out=xs1[sl][:], in_=x1v[t]), writes=[b_xs1[sl]], dma=ds_x1[sl])
                yield
                yield from norm_front_gen(
                    xs1[sl][:], b_xs1[sl], g1, b_g1, st1[:, 0, t:t + 1], st1[:, 1, t:t + 1],
                    st1[:, 2, t:t + 1], (b_s1t[sl][0], b_s1t[sl][1], b_s1t[sl][2]), xn1[sl], b_xn1[sl], tp1, b_tp1,
                    h1T2[hb][:, :, i * 128:(i + 1) * 128], b_h1T2[hb][i], junk1, b_junk1)

            def front_all(s):
                for i0 in (0, 2):
                    g0, g1 = a1_front(s, i0), a1_front(s, i0 + 1)
                    a0 = a1 = True
                    for _ in range(2):
                        try:
                            next(g0)
                        except StopIteration:
                            a0 = False
                        yield
                    while a0 or a1:
                        if a0:
                            try:
                                next(g0)
                            except StopIteration:
                                a0 = False
                        if a1:
                            try:
                                next(g1)
                            except StopIteration:
                                a1 = False
                        yield

            def proj_qk(s, ss, mc, hT_, bh):
                qs_ = mc % 2
                fns = [lambda e, c=c: e.matmul(
                    qkps[qs_][:], lhsT=W1[:, c, mc * 128:(mc + 1) * 128], rhs=hT_[:, c, :],
                    start=(c == 0), stop=(c == 7)) for c in range(8)]
                group("pe", fns, reads=bh + [b_W1], writes=[b_qkps[qs_]])
                scale = 0.125 if mc < 8 else 1.0
                sc.add("act", lambda e: e.mul(out=qkst[ss][:, mc, :], in_=qkps[qs_][:], mul=scale),
                       reads=[b_qkps[qs_]], writes=[b_qkst[ss]] if mc == 0 else [],
                       deps=list(b_qkst[ss].w.values()) if mc > 0 else [])

            def proj_v(s, ss, i, hT_, bh):
                vs_ = i % 2
                fns = [lambda e, c=c, hf=hf: e.matmul(
                    vps[vs_][:, hf * 512:(hf + 1) * 512], lhsT=hT_[:, c, i * 128:(i + 1) * 128],
                    rhs=W1[:, c, 2048 + hf * 512:2048 + (hf + 1) * 512], start=(c == 0), stop=(c == 7))
                    for hf in range(2) for c in range(8)]
                group("pe", fns, reads=[bh[i], b_W1], writes=[b_vps[vs_]])
                sc.add("dve", lambda e: e.tensor_copy(
                    out=vst[ss][:, i, :, 0:64], in_=vps[vs_][:].rearrange("p (h d) -> p h d", d=64)),
                    reads=[b_vps[vs_]], writes=[b_vst[ss]] if i == 0 else [],
                    deps=list(b_vst[ss].w.values()) if i > 0 else [])

            def proj_store(s, ss):
                sc.add("sp", lambda e: e.dma_start(out=qT_d[:, :, s * 512:(s + 1) * 512], in_=qkst[ss][:, 0:8, :]),
                       reads=[b_qkst[ss]], dma=ds_s1[ss])
                sc.add("sp", lambda e: e.dma_start(out=kT1_d[:, :, s * 512:(s + 1) * 512], in_=qkst[ss][:, 8:16, :]),
                       reads=[b_qkst[ss]], dma=ds_s1[ss])
                sc.add("sp", lambda e: e.dma_start(
                    out=v1v[:, 4 * s:4 * s + 4, :], in_=vst[ss][:].rearrange("p i h e -> p i (h e)")),
                    reads=[b_vst[ss]], dma=ds_s1v[ss])

            def proj(s):
                ss = s % 2
                hT_, bh = h1T2[s % 2], b_h1T2[s % 2]
                for mc in range(16):
                    proj_qk(s, ss, mc, hT_, bh)
                    yield
                b_qkst[ss].w = {}; _merge(b_qkst[ss].w, (sc.esem["act"], sc.ecnt["act"]))
                for i in range(4):
                    proj_v(s, ss, i, hT_, bh)
                    yield
                b_vst[ss].w = {}; _merge(b_vst[ss].w, (sc.esem["dve"], sc.ecnt["dve"]))
                proj_store(s, ss)
                yield

            for _ in front_all(0):
                pass
            for s in range(NST):
                run_sparse0(front_all(s + 1) if s + 1 < NST else None, proj(s), 1)
            fin = [(d.sem, d.cnt) for d in ds_s1 + ds_s1v]
            sc.add("pool", lambda e: e.memset(negh[:, 31:32], -0.5), deps=fin)
            sc.flush(blk)

        with ExitStack() as st, nc.Block("B1") as blk:
            NK = 7
            Eful = sb(st, "Eful", [128, NH, 14, 64], BF16)
            Eint = sb(st, "Eint", [128, NH, 10, 64], BF16)
            stg = sb(st, "Estg", [128, NH * 14 * 64], F32)
            q1s = [sb(st, "q1s%d" % i, [128, 8, 128], BF16) for i in range(2)]
            k1r = [sb(st, "k1r%d" % i, [128, 8, 128], BF16) for i in range(NK)]
            v1r = [sb(st, "v1r%d" % i, [128, NH, 65], BF16) for i in range(NK)]
            P0 = [sb(st, "P0_%d" % i, [128, 1024], BF16) for i in range(2)]
            PTn = [sb(st, "PTn%d" % i, [128, 1024], BF16) for i in range(3)]
            rc8 = sb(st, "rc8", [128, 2, 4], F32)
            o1tm = [sb(st, "o1tm%d" % i, [128, NH, 64], BF16) for i in range(2)]
            o1Ts = [sb(st, "o1Ts%d" % i, [128, 8, 128], BF16) for i in range(2)]
            S1 = [ps(st, "S1_%d" % i, [128, 1024], F32) for i in range(2)]
            O1 = ps(st, "O1", [128, 2, 512], F32)
            tpo = ps(st, "tpo", [128, 8, 128], BF16)
            b_Ef = Buf(); b_Ei = Buf(); b_stg = Buf()
            b_q1s = [Buf() for _ in range(2)]; b_k1r = [Buf() for _ in range(NK)]; b_v1r = [Buf() for _ in range(NK)]
            b_P0 = [Buf() for _ in range(2)]; b_PTn = [Buf() for _ in range(3)]; b_rc8 = Buf()
            b_o1tm = [Buf() for _ in range(2)]; b_o1Ts = [Buf() for _ in range(2)]
            b_S1 = [Buf() for _ in range(2)]; b_O1 = Buf(); b_tpo = Buf()
            ds_e = sc.dsem(); ds_q1 = [sc.dsem() for _ in range(2)]; ds_k1 = [sc.dsem() for _ in range(NK)]
            ds_o1 = [sc.dsem() for _ in range(2)]
            sc.add("sp", lambda e: e.dma_start(out=stg[:], in_=cful_d), writes=[b_stg], deps=etab_done, dma=ds_e)
            sc.add("act", lambda e: e.activation(out=Eful[:].rearrange("p h r c -> p (h r c)"), in_=stg[:], func=AF.Exp),
                   reads=[b_stg], writes=[b_Ef])
            sc.add("sp", lambda e: e.dma_start(out=stg[:, 0:NH * 10 * 64], in_=cint_d), writes=[b_stg], dma=ds_e)
            sc.add("act", lambda e: e.activation(out=Eint[:].rearrange("p h r c -> p (h r c)"),
                                                 in_=stg[:, 0:NH * 10 * 64], func=AF.Exp),
                   reads=[b_stg], writes=[b_Ei])

            def pair_cfg(i):
                if NT >= 6 and 2 <= i <= NT - 3:
                    return [i - 2 + j for j in range(5)], Eint, b_Ei, [8 - 2 * j for j in range(5)]
                if i == 0:
                    K_, t0 = 7, 0
                elif i == 1:
                    K_, t0 = 5, 0
                elif i == NT - 2:
                    K_, t0 = 3, NT - 4
                else:
                    K_, t0 = 1, NT - 4
                return [t0 + j for j in range(4)], Eful, b_Ef, [13 - K_ - 2 * j for j in range(4)]

            loaded = set()

            def load_keys(jt):
                if jt in loaded or jt < 0 or jt >= NT:
                    return
                loaded.add(jt)
                sl = jt % NK
                sc.add("sp", lambda e, jt=jt, sl=sl: e.dma_start(out=k1r[sl][:], in_=kT1_d[:, :, jt * 128:(jt + 1) * 128]),
                       writes=[b_k1r[sl]], dma=ds_k1[sl])
                sc.add("sp", lambda e, jt=jt, sl=sl: e.dma_start(
                    out=v1r[sl][:].rearrange("p h e -> p (h e)"), in_=v1_d[jt * 128:(jt + 1) * 128, :]),
                    writes=[b_v1r[sl]], dma=ds_k1[sl])
                set_w(b_k1r[sl], ds_k1[sl]); set_w(b_v1r[sl], ds_k1[sl])

            def load_q1(i):
                sl = i % 2
                sc.add("sp", lambda e, i=i, sl=sl: e.dma_start(out=q1s[sl][:], in_=qT_d[:, :, i * 128:(i + 1) * 128]),
                       writes=[b_q1s[sl]], dma=ds_q1[sl])

            units1 = []
            for i in range(NT):
                tiles, Et, b_Et, planes = pair_cfg(i)
                for hf in range(2):
                    for j, jt in enumerate(tiles):
                        units1.append((i, hf, j, jt, len(tiles), planes[j], Et, b_Et))
            NU1 = len(units1)

            def prefetch(i):
                if i >= NT:
                    return
                load_q1(i)
                for jt in pair_cfg(i)[0]:
                    load_keys(jt)

            def qk1(u):
                i, hf, j, jt, nj, p0_, Et, b_Et = units1[u]
                ksl, qsl, ss = jt % NK, i % 2, u % 2
                fns = []
                for cl in range(4):
                    c = 4 * hf + cl
                    fns.append(lambda e, c=c, cl=cl: e.matmul(
                        S1[ss][:, cl * 128:(cl + 1) * 128], lhsT=k1r[ksl][0:64, c, :], rhs=q1s[qsl][0:64, c, :],
                        start=True, stop=True))
                    fns.append(lambda e, c=c, cl=cl: e.matmul(
                        S1[ss][:, 512 + cl * 128:512 + (cl + 1) * 128], lhsT=k1r[ksl][64:128, c, :],
                        rhs=q1s[qsl][64:128, c, :], start=True, stop=True))
                group("pe", fns, reads=[b_k1r[ksl], b_q1s[qsl]], writes=[b_S1[ss]])

            def exp1(u):
                ss = u % 2
                sc.add("act", lambda e: e.activation(out=P0[ss][:], in_=S1[ss][:], func=AF.Exp),
                       reads=[b_S1[ss]], writes=[b_P0[ss]])

            def mul1(u):
                i, hf, j, jt, nj, p0_, Et, b_Et = units1[u]
                ss, pp = u % 2, u % 3
                e_ap = Et[:, 8 * hf:8 * hf + 8, p0_:p0_ + 2, :].rearrange("p (cl par) b c -> p par cl (b c)", par=2)
                meng = "dve"
                sc.add(meng, lambda e: e.tensor_tensor(
                    out=PTn[pp][:].rearrange("p (par cl q) -> p par cl q", par=2, cl=4),
                    in0=P0[ss][:].rearrange("p (par cl q) -> p par cl q", par=2, cl=4), in1=e_ap, op=ALU.mult),
                    reads=[b_P0[ss], b_Et], writes=[b_PTn[pp]])

            def pv1(u):
                i, hf, j, jt, nj, p0_, Et, b_Et = units1[u]
                ksl, pp = jt % NK, u % 3
                fns = []
                for hh in range(8):
                    cl, par = hh // 2, hh % 2
                    h = 8 * hf + hh
                    fns.append(lambda e, hh=hh, cl=cl, par=par, h=h: e.matmul(
                        O1[:, hh // 4, (hh % 4) * 65:(hh % 4) * 65 + 65],
                        lhsT=PTn[pp][:, par * 512 + cl * 128:par * 512 + (cl + 1) * 128],
                        rhs=v1r[ksl][:, h, :], start=(j == 0 and hh % 4 == 0), stop=(j == nj - 1),
                        skip_group_check=True))
                if j == 0 or j == nj - 1:
                    group("pe", fns, reads=[b_PTn[pp], b_v1r[ksl]], writes=[b_O1])
                else:
                    group("pe", fns, reads=[b_PTn[pp], b_v1r[ksl], b_O1])

            O4 = O1[:, :, 0:260].rearrange("p b (h e) -> p b h e", e=65)

            def half_end(i, hf):
                osl = i % 2
                sc.add("dve", lambda e: e.reciprocal(out=rc8[:], in_=O4[:, :, :, 64]), reads=[b_O1], writes=[b_rc8])
                sc.add("dve", lambda e: e.tensor_tensor(
                    out=o1tm[osl][:, 8 * hf:8 * hf + 8, :].rearrange("p (b h) d -> p b h d", b=2),
                    in0=O4[:, :, :, 0:64], in1=rc8[:, :, :, None].broadcast_to([128, 2, 4, 64]), op=ALU.mult),
                    reads=[b_O1, b_rc8], writes=[b_o1tm[osl]] if hf == 0 else [],
                    deps=list(b_o1tm[osl].w.values()) if hf == 1 else [])
                if hf == 1:
                    b_o1tm[osl].w = {}; _merge(b_o1tm[osl].w, (sc.esem["dve"], sc.ecnt["dve"]))
                b_O1.r = {}; _merge(b_O1.r, (sc.esem["dve"], sc.ecnt["dve"]))

            def pair_end(i):
                osl = i % 2
                group("pe", [lambda e, c=c: e.transpose(out=tpo[:, c, :], in_=o1tm[osl][:, 2 * c:2 * c + 2, :],
                                                        identity=ident[:]) for c in range(8)],
                      reads=[b_o1tm[osl]], writes=[b_tpo])
                sc.add("act", lambda e: e.copy(out=o1Ts[osl][:], in_=tpo[:]), reads=[b_tpo], writes=[b_o1Ts[osl]])
                sc.add("sp", lambda e: e.dma_start(out=oT_d[:, :, i * 128:(i + 1) * 128], in_=o1Ts[osl][:]),
                       reads=[b_o1Ts[osl]], dma=ds_o1[osl])

            prefetch(0)
            prefetch(1)
            pend1 = {}
            qk1(0)
            exp1(0)
            mul1(0)
            if NU1 > 1:
                qk1(1)
            for u in range(NU1):
                i, hf, j, jt, nj, p0_, Et, b_Et = units1[u]
                if u + 1 < NU1:
                    exp1(u + 1)
                    mul1(u + 1)
                if u + 2 < NU1:
                    i2, hf2, j2 = units1[u + 2][0:3]
                    if hf2 == 0 and j2 == 0:
                        prefetch(i2 + 1)
                    qk1(u + 2)
                pv1(u)
                if j == nj - 1:
                    half_end(i, hf)
                    if hf == 1:
                        pend1[min(u + 2, NU1 - 1)] = i
                if u in pend1:
                    pair_end(pend1.pop(u))
            fin = [(d.sem, d.cnt) for d in ds_o1]
            sc.add("pool", lambda e: e.memset(negh[:, 31:32], -0.5), deps=fin)
            sc.flush(blk)

        post_phase(1, x1_d, b_wo_d, y_d, True)
    return nc


def _rope_table(S):
    t = np.arange(S)
    row = (t // GW).astype(np.float32)
    col = (t % GW).astype(np.float32)
    inv = (10000.0 ** (-np.arange(0, 32, 2, dtype=np.float32) / 32)).astype(np.float32)
    ang = np.concatenate([row[:, None] * inv, col[:, None] * inv], axis=-1).astype(np.float32)
    cos = np.cos(ang).astype(np.float32).reshape(S, 2, 16)
    sin = np.sin(ang).astype(np.float32).reshape(S, 2, 16)
    C = np.stack([cos, cos], axis=2).reshape(S, 64)
    Sg = np.stack([-sin, sin], axis=2).reshape(S, 64)
    return np.ascontiguousarray(np.concatenate([C, Sg], axis=1).astype(np.float32))


_NC_CACHE = {}


def kernel(x_prompt, x_sample, norm_mix, norm_mlp, norm_final, a_w_qkv, a_q_norm, a_k_norm,
           a_w_o, b_w_qkv, b_rel_bias, b_w_o, mlp_w_in, mlp_w_out):
    f = lambda a: np.ascontiguousarray(np.asarray(a, dtype=np.float32))
    xp, xsm = f(x_prompt), f(x_sample)
    seqs = [xp[i] for i in range(xp.shape[0])] + [xsm[i] for i in range(xsm.shape[0])]
    S = seqs[0].shape[0]
    ROWS = S // GW
    if ROWS not in _NC_CACHE:
        _NC_CACHE[ROWS] = build(ROWS)
    nc = _NC_CACHE[ROWS]
    shared = {
        "rope": _rope_table(S),
        "ident": np.eye(128, dtype=np.float32).astype(ml_dtypes.bfloat16),
        "norm_mix": f(norm_mix), "norm_mlp": f(norm_mlp), "norm_final": f(norm_final),
        "a_w_qkv": f(a_w_qkv)[0], "a_q_norm": f(a_q_norm)[0], "a_k_norm": f(a_k_norm)[0],
        "a_w_o": f(a_w_o)[0], "b_w_qkv": f(b_w_qkv)[0], "b_rel_bias": f(b_rel_bias)[0],
        "b_w_o": f(b_w_o)[0], "mlp_w_in": f(mlp_w_in), "mlp_w_out": f(mlp_w_out),
    }
    in_maps = [dict(shared, x=np.ascontiguousarray(sq)) for sq in seqs]
    res = run_bass_kernel_spmd(nc, in_maps, core_ids=list(range(len(seqs))))
    ys = [np.asarray(r["y"], dtype=np.float32) for r in res.results]
    nb = xp.shape[0]
    return (np.stack(ys[:nb], axis=0), np.stack(ys[nb:], axis=0))
```
